# Optimizing a Trainium2 kernel written in Bass

```python
import math
import jax, jax.numpy as jnp
from jax import lax
import numpy as np

D_MODEL = 1024
BATCH = 4
SEQ = 4096
DEPTH = 4
DEC_BATCH = 128
DEC_SEQ = 1
PAST_LEN = 8192
PAGE_SIZE = 128

N_A_LAYERS = DEPTH // 2
N_B_LAYERS = DEPTH - N_A_LAYERS
GROUP_SIZE = 16
N_GROUPS = D_MODEL // GROUP_SIZE
STATE_DIM = 64
DT_MIN = 1e-3
DT_MAX = 1e-1
HEAD_DIM = 64
N_HEADS = D_MODEL // HEAD_DIM
N_KV_HEADS = 4
Q_PER_KV = N_HEADS // N_KV_HEADS
WINDOW = 128
ROPE_THETA = 10000.0
D_FF = 4 * D_MODEL
EPS = 1e-6

kernel_name = "yoco_s5_swa_sink_decoder_step"


def rms_norm(x, g):
    x32 = x.astype(jnp.float32)
    y = x32 * lax.rsqrt(jnp.mean(x32 * x32, axis=-1, keepdims=True) + EPS)
    return (y * g.astype(jnp.float32)).astype(x.dtype)


def rope(x, pos):
    half = HEAD_DIM // 2
    inv = ROPE_THETA ** (-jnp.arange(half, dtype=jnp.float32) / half)
    ang = pos.astype(jnp.float32)[:, None] * inv[None, :]
    cos = jnp.cos(ang)[:, None, :]
    sin = jnp.sin(ang)[:, None, :]
    x32 = x.astype(jnp.float32)
    x1, x2 = x32[..., :half], x32[..., half:]
    out = jnp.concatenate([x1 * cos - x2 * sin, x2 * cos + x1 * sin], axis=-1)
    return out.astype(x.dtype)


def s5_discretize(a_re, a_im, log_dt, b_re, b_im):
    dt = jnp.exp(log_dt.astype(jnp.float32))
    lam = lax.complex(a_re.astype(jnp.float32), a_im.astype(jnp.float32))
    lam_bar = jnp.exp(lam * dt)
    b = lax.complex(b_re.astype(jnp.float32), b_im.astype(jnp.float32))
    b_bar = ((lam_bar - 1.0) / lam)[..., None] * b
    return lam_bar, b_bar


def s5_mixer(u, s0, lam_bar, b_bar, c, d_skip, w_glu, b_glu):
    bsz, seq_len, _ = u.shape
    u32 = u.astype(jnp.float32)
    ug = u32.reshape(bsz, seq_len, N_GROUPS, GROUP_SIZE).astype(jnp.complex64)
    bu = jnp.einsum('blgc,gpc->blgp', ug, b_bar)
    if s0 is not None:
        bu = bu.at[:, 0].add(lam_bar * s0)
    a = jnp.broadcast_to(lam_bar, (1, seq_len) + lam_bar.shape)

    def combine(e1, e2):
        a1, b1 = e1
        a2, b2 = e2
        return a1 * a2, a2 * b1 + b2

    _, s = lax.associative_scan(combine, (a, bu), axis=1)
    y = jnp.real(jnp.einsum('blgp,gcp->blgc', s, c)).reshape(bsz, seq_len, D_MODEL)
    y = y + d_skip.astype(jnp.float32) * u32
    g = jax.nn.gelu(y).astype(u.dtype)
    z = g @ w_glu + b_glu
    out = z[..., :D_MODEL] * jax.nn.sigmoid(z[..., D_MODEL:])
    return out.astype(u.dtype), s[:, -1]


def sq_relu_mlp(h, w_in, w_out):
    return jnp.square(jax.nn.relu(h @ w_in)) @ w_out


def shared_kv(x, g_kv, w_k, w_v, k_gain, pos):
    bsz, seq_len, _ = x.shape
    h = rms_norm(x, g_kv)
    k = (h @ w_k).reshape(bsz, seq_len, N_KV_HEADS, HEAD_DIM)
    k = rope(rms_norm(k, k_gain), pos)
    v = (h @ w_v).reshape(bsz, seq_len, N_KV_HEADS, HEAD_DIM)
    return k, v


def queries(x, g, w_q, q_gain, pos):
    bsz, seq_len, _ = x.shape
    h = rms_norm(x, g)
    q = (h @ w_q).reshape(bsz, seq_len, N_HEADS, HEAD_DIM)
    q = rope(rms_norm(q, q_gain), pos)
    return q.reshape(bsz, seq_len, N_KV_HEADS, Q_PER_KV, HEAD_DIM)


def sink_attention(q, k, v, qpos, kpos, sinks):
    s = jnp.einsum('...qhgd,...khd->...hgqk', q.astype(jnp.float32), k.astype(jnp.float32)) * (HEAD_DIM ** -0.5)
    diff = qpos[..., :, None] - kpos[..., None, :]
    valid = (diff >= 0) & (diff < WINDOW) & (kpos[..., None, :] >= 0)
    s = jnp.where(valid[..., None, None, :, :], s, -jnp.inf)
    sink = sinks.astype(jnp.float32).reshape(N_KV_HEADS, Q_PER_KV)[:, :, None, None]
    m = jnp.maximum(jnp.max(s, axis=-1, keepdims=True), sink)
    p = jnp.exp(s - m)
    denom = jnp.sum(p, axis=-1, keepdims=True) + jnp.exp(sink - m)
    o = jnp.einsum('...hgqk,...khd->...qhgd', p / denom, v.astype(jnp.float32))
    return o.astype(q.dtype)


def swa_prompt(q, k, v, pos, sinks):
    bsz, seq_len = q.shape[:2]
    nb = seq_len // WINDOW
    qb = q.reshape(bsz, nb, WINDOW, N_KV_HEADS, Q_PER_KV, HEAD_DIM)

    def with_prev(t):
        tb = t.reshape(bsz, nb, WINDOW, N_KV_HEADS, HEAD_DIM)
        prev = jnp.concatenate([jnp.zeros_like(tb[:, :1]), tb[:, :-1]], axis=1)
        return jnp.concatenate([prev, tb], axis=2)

    qpos = pos.reshape(nb, WINDOW)
    kpos = jnp.concatenate([qpos - WINDOW, qpos], axis=1)
    o = sink_attention(qb, with_prev(k), with_prev(v), qpos, kpos, sinks)
    return o.reshape(bsz, seq_len, N_HEADS * HEAD_DIM)


def setup_inputs(seed: int = 0) -> dict:
    key = jax.random.key(seed)
    ks = jax.random.split(key, 32)
    f32 = jnp.float32

    def nrm(k, shape, scale):
        return jax.random.normal(k, shape, f32) * scale

    cache_rows = min(WINDOW, PAST_LEN)
    ssm_shape = (N_A_LAYERS, N_GROUPS, STATE_DIM)
    n_idx = jnp.arange(STATE_DIM, dtype=f32)
    return {
        'x_prompt': nrm(ks[0], (BATCH, SEQ, D_MODEL), 1.0),
        'x_sample': nrm(ks[1], (DEC_BATCH, DEC_SEQ, D_MODEL), 1.0),
        'state_ssm_re': nrm(ks[2], (N_A_LAYERS, DEC_BATCH, N_GROUPS, STATE_DIM), 0.1),
        'state_ssm_im': nrm(ks[3], (N_A_LAYERS, DEC_BATCH, N_GROUPS, STATE_DIM), 0.1),
        'cache_k': nrm(ks[4], (DEC_BATCH, cache_rows, N_KV_HEADS, HEAD_DIM), 1.0),
        'cache_v': nrm(ks[5], (DEC_BATCH, cache_rows, N_KV_HEADS, HEAD_DIM), 1.0),
        'norm_mix': 1.0 + nrm(ks[6], (DEPTH, D_MODEL), 0.02),
        'norm_mlp': 1.0 + nrm(ks[7], (DEPTH, D_MODEL), 0.02),
        'ssm_a_re': -0.5 + nrm(ks[8], ssm_shape, 0.01),
        'ssm_a_im': math.pi * n_idx + nrm(ks[9], ssm_shape, 0.01),
        'ssm_log_dt': jax.random.uniform(ks[10], ssm_shape, f32, math.log(DT_MIN), math.log(DT_MAX)),
        'ssm_b_re': nrm(ks[11], (N_A_LAYERS, N_GROUPS, STATE_DIM, GROUP_SIZE), (2 * GROUP_SIZE) ** -0.5),
        'ssm_b_im': nrm(ks[12], (N_A_LAYERS, N_GROUPS, STATE_DIM, GROUP_SIZE), (2 * GROUP_SIZE) ** -0.5),
        'ssm_c_re': nrm(ks[13], (N_A_LAYERS, N_GROUPS, GROUP_SIZE, STATE_DIM), STATE_DIM ** -0.5),
        'ssm_c_im': nrm(ks[14], (N_A_LAYERS, N_GROUPS, GROUP_SIZE, STATE_DIM), STATE_DIM ** -0.5),
        'ssm_d': nrm(ks[15], (N_A_LAYERS, D_MODEL), 1.0),
        'w_glu': nrm(ks[16], (N_A_LAYERS, D_MODEL, 2 * D_MODEL), D_MODEL ** -0.5),
        'b_glu': nrm(ks[17], (N_A_LAYERS, 2 * D_MODEL), 0.01),
        'norm_kv': 1.0 + nrm(ks[18], (D_MODEL,), 0.02),
        'w_k': nrm(ks[19], (D_MODEL, N_KV_HEADS * HEAD_DIM), D_MODEL ** -0.5),
        'w_v': nrm(ks[20], (D_MODEL, N_KV_HEADS * HEAD_DIM), D_MODEL ** -0.5),
        'k_norm': 1.0 + nrm(ks[21], (HEAD_DIM,), 0.02),
        'w_q': nrm(ks[22], (N_B_LAYERS, D_MODEL, N_HEADS * HEAD_DIM), D_MODEL ** -0.5),
        'q_norm': 1.0 + nrm(ks[23], (N_B_LAYERS, HEAD_DIM), 0.02),
        'attn_sinks': nrm(ks[24], (N_B_LAYERS, N_HEADS), 0.5),
        'w_o': nrm(ks[25], (N_B_LAYERS, N_HEADS * HEAD_DIM, D_MODEL), (N_HEADS * HEAD_DIM) ** -0.5),
        'w_mlp_in': nrm(ks[26], (DEPTH, D_MODEL, D_FF), D_MODEL ** -0.5),
        'w_mlp_out': nrm(ks[27], (DEPTH, D_FF, D_MODEL), D_FF ** -0.5),
    }


def reference(x_prompt, x_sample, state_ssm_re, state_ssm_im, cache_k, cache_v,
              norm_mix, norm_mlp, ssm_a_re, ssm_a_im, ssm_log_dt, ssm_b_re, ssm_b_im,
              ssm_c_re, ssm_c_im, ssm_d, w_glu, b_glu, norm_kv, w_k, w_v, k_norm,
              w_q, q_norm, attn_sinks, w_o, w_mlp_in, w_mlp_out):
    seq_p = x_prompt.shape[1]
    seq_s = x_sample.shape[1]
    pos_p = jnp.arange(seq_p, dtype=jnp.int32)
    pos_s = PAST_LEN + jnp.arange(seq_s, dtype=jnp.int32)
    n_cached = cache_k.shape[1]
    kpos_s = jnp.concatenate([PAST_LEN - n_cached + jnp.arange(n_cached, dtype=jnp.int32), pos_s])

    xp, xs = x_prompt, x_sample
    sp_re, sp_im, ss_re, ss_im = [], [], [], []
    kp = vp = k_all_s = v_all_s = None
    for layer in range(DEPTH):
        if layer < N_A_LAYERS:
            i = layer
            lam_bar, b_bar = s5_discretize(ssm_a_re[i], ssm_a_im[i], ssm_log_dt[i], ssm_b_re[i], ssm_b_im[i])
            c = lax.complex(ssm_c_re[i].astype(jnp.float32), ssm_c_im[i].astype(jnp.float32))
            s0 = lax.complex(state_ssm_re[i].astype(jnp.float32), state_ssm_im[i].astype(jnp.float32))
            yp, fin_p = s5_mixer(rms_norm(xp, norm_mix[layer]), None, lam_bar, b_bar, c, ssm_d[i], w_glu[i], b_glu[i])
            ys, fin_s = s5_mixer(rms_norm(xs, norm_mix[layer]), s0, lam_bar, b_bar, c, ssm_d[i], w_glu[i], b_glu[i])
            xp = xp + yp
            xs = xs + ys
            sp_re.append(jnp.real(fin_p))
            sp_im.append(jnp.imag(fin_p))
            ss_re.append(jnp.real(fin_s))
            ss_im.append(jnp.imag(fin_s))
        else:
            if layer == N_A_LAYERS:
                kp, vp = shared_kv(xp, norm_kv, w_k, w_v, k_norm, pos_p)
                ks_new, vs_new = shared_kv(xs, norm_kv, w_k, w_v, k_norm, pos_s)
                k_all_s = jnp.concatenate([cache_k.astype(ks_new.dtype), ks_new], axis=1)
                v_all_s = jnp.concatenate([cache_v.astype(vs_new.dtype), vs_new], axis=1)
            j = layer - N_A_LAYERS
            qp = queries(xp, norm_mix[layer], w_q[j], q_norm[j], pos_p)
            qs = queries(xs, norm_mix[layer], w_q[j], q_norm[j], pos_s)
            op = swa_prompt(qp, kp, vp, pos_p, attn_sinks[j])
            os_ = sink_attention(qs, k_all_s, v_all_s, pos_s, kpos_s, attn_sinks[j])
            os_ = os_.reshape(xs.shape[0], seq_s, N_HEADS * HEAD_DIM)
            xp = xp + op @ w_o[j]
            xs = xs + os_ @ w_o[j]
        xp = xp + sq_relu_mlp(rms_norm(xp, norm_mlp[layer]), w_mlp_in[layer], w_mlp_out[layer])
        xs = xs + sq_relu_mlp(rms_norm(xs, norm_mlp[layer]), w_mlp_in[layer], w_mlp_out[layer])

    keep_p = min(WINDOW, seq_p)
    keep_s = min(WINDOW, PAST_LEN + seq_s)
    return (xp, xs,
            jnp.stack(sp_re), jnp.stack(sp_im), kp[:, -keep_p:], vp[:, -keep_p:],
            jnp.stack(ss_re), jnp.stack(ss_im), k_all_s[:, -keep_s:], v_all_s[:, -keep_s:])
```

```python
import math
import numpy as np
import concourse.bass as bass
import concourse.mybir as mybir
from concourse.bass_utils import run_bass_kernel_spmd

F32 = mybir.dt.float32
BF16 = mybir.dt.bfloat16
I32 = mybir.dt.int32
AF = mybir.ActivationFunctionType
ALU = mybir.AluOpType

ENGS = ("pe", "act", "dve", "pool", "sp")

NCORES = 8
TP = 2048
TS = 16
NT = TP + TS
D = 1024
DFF = 4096
EPS = 1e-6


class Buf:
    __slots__ = ("w", "r", "name", "excl")

    def __init__(self, name="", excl=False):
        self.w = None
        self.r = {}
        self.name = name
        self.excl = excl


class Op:
    __slots__ = ("eng", "fn", "deps", "dma", "sig", "idx", "dsem", "dval")

    def __init__(self, eng, fn, dma):
        self.eng = eng
        self.fn = fn
        self.deps = []
        self.dma = dma
        self.sig = False
        self.idx = None
        self.dsem = None
        self.dval = None


class Sched:
    NDS = 24

    def __init__(self, nc):
        self.nc = nc
        self.ops = {e: [] for e in ENGS}
        self.ndma = {e: 0 for e in ENGS}
        self.bar = Buf("barrier")
        self.bar_t = None

    def barrier(self):
        if self.bar_t is None:
            self.bar_t = self.nc.alloc_sbuf_tensor_at("bar_t", [128, 8], F32, offset=SB_BASE)
        t = self.bar_t
        self.add("pool", lambda e: e.memset(t[:, :], 0.0), writes=[self.bar])

    def add(self, eng, fn, reads=(), writes=(), dma=False):
        op = Op(eng, fn, dma)
        deps = {}
        reads = list(reads) + [self.bar]

        def need(d, kind):
            if d is None:
                return
            if (not d.dma) and (not dma) and d.eng == eng:
                if kind != "raw" or eng == "pe":
                    return
            deps[id(d)] = d

        for b in reads:
            need(b.w, "raw")
            if b.excl:
                for r in b.r.values():
                    need(r, "war")
        for b in writes:
            need(b.w, "waw")
            for r in b.r.values():
                need(r, "war")
        for d in deps.values():
            d.sig = True
            op.deps.append(d)
        for b in writes:
            b.w = op
            b.r = {}
        for b in reads:
            if dma:
                b.r[("dma", id(op))] = op
            else:
                b.r[eng] = op
        if dma:
            op.sig = True
            k = self.ndma[eng]
            self.ndma[eng] += 1
            op.dsem = (eng, k % self.NDS)
            op.dval = 16 * (k // self.NDS + 1)
        self.ops[eng].append(op)
        return op

    def pe(self, fn, reads=(), writes=()):
        return self.add("pe", fn, reads, writes)

    def act(self, fn, reads=(), writes=()):
        return self.add("act", fn, reads, writes)

    def dve(self, fn, reads=(), writes=()):
        return self.add("dve", fn, reads, writes)

    def pool(self, fn, reads=(), writes=()):
        return self.add("pool", fn, reads, writes)

    def cc(self, fn, reads=(), writes=()):
        op = self.add("pool", fn, reads, writes, dma=True)
        self.ndma["pool"] -= 1
        self.ncc = getattr(self, "ncc", 0) + 1
        op.dsem = ("cc", self.ncc)
        op.dval = 1
        return op

    def dma(self, q, out, in_, reads=(), writes=(), **kw):
        return self.add(q, lambda e: e.dma_start(out=out, in_=in_, **kw), reads, writes, dma=True)

    def emit(self):
        nc = self.nc
        for e in ENGS:
            k = 0
            for op in self.ops[e]:
                if op.sig and not op.dma:
                    k += 1
                    op.idx = k
        esem = {e: nc.alloc_semaphore("es_" + e) for e in ENGS}
        dsem = {}
        for e in ENGS:
            for j in range(min(self.NDS, self.ndma[e])):
                dsem[(e, j)] = nc.alloc_semaphore("ds_%s_%d" % (e, j))
        for j in range(1, getattr(self, "ncc", 0) + 1):
            dsem[("cc", j)] = nc.alloc_semaphore("cc_%d" % j)
        handles = {"pe": "tensor", "act": "scalar", "dve": "vector", "pool": "gpsimd", "sp": "sync"}
        with nc.Block() as block:
            for e in ENGS:
                ops = self.ops[e]
                if not ops:
                    continue

                def body(h, e=e, ops=ops):
                    waited = {}

                    def wait(key, sem, val):
                        if waited.get(key, 0) >= val:
                            return
                        waited[key] = val
                        h.wait_ge(sem, val)

                    for op in ops:
                        for d in op.deps:
                            if d.dma:
                                wait(d.dsem, dsem[d.dsem], d.dval)
                            else:
                                wait(d.eng, esem[d.eng], d.idx)
                        if op.dma and op.dsem[0] == "cc":
                            op.fn(h).then_inc(dsem[op.dsem], 1)
                        elif op.dma:
                            if op.dval > 16:
                                wait(op.dsem, dsem[op.dsem], op.dval - 16)
                            op.fn(h).then_inc(dsem[op.dsem], 16)
                        else:
                            ins = op.fn(h)
                            if op.sig:
                                ins.then_inc(esem[e], 1)
                    last = {}
                    for op in ops:
                        if op.dma:
                            last[op.dsem] = op.dval
                    for k, v in last.items():
                        wait(k, dsem[k], v)

                getattr(block, handles[e])(body)


def V(t, off, dims, p0=0, npart=128):
    shp = list(t.shape)
    ps = 1
    for s in shp[1:]:
        ps *= int(s)
    return bass.AP(tensor=t, offset=p0 * ps + off, ap=[[ps, npart]] + [list(d) for d in dims])


SB_BASE = 16512
SB_TOP = 229344
_DT_SIZE = {F32: 4, BF16: 2, I32: 4}


class Arena:
    def __init__(self, nc):
        self.nc = nc
        self.p = SB_BASE + 32
        self.k = 0

    def alloc(self, name, shape, dtype):
        n = _DT_SIZE[dtype]
        for s in shape[1:]:
            n *= int(s)
        off = self.p
        self.p = (off + n + 31) // 32 * 32
        assert self.p <= SB_TOP, "SBUF arena overflow %s: %d > %d" % (name, self.p, SB_TOP)
        self.k += 1
        return self.nc.alloc_sbuf_tensor_at("%s_%d" % (name, self.k), list(shape), dtype, offset=off)

    def mark(self):
        return self.p

    def release(self, m):
        self.p = m


class Rot:
    def __init__(self, nc, name, shape, dtype, n, psum=False):
        self.items = []
        for i in range(n):
            if psum:
                t = nc.alloc_psum_tensor("%s%d" % (name, i), shape, dtype)
            else:
                t = nc.alloc("%s%d" % (name, i), shape, dtype)
            self.items.append((t, Buf("%s%d" % (name, i), excl=psum)))
        self.i = 0

    def get(self):
        it = self.items[self.i % len(self.items)]
        self.i += 1
        return it


TW = 256
TILES = [(i * TW, TW, (2 * i, 2 * i + 1)) for i in range(TP // TW)] + [(TP, TS, (16,))]


def build(stages=("all",)):
    nc = bass.Bass("TRN2", target_bir_lowering=False)
    S = Sched(nc)
    AR = Arena(nc)
    ALL = "all" in stages

    def phase_end(m):
        AR.release(m)
        S.barrier()

    class StopBuild(Exception):
        pass

    def dbg(name, sb_ap, shape, reads, dt=F32):
        if ("dbg_" + name) in stages or "dbg_all" in stages:
            o = dout("dbg_" + name, shape, dt)
            S.dma("sp", o, sb_ap, reads=reads)

    def stop(name):
        if ("stop_" + name) in stages:
            raise StopBuild()

    def on(s):
        return ALL or s in stages

    def din(name, shape, dt=F32):
        return nc.dram_tensor(name, list(shape), dt, kind="ExternalInput").ap()

    def dout(name, shape, dt=F32):
        return nc.dram_tensor(name, list(shape), dt, kind="ExternalOutput").ap()

    xp = din("xp", [TP, D])
    xs = din("xs", [TS, D])
    ident_d = din("ident", [128, 128])
    norm_mix = din("norm_mix", [4, D])
    norm_mlp = din("norm_mlp", [4, D])
    norm_kv = din("norm_kv", [D])
    b_glu = din("b_glu", [2, 2 * D])
    w_mlp_in = din("w_mlp_in", [4, D, DFF])
    w_mlp_out = din("w_mlp_out", [4, DFF, D])
    y_p = dout("y_p", [TP, D])
    y_s = dout("y_s", [TS, D])

    XT_d = nc.dram_tensor("XT_d", [128, 8, NT], F32).ap()
    UT_d = nc.dram_tensor("UT_d", [128, 8, NT], BF16).ap()
    XT_b = [Buf("XT_%d" % i) for i in range(17)]
    UT_b = [Buf("UT_%d" % i) for i in range(17)]

    ident_f = AR.alloc("ident_f", [128, 128], F32)
    ident_b = AR.alloc("ident_b", [128, 128], BF16)
    ones_b = AR.alloc("ones_b", [128, 128], BF16)
    B_identf, B_identb, B_ones = Buf(), Buf(), Buf()
    S.dma("sp", ident_f[:, :], ident_d, writes=[B_identf])
    S.dve(lambda e: e.tensor_copy(out=ident_b[:, :], in_=ident_f[:, :]), reads=[B_identf], writes=[B_identb])
    S.dve(lambda e: e.memset(ones_b[:, :], 1.0), writes=[B_ones])

    psA = Rot(nc, "psA", [128, 512], F32, 8, psum=True)

    vec_in = AR.alloc("vec_in", [104, 128], F32)
    VEC = AR.alloc("VEC", [128, 104], F32)
    B_vecin, B_vec = Buf(), Buf()
    S.dma("sp", vec_in[0:32, :], norm_mix.rearrange("l (q p) -> (l q) p", p=128), writes=[B_vecin])
    S.dma("sp", vec_in[32:64, :], norm_mlp.rearrange("l (q p) -> (l q) p", p=128), writes=[B_vecin])
    S.dma("sp", vec_in[64:72, :], norm_kv.rearrange("(q p) -> q p", p=128), writes=[B_vecin])
    S.dma("sp", vec_in[72:104, :], b_glu.rearrange("l (q p) -> (l q) p", p=128), writes=[B_vecin])
    ps, pb = psA.get()
    S.pe(lambda e, ps=ps: e.transpose(out=ps[:, 0:104], in_=vec_in[:, :], identity=ident_f[0:104, 0:104]),
         reads=[B_vecin, B_identf], writes=[pb])
    S.dve(lambda e, ps=ps: e.tensor_copy(out=VEC[:, :], in_=ps[:, 0:104]), reads=[pb], writes=[B_vec])

    def blk_rows(b):
        return 128 if b < 16 else TS

    evq = [0]

    def evac_copy(out, in_, reads, writes, eng=None):
        evq[0] += 1
        if eng == "act" or (eng is None and evq[0] % 2):
            S.act(lambda e: e.copy(out=out, in_=in_), reads, writes)
        else:
            S.dve(lambda e: e.tensor_copy(out=out, in_=in_), reads, writes)

    FUSE_IO = ALL
    m0 = AR.mark()
    xin = Rot(AR, "xin", [128, D], F32, 2)
    xst = Rot(AR, "xst", [128, 8, 128], F32, 2)
    for b in range(0 if FUSE_IO else 17):
        n = blk_rows(b)
        t, tb = xin.get()
        src = xp[b * 128:(b + 1) * 128, :] if b < 16 else xs[:, :]
        S.dma("sp", t[0:n, :], src, writes=[tb])
        st, stb = xst.get()
        for half in range(2):
            ps, pb = psA.get()
            for qq in range(4):
                q = half * 4 + qq
                S.pe(lambda e, ps=ps, t=t, q=q, qq=qq, n=n: e.transpose(
                    out=ps[:, qq * 128:qq * 128 + n], in_=t[0:n, q * 128:(q + 1) * 128],
                    identity=ident_f[0:n, 0:n]), reads=[tb, B_identf], writes=[pb])
            evac_copy(st[:, half * 4:half * 4 + 4, 0:n],
                      V(ps, 0, [[128, 4], [1, n]]), [pb], [stb])
        S.dma("pool", XT_d[:, :, b * 128:b * 128 + n], st[:, :, 0:n], reads=[stb], writes=[XT_b[b]])

    phase_end(m0)

    xt_r = Rot(AR, "xt", [128, 8, TW], F32, 2)
    ut_r = Rot(AR, "ut", [128, 8, TW], BF16, 2)
    sq_r = Rot(AR, "sq", [128, 8, TW], BF16, 1)
    rs_r = Rot(AR, "rs", [128, TW], F32, 2)
    rstd_r = Rot(AR, "rstd", [128, TW], F32, 2)

    def load_xt(tile):
        c0, n, bl = tile
        t, tb = xt_r.get()
        S.dma("sp", t[:, :, 0:n], XT_d[:, :, c0:c0 + n], reads=[XT_b[i] for i in bl], writes=[tb])
        return t, tb

    def load_xt_input(tile, xin1, b_xin1):
        c0, n, bl = tile
        t, tb = xt_r.get()
        for sb, blk in enumerate(bl):
            nn = blk_rows(blk)
            srcr = xp[blk * 128:(blk + 1) * 128, :] if blk < 16 else xs[:, :]
            S.dma("sp", V(xin1, 0, [[1, D]], npart=nn), srcr, writes=[b_xin1])
            for half in range(2):
                ps, pb = psA.get()
                for qq in range(4):
                    q = half * 4 + qq
                    S.pe(lambda e, ps=ps, q=q, qq=qq, nn=nn: e.transpose(
                        out=ps[:, qq * 128:qq * 128 + nn], in_=V(xin1, q * 128, [[1, 128]], npart=nn),
                        identity=ident_f[0:nn, 0:nn]), reads=[b_xin1, B_identf], writes=[pb])
                evac_copy(t[:, half * 4:half * 4 + 4, sb * 128:sb * 128 + nn],
                          V(ps, 0, [[128, 4], [1, nn]]), [pb], [tb])
        store_xt(tile, t, tb)
        return t, tb

    def store_xt(tile, t, tb):
        c0, n, bl = tile
        S.dma("pool", XT_d[:, :, c0:c0 + n], t[:, :, 0:n], reads=[tb], writes=[XT_b[i] for i in bl])

    def rmsnorm(xt, xb, n, grow, out=None):
        sq, sqb = sq_r.get()
        S.act(lambda e: e.activation(out=sq[:, :, 0:n], in_=xt[:, :, 0:n], func=AF.Square), reads=[xb], writes=[sqb])
        ps, pb = psA.get()
        for q in range(8):
            S.pe(lambda e, q=q: e.matmul(out=ps[:, 0:n], lhsT=ones_b[:, :], rhs=sq[:, q, 0:n],
                                         start=(q == 0), stop=(q == 7)), reads=[sqb, B_ones], writes=[pb])
        rs, rsb = rs_r.get()
        S.act(lambda e: e.activation(out=rs[:, 0:n], in_=ps[:, 0:n], func=AF.Sqrt, bias=EPS, scale=1.0 / D),
              reads=[pb], writes=[rsb])
        rstd, rstdb = rstd_r.get()
        S.dve(lambda e: e.reciprocal(out=rstd[:, 0:n], in_=rs[:, 0:n]), reads=[rsb], writes=[rstdb])
        if out is None:
            ut, utb = ut_r.get()
            c0 = 0
        else:
            ut, utb, c0 = out
        for q in range(8):
            S.dve(lambda e, q=q: e.scalar_tensor_tensor(
                out=ut[:, q, c0:c0 + n], in0=xt[:, q, 0:n], scalar=VEC[:, grow + q:grow + q + 1], in1=rstd[:, 0:n],
                op0=ALU.mult, op1=ALU.mult), reads=[xb, rstdb, B_vec], writes=[utb])
        return ut, utb

    a_re_d = din("ssm_a_re", [2, 64, 64])
    a_im_d = din("ssm_a_im", [2, 64, 64])
    ldt_d = din("ssm_log_dt", [2, 64, 64])
    b_re_d = din("ssm_b_re", [2, 64, 64, 16])
    b_im_d = din("ssm_b_im", [2, 64, 64, 16])
    c_re_d = din("ssm_c_re", [2, 64, 16, 64])
    c_im_d = din("ssm_c_im", [2, 64, 16, 64])
    d_d = din("ssm_d", [2, D])
    w_glu_d = din("w_glu", [2, D, 2 * D])
    st_re_d = din("st_re", [2, TS, 4096])
    st_im_d = din("st_im", [2, TS, 4096])
    ev_d = din("ev", [128, 25])
    cmask_d = din("cmask", [128, 128])
    flag_d = din("flag", [128, 1])
    sp_state = dout("sp_state", [2, 64, 128])
    ss_state = dout("ss_state", [2, 2, TS, 4096])
    UG_d = nc.dram_tensor("UG_d", [2, 128, 64, 128], BF16).ap()
    VS_d = nc.dram_tensor("VS_d", [2, 128, 64, 128], BF16).ap()
    cc_in = [nc.dram_tensor("cc_in%d" % i, [128, 64], F32) for i in range(2)]
    cc_out = [nc.dram_tensor("cc_out%d" % i, [256, 64], F32) for i in range(2)]
    UG_b = [Buf(), Buf()]
    VS_b = [Buf(), Buf()]

    EV = AR.alloc("EV", [128, 25], F32)
    CMASK = AR.alloc("CMASK", [128, 128], F32)
    FLAG = AR.alloc("FLAG", [128, 1], F32)
    B_c2 = Buf()
    S.dma("sp", EV[:, :], ev_d, writes=[B_c2])
    S.dma("sp", CMASK[:, :], cmask_d, writes=[B_c2])
    S.dma("sp", FLAG[:, :], flag_d, writes=[B_c2])

    TWO_PI = 2.0 * math.pi
    PI_LO = 3.1415925

    def TB(name, shape, dt):
        return AR.alloc(name, shape, dt), Buf(name)

    def s5_layer(l):
        m_layer = AR.mark()
        WVT = AR.alloc("WVT", [128, 32, 2, 128], BF16)
        WO = AR.alloc("WO", [128, 32, 2, 128], BF16)
        WK = AR.alloc("WK", [128, 64, 128], BF16)
        A8x = AR.alloc("A8x", [128, 64], F32)
        Bm8 = AR.alloc("Bm8", [128, 64], F32)
        A1x = AR.alloc("A1x", [128, 64], F32)
        Bm1 = AR.alloc("Bm1", [128, 64], F32)
        ARx = AR.alloc("ARx", [128, 64], F32)
        BmR = AR.alloc("BmR", [128, 64], F32)
        b_fin = Buf("fin")
        m_prep = AR.mark()
        bp = Buf("prep_small")

        def dv(fn, extra_r=(), extra_w=()):
            S.dve(fn, reads=[bp] + list(extra_r), writes=[bp] + list(extra_w))

        def ac(fn, extra_r=(), extra_w=()):
            S.act(fn, reads=[bp] + list(extra_r), writes=[bp] + list(extra_w))

        par_in = AR.alloc("par_in", [96, 128], F32)
        S.dma("sp", par_in[0:32, :], a_re_d[l].rearrange("(i h) p -> i (h p)", h=2), writes=[bp])
        S.dma("sp", par_in[32:64, :], a_im_d[l].rearrange("(i h) p -> i (h p)", h=2), writes=[bp])
        S.dma("sp", par_in[64:96, :], ldt_d[l].rearrange("(i h) p -> i (h p)", h=2), writes=[bp])
        BRE = AR.alloc("BRE", [128, 32, 16], F32)
        BIM = AR.alloc("BIM", [128, 32, 16], F32)
        BBR = AR.alloc("BBR", [128, 32, 16], F32)
        BBI = AR.alloc("BBI", [128, 32, 16], F32)
        BT = AR.alloc("BT", [128, 32, 16], F32)
        b_bld = Buf("bload")
        S.dma("sp", BRE[:, :, :], b_re_d[l].rearrange("(i h) p c -> (h p) i c", h=2), writes=[b_bld])
        S.dma("sp", BIM[:, :, :], b_im_d[l].rearrange("(i h) p c -> (h p) i c", h=2), writes=[b_bld])
        b_cev = Buf('cev')
        b_cld = Buf("cload")
        CRE = AR.alloc("CRE", [128, 32, 16], F32)
        CIM = AR.alloc("CIM", [128, 32, 16], F32)
        CINs = []
        for (srcd, nm) in ((c_re_d, "cr"), (c_im_d, "ci")):
            CIN = AR.alloc("CIN" + nm, [128, 4, 2, 64], F32)
            sv = srcd[l].rearrange("(i4 i8 h) c p -> i8 c i4 h p", i8=8, h=2)
            for i8 in range(8):
                for hh in range(2):
                    S.dma("sp", CIN[i8 * 16:(i8 + 1) * 16, :, hh, :], sv[i8][:, :, hh, :], writes=[b_cld])
            CINs.append(CIN)
        ps, pb = psA.get()
        S.pe(lambda e, ps=ps: e.transpose(out=ps[:, 0:96], in_=par_in[:, :], identity=ident_f[0:96, 0:96]),
             reads=[bp, B_identf], writes=[pb])
        PAR = AR.alloc("PAR", [128, 96], F32)
        dv(lambda e, ps=ps: e.tensor_copy(out=PAR[:, :], in_=ps[:, 0:96]), [pb])
        for (CIN, dst) in zip(CINs, (CRE, CIM)):
            ps, pb = psA.get()
            for i4 in range(4):
                S.pe(lambda e, ps=ps, i4=i4, CIN=CIN: e.transpose(
                    out=ps[:, i4 * 128:(i4 + 1) * 128], in_=V(CIN, i4 * 128, [[1, 128]]), identity=ident_f[:, :]),
                    reads=[b_cld, B_identf], writes=[pb])
            S.act(lambda e, ps=ps, dst=dst: e.copy(out=V(dst, 0, [[1, 512]]), in_=ps[:, :]), reads=[pb], writes=[b_cev])
        ARE = PAR[:, 0:32]
        AIM = PAR[:, 32:64]
        DT = AR.alloc("DT", [128, 32], F32)
        XR = AR.alloc("XR", [128, 32], F32)
        XI = AR.alloc("XI", [128, 32], F32)
        def exp_taylor(dst, src, deg):
            dv(lambda e: e.tensor_scalar(out=dst, in0=src, scalar1=1.0 / deg, scalar2=1.0, op0=ALU.mult, op1=ALU.add))
            for k in range(deg - 1, 0, -1):
                dv(lambda e, k=k: e.scalar_tensor_tensor(out=dst, in0=dst, scalar=1.0 / k, in1=src,
                                                         op0=ALU.mult, op1=ALU.mult))
                dv(lambda e: e.tensor_scalar(out=dst, in0=dst, scalar1=1.0, scalar2=None, op0=ALU.add))

        NI = AR.alloc("NI", [128, 32], I32)
        NF = AR.alloc("NF", [128, 32], F32)
        RX = AR.alloc("RX", [128, 32], F32)
        I2 = AR.alloc("I2", [128, 32], I32)
        LDT = PAR[:, 64:96]
        dv(lambda e: e.tensor_scalar(out=NI[:, :], in0=LDT, scalar1=1.0 / math.log(2.0), scalar2=None, op0=ALU.mult))
        dv(lambda e: e.tensor_copy(out=NF[:, :], in_=NI[:, :]))
        dv(lambda e: e.scalar_tensor_tensor(out=RX[:, :], in0=NF[:, :], scalar=-0.693359375, in1=LDT,
                                            op0=ALU.mult, op1=ALU.add))
        dv(lambda e: e.scalar_tensor_tensor(out=RX[:, :], in0=NF[:, :], scalar=2.12194440e-4, in1=RX[:, :],
                                            op0=ALU.mult, op1=ALU.add))
        exp_taylor(DT[:, :], RX[:, :], 12)
        dv(lambda e: e.tensor_scalar(out=NI[:, :], in0=NF[:, :], scalar1=127.0, scalar2=None, op0=ALU.add))
        dv(lambda e: e.tensor_scalar(out=I2[:, :], in0=NI[:, :], scalar1=23, scalar2=None, op0=ALU.logical_shift_left))
        dv(lambda e: e.tensor_tensor(out=DT[:, :], in0=DT[:, :], in1=I2[:, :].bitcast(F32), op=ALU.mult))
        dv(lambda e: e.tensor_tensor(out=XR[:, :], in0=ARE, in1=DT[:, :], op=ALU.mult))
        dv(lambda e: e.tensor_tensor(out=XI[:, :], in0=AIM, in1=DT[:, :], op=ALU.mult))
        NE = 25
        ANG = AR.alloc("ANG", [128, NE, 32], F32)
        MAG = AR.alloc("MAG", [128, NE, 32], F32)
        QF = AR.alloc("QF", [128, NE, 32], F32)
        QI = AR.alloc("QI", [128, NE, 32], I32)
        RR = AR.alloc("RR", [128, NE, 32], F32)
        W1 = AR.alloc("W1", [128, NE, 32], F32)
        W2 = AR.alloc("W2", [128, NE, 32], F32)
        LR = AR.alloc("LR", [128, NE, 32], F32)
        LI = AR.alloc("LI", [128, NE, 32], F32)
        ev_b = V(EV, 0, [[1, NE], [0, 32]])
        dv(lambda e: e.tensor_tensor(out=ANG[:, :, :], in0=V(XI, 0, [[0, NE], [1, 32]]), in1=ev_b, op=ALU.mult), [B_c2])
        dv(lambda e: e.tensor_tensor(out=MAG[:, :, :], in0=V(XR, 0, [[0, NE], [1, 32]]), in1=ev_b, op=ALU.mult), [B_c2])
        dv(lambda e: e.tensor_scalar(out=MAG[:, :, :], in0=MAG[:, :, :], scalar1=0.125, scalar2=None, op0=ALU.mult))
        exp_taylor(QF[:, :, :], MAG[:, :, :], 10)
        dv(lambda e: e.tensor_tensor(out=MAG[:, :, :], in0=QF[:, :, :], in1=QF[:, :, :], op=ALU.mult))
        dv(lambda e: e.tensor_tensor(out=QF[:, :, :], in0=MAG[:, :, :], in1=MAG[:, :, :], op=ALU.mult))
        dv(lambda e: e.tensor_tensor(out=MAG[:, :, :], in0=QF[:, :, :], in1=QF[:, :, :], op=ALU.mult))
        dv(lambda e: e.tensor_scalar(out=QI[:, :, :], in0=ANG[:, :, :], scalar1=1.0 / TWO_PI, scalar2=None, op0=ALU.mult))
        dv(lambda e: e.tensor_copy(out=QF[:, :, :], in_=QI[:, :, :]))
        dv(lambda e: e.scalar_tensor_tensor(out=RR[:, :, :], in0=QF[:, :, :], scalar=-TWO_PI, in1=ANG[:, :, :],
                                            op0=ALU.mult, op1=ALU.add))

        def wrap_sin(dst, shift):
            dv(lambda e: e.tensor_scalar(out=W1[:, :, :], in0=RR[:, :, :], scalar1=shift, scalar2=None, op0=ALU.add))
            dv(lambda e: e.tensor_scalar(out=W2[:, :, :], in0=W1[:, :, :], scalar1=math.pi, scalar2=TWO_PI,
                                         op0=ALU.is_gt, op1=ALU.mult))
            dv(lambda e: e.tensor_tensor(out=W1[:, :, :], in0=W1[:, :, :], in1=W2[:, :, :], op=ALU.subtract))
            dv(lambda e: e.tensor_scalar(out=W2[:, :, :], in0=W1[:, :, :], scalar1=-math.pi, scalar2=TWO_PI,
                                         op0=ALU.is_lt, op1=ALU.mult))
            dv(lambda e: e.tensor_tensor(out=W1[:, :, :], in0=W1[:, :, :], in1=W2[:, :, :], op=ALU.add))
            dv(lambda e: e.tensor_scalar(out=W1[:, :, :], in0=W1[:, :, :], scalar1=PI_LO, scalar2=-PI_LO,
                                         op0=ALU.min, op1=ALU.max))
            ac(lambda e: e.activation(out=W2[:, :, :], in_=W1[:, :, :], func=AF.Sin))
            dv(lambda e: e.tensor_tensor(out=dst[:, :, :], in0=W2[:, :, :], in1=MAG[:, :, :], op=ALU.mult))

        wrap_sin(LI, 0.0)
        wrap_sin(LR, math.pi / 2)

        sm = [AR.alloc("sm%d" % i, [128, 32], F32) for i in range(8)]
        LBR = LR[:, 8, :]
        LBI = LI[:, 8, :]
        NR, DEN, U1, U2, FR, FI, U3, U4 = sm
        dv(lambda e: e.tensor_scalar(out=NR[:, :], in0=LBR, scalar1=-1.0, scalar2=None, op0=ALU.add))
        dv(lambda e: e.tensor_tensor(out=U1[:, :], in0=ARE, in1=ARE, op=ALU.mult))
        dv(lambda e: e.tensor_tensor(out=U2[:, :], in0=AIM, in1=AIM, op=ALU.mult))
        dv(lambda e: e.tensor_tensor(out=DEN[:, :], in0=U1[:, :], in1=U2[:, :], op=ALU.add))
        dv(lambda e: e.reciprocal(out=DEN[:, :], in_=DEN[:, :]))
        dv(lambda e: e.tensor_tensor(out=U1[:, :], in0=NR[:, :], in1=ARE, op=ALU.mult))
        dv(lambda e: e.tensor_tensor(out=U2[:, :], in0=LBI, in1=AIM, op=ALU.mult))
        dv(lambda e: e.tensor_tensor(out=U1[:, :], in0=U1[:, :], in1=U2[:, :], op=ALU.add))
        dv(lambda e: e.tensor_tensor(out=FR[:, :], in0=U1[:, :], in1=DEN[:, :], op=ALU.mult))
        dv(lambda e: e.tensor_tensor(out=U3[:, :], in0=LBI, in1=ARE, op=ALU.mult))
        dv(lambda e: e.tensor_tensor(out=U4[:, :], in0=NR[:, :], in1=AIM, op=ALU.mult))
        dv(lambda e: e.tensor_tensor(out=U3[:, :], in0=U3[:, :], in1=U4[:, :], op=ALU.subtract))
        dv(lambda e: e.tensor_tensor(out=FI[:, :], in0=U3[:, :], in1=DEN[:, :], op=ALU.mult))

        for (Ax, Bm, m) in ((A8x, Bm8, 15), (A1x, Bm1, 8), (ARx, BmR, 24)):
            dv(lambda e, Ax=Ax, m=m: e.tensor_copy(out=V(Ax, 0, [[32, 2], [1, 32]]), in_=V(LR, m * 32, [[0, 2], [1, 32]])),
               extra_w=[b_fin])
            dv(lambda e, Bm=Bm, m=m: e.tensor_scalar(out=Bm[:, 0:32], in0=LI[:, m, :], scalar1=-1.0, scalar2=None,
                                                     op0=ALU.mult), extra_w=[b_fin])
            dv(lambda e, Bm=Bm, m=m: e.tensor_copy(out=Bm[:, 32:64], in_=LI[:, m, :]), extra_w=[b_fin])

        fr_b = V(FR, 0, [[1, 32], [0, 16]])
        fi_b = V(FI, 0, [[1, 32], [0, 16]])
        dv(lambda e: e.tensor_tensor(out=BBR[:, :, :], in0=BRE[:, :, :], in1=fr_b, op=ALU.mult), [b_bld])
        dv(lambda e: e.tensor_tensor(out=BT[:, :, :], in0=BIM[:, :, :], in1=fi_b, op=ALU.mult))
        dv(lambda e: e.tensor_tensor(out=BBR[:, :, :], in0=BBR[:, :, :], in1=BT[:, :, :], op=ALU.subtract))
        dv(lambda e: e.tensor_tensor(out=BBI[:, :, :], in0=BIM[:, :, :], in1=fr_b, op=ALU.mult))
        dv(lambda e: e.tensor_tensor(out=BT[:, :, :], in0=BRE[:, :, :], in1=fi_b, op=ALU.mult))
        dv(lambda e: e.tensor_tensor(out=BBI[:, :, :], in0=BBI[:, :, :], in1=BT[:, :, :], op=ALU.add))

        tmpA = AR.alloc("tmpA", [128, 32, 8, 16], F32)
        tmpB = AR.alloc("tmpB", [128, 32, 8, 16], F32)
        XK = AR.alloc("XK", [128, 32, 2, 128], BF16)
        m_xv = AR.mark()
        XV = AR.alloc("XV", [128, 32, 2, 128], BF16)

        def lam_v(T, m0):
            return V(T, m0 * 32, [[1, 32], [32, 8], [0, 16]])

        def coef_v(T):
            return V(T, 0, [[16, 32], [0, 8], [1, 16]])

        def xout(T, ri):
            return V(T, ri * 128, [[256, 32], [16, 8], [1, 16]])

        def build_x(dst, m0, Pre, Pim, conj_neg):
            dv(lambda e: e.tensor_tensor(out=tmpA[:, :, :, :], in0=lam_v(LR, m0), in1=coef_v(Pre), op=ALU.mult), [b_cev])
            dv(lambda e: e.tensor_tensor(out=tmpB[:, :, :, :], in0=lam_v(LI, m0), in1=coef_v(Pim), op=ALU.mult))
            dv(lambda e: e.tensor_tensor(out=xout(dst, 0), in0=tmpA[:, :, :, :], in1=tmpB[:, :, :, :], op=ALU.subtract),
               extra_w=[b_fin])
            dv(lambda e: e.tensor_tensor(out=tmpA[:, :, :, :], in0=lam_v(LI, m0), in1=coef_v(Pre), op=ALU.mult))
            dv(lambda e: e.tensor_tensor(out=tmpB[:, :, :, :], in0=lam_v(LR, m0), in1=coef_v(Pim), op=ALU.mult))
            if conj_neg:
                dv(lambda e: e.scalar_tensor_tensor(out=xout(dst, 1), in0=tmpA[:, :, :, :], scalar=-1.0,
                                                    in1=tmpB[:, :, :, :], op0=ALU.mult, op1=ALU.subtract), extra_w=[b_fin])
            else:
                dv(lambda e: e.tensor_tensor(out=xout(dst, 1), in0=tmpA[:, :, :, :], in1=tmpB[:, :, :, :], op=ALU.add),
                   extra_w=[b_fin])

        build_x(XV, 0, BBR, BBI, False)
        build_x(XK, 16, BBR, BBI, False)
        build_x(WO, 8, CRE, CIM, True)

        for bk in range(8):
            ps, pb = psA.get()
            psb = ps[:, :].bitcast(BF16)
            for n in range(8):
                blk = bk * 8 + n
                S.pe(lambda e, psb=psb, n=n, blk=blk: e.transpose(
                    out=psb[:, n * 128:(n + 1) * 128], in_=V(XV, blk * 128, [[1, 128]]), identity=ident_b[:, :]),
                    reads=[bp, B_identb], writes=[pb])
            evac_copy(V(WVT, bk * 1024, [[1, 1024]]), psb[:, :], [pb], [b_fin])

        AR.release(m_xv)
        S.barrier()
        D8a = AR.alloc("D8a", [64, 16], F32)
        D8 = AR.alloc("D8", [64, 8, 16], F32)
        DPART = AR.alloc("DPART", [128, 64], F32)
        DIAG = AR.alloc("DIAG", [128, 64, 128], BF16)
        S.dma("sp", D8a[:, :], d_d[l].rearrange("(g c) -> g c", c=16), writes=[bp])
        dv(lambda e: e.tensor_copy(out=D8[:, :, :], in_=V(D8a, 0, [[0, 8], [1, 16]], npart=64)))
        ps, pb = psA.get()
        S.pe(lambda e, ps=ps: e.transpose(out=ps[:, 0:64], in_=V(D8, 0, [[1, 128]], npart=64), identity=ident_f[0:64, 0:64]),
             reads=[bp, B_identf], writes=[pb])
        dv(lambda e, ps=ps: e.tensor_copy(out=DPART[:, :], in_=ps[:, 0:64]), [pb])
        dv(lambda e: e.tensor_tensor(out=DIAG[:, :, :], in0=V(ident_f, 0, [[0, 64], [1, 128]]),
                                     in1=V(DPART, 0, [[1, 64], [0, 128]]), op=ALU.mult), [B_identf])
        for h in range(2):
            for i0 in range(0, 32, 4):
                banks = [psA.get() for _ in range(4)]
                for ri in range(2):
                    for ii in range(4):
                        i = i0 + ii
                        ps, pb = banks[ii]
                        S.pe(lambda e, ps=ps, i=i, ri=ri, h=h: e.matmul(
                            out=ps[:, 0:128],
                            lhsT=V(XK, (i * 2 + ri) * 128, [[1, 128]], p0=h * 64, npart=64),
                            rhs=V(WO, (i * 2 + ri) * 128, [[1, 128]], p0=h * 64, npart=64),
                            start=(ri == 0), stop=False), reads=[bp, b_fin], writes=[pb])
                for ii in range(4):
                    g = 2 * (i0 + ii) + h
                    ps, pb = banks[ii]
                    S.pe(lambda e, ps=ps, g=g: e.matmul(
                        out=ps[:, 0:128], lhsT=V(DIAG, g * 128, [[1, 128]]), rhs=ident_b[:, :],
                        start=False, stop=True), reads=[bp, B_identb], writes=[pb])
                for ii in range(4):
                    g = 2 * (i0 + ii) + h
                    ps, pb = banks[ii]
                    dv(lambda e, ps=ps, g=g: e.tensor_tensor(
                        out=V(WK, g * 128, [[1, 128]]), in0=ps[:, 0:128], in1=CMASK[:, :], op=ALU.mult),
                        [pb, B_c2], [b_fin])

        dbg("LR", LR[:, :, :], [128, NE, 32], [bp])
        dbg("LI", LI[:, :, :], [128, NE, 32], [bp])
        dbg("FR", FR[:, :], [128, 32], [bp])
        dbg("DT", DT[:, :], [128, 32], [bp])
        dbg("WVT", WVT[:, :, :, :], [128, 32, 2, 128], [b_fin], BF16)
        dbg("WO", WO[:, :, :, :], [128, 32, 2, 128], [b_fin], BF16)
        dbg("WK", WK[:, :, :], [128, 64, 128], [b_fin], BF16)
        dbg("A8x", A8x[:, :], [128, 64], [b_fin])
        dbg("Bm8", Bm8[:, :], [128, 64], [b_fin])
        stop("prep")
        AR.release(m_prep)
        S.barrier()

        T1, b_t1 = TB("T1", [128, 64], F32)
        T2, b_t2 = TB("T2", [128, 64], F32)
        T3, b_t3 = TB("T3", [128, 64], F32)
        ZST, b_zst = TB("ZST", [128, 64], F32)
        SI, b_si = TB("SI", [128, 64], F32)
        Ugs, b_ugs = TB("Ugs", [128, 64, 16], BF16)
        m_work = AR.mark()
        UTS, b_uts = TB("UTS", [128, 8, 1024], BF16)
        Utok, b_utok = TB("Utok", [128, 8, 1024], BF16)
        ug_r = Rot(AR, "Ug", [128, 64, 128], BF16, 2)
        vss_r = Rot(AR, "VSS", [128, 64, 129], BF16, 2)
        RB = 8
        MB = 128 // RB
        SINI, b_sini = TB("SINI", [128, 64], F32)
        BSUM = [TB("BSUM%d" % i, [128, 64, MB], F32) for i in range(2)]
        TT1, b_tt1 = TB("TT1", [128, 64, MB], F32)
        TT2, b_tt2 = TB("TT2", [128, 64, MB], F32)
        zh_r = Rot(AR, "ZH", [128, MB, 64], F32, 1)
        S.dve(lambda e: e.memset(ZST[:, :], 0.0), writes=[b_zst])

        def cmul_add(P, b_p, VSt, b_v, col0):
            S.dve(lambda e: e.tensor_tensor(out=TT1[:, :, :], in0=P[:, :, :], in1=V(A8x, 0, [[1, 64], [0, MB]]), op=ALU.mult),
                  reads=[b_p, b_fin], writes=[b_tt1])
            S.dve(lambda e: e.tensor_tensor(out=V(TT2, 0, [[32 * MB, 2], [MB, 32], [1, MB]]),
                                            in0=V(P, 32 * MB, [[-32 * MB, 2], [MB, 32], [1, MB]]),
                                            in1=V(Bm8, 0, [[32, 2], [1, 32], [0, MB]]), op=ALU.mult),
                  reads=[b_p, b_fin], writes=[b_tt2])
            S.dve(lambda e: e.tensor_tensor(out=TT1[:, :, :], in0=TT1[:, :, :], in1=TT2[:, :, :], op=ALU.add),
                  reads=[b_tt1, b_tt2], writes=[b_tt1])
            S.dve(lambda e: e.tensor_tensor(out=P[:, :, :], in0=TT1[:, :, :], in1=V(VSt, col0, [[129, 64], [RB, MB]]),
                                            op=ALU.add), reads=[b_tt1, b_v], writes=[b_p])

        def scan_blocked(prev, VSt, b_v, hist, st):
            pt, pbuf, poff = prev
            ipt, ipbuf, ipoff = prev
            BS, b_bs = BSUM[st]
            if hist:
                S.dve(lambda e, pt=pt, poff=poff: e.tensor_copy(out=SINI[:, :], in_=V(pt, poff, [[1, 64]])),
                      reads=[pbuf], writes=[b_sini])
                ipt, ipbuf, ipoff = SINI, b_sini, 0
            if not hist:
                S.dve(lambda e: e.tensor_copy(out=BS[:, :, :], in_=V(VSt, 1, [[129, 64], [RB, MB]])), reads=[b_v[0]], writes=[b_bs])
                for s in range(1, RB):
                    cmul_add(BS, b_bs, VSt, b_v[s], 1 + s)
            ZH, zb = zh_r.get()
            for m in range(MB):
                S.dve(lambda e, pt=pt, poff=poff: e.tensor_tensor(
                    out=T1[:, :], in0=V(pt, poff, [[1, 64]]), in1=ARx[:, :], op=ALU.mult),
                    reads=[pbuf, b_fin], writes=[b_t1])
                S.dve(lambda e, pt=pt, poff=poff: e.tensor_tensor(
                    out=V(T2, 0, [[32, 2], [1, 32]]), in0=V(pt, poff + 32, [[-32, 2], [1, 32]]),
                    in1=V(BmR, 0, [[32, 2], [1, 32]]), op=ALU.mult), reads=[pbuf, b_fin], writes=[b_t2])
                S.dve(lambda e: e.tensor_tensor(out=T3[:, :], in0=T1[:, :], in1=T2[:, :], op=ALU.add),
                      reads=[b_t1, b_t2], writes=[b_t3])
                S.dve(lambda e, ZH=ZH, m=m: e.tensor_tensor(
                    out=ZH[:, m, :], in0=T3[:, :], in1=V(BS, m, [[MB, 64]]), op=ALU.add),
                    reads=[b_t3, b_bs], writes=[zb])
                pt, pbuf, poff = ZH, zb, m * 64
            if hist:
                S.pool(lambda e: e.tensor_copy(out=V(VSt, 0, [[129, 64]]), in_=V(ipt, ipoff, [[1, 64]])),
                       reads=[ipbuf], writes=[b_v[RB]])
                S.dve(lambda e: e.tensor_copy(out=V(BS, 0, [[MB, 64]]), in_=V(ipt, ipoff, [[1, 64]])),
                      reads=[ipbuf], writes=[b_bs])
                S.dve(lambda e, ZH=ZH: e.tensor_copy(out=V(BS, 1, [[MB, 64], [1, MB - 1]]),
                                                     in_=V(ZH, 0, [[1, 64], [64, MB - 1]])), reads=[zb], writes=[b_bs])
                for s in range(RB - 1):
                    cmul_add(BS, b_bs, VSt, b_v[s], 1 + s)
                    S.dve(lambda e, s=s: e.tensor_copy(out=V(VSt, 1 + s, [[129, 64], [RB, MB]]), in_=BS[:, :, :]),
                          reads=[b_bs], writes=[b_v[s]])
                S.pool(lambda e, ZH=ZH: e.tensor_copy(out=V(VSt, RB, [[129, 64], [RB, MB]]), in_=V(ZH, 0, [[1, 64], [64, MB]])),
                       reads=[zb], writes=[b_v[RB - 1]])
            return pt, pbuf, poff

        def scan(prev, VSt, b_v, nsteps, hist):
            pt, pbuf, poff = prev
            if hist:
                S.pool(lambda e, pt=pt, poff=poff: e.tensor_copy(out=V(VSt, 0, [[129, 64]]), in_=V(pt, poff, [[1, 64]])),
                       reads=[pbuf], writes=[b_v])
            ring = rb = None
            for k in range(nsteps):
                r = k % 32
                if r == 0:
                    ring, rb = zh_r.get()
                S.dve(lambda e, pt=pt, poff=poff: e.tensor_tensor(
                    out=T1[:, :], in0=V(pt, poff, [[1, 64]]), in1=A8x[:, :], op=ALU.mult),
                    reads=[pbuf, b_fin], writes=[b_t1])
                S.dve(lambda e, pt=pt, poff=poff: e.tensor_tensor(
                    out=V(T2, 0, [[32, 2], [1, 32]]), in0=V(pt, poff + 32, [[-32, 2], [1, 32]]),
                    in1=V(Bm8, 0, [[32, 2], [1, 32]]), op=ALU.mult), reads=[pbuf, b_fin], writes=[b_t2])
                S.dve(lambda e: e.tensor_tensor(out=T3[:, :], in0=T1[:, :], in1=T2[:, :], op=ALU.add),
                      reads=[b_t1, b_t2], writes=[b_t3])
                S.dve(lambda e, ring=ring, r=r, k=k: e.tensor_tensor(
                    out=ring[:, r, :], in0=T3[:, :], in1=V(VSt, k + 1, [[129, 64]]), op=ALU.add),
                    reads=[b_t3, b_v], writes=[rb])
                pt, pbuf, poff = ring, rb, r * 64
                if hist and r == 31:
                    k0 = k - 31
                    S.pool(lambda e, ring=ring, k0=k0: e.tensor_copy(
                        out=V(VSt, k0 + 1, [[129, 64], [1, 32]]), in_=V(ring, 0, [[1, 64], [64, 32]])),
                        reads=[rb], writes=[b_v])
            return pt, pbuf, poff

        def stage2(src_t, src_b, dst_t, dst_b, ncol, npart, eng=None):
            for gb in range(8):
                ps, pb = psA.get()
                psb = ps[:, :].bitcast(BF16)
                for gg in range(8):
                    g = gb * 8 + gg
                    S.pe(lambda e, psb=psb, gg=gg, g=g: e.transpose(
                        out=psb[:, gg * 128:gg * 128 + ncol],
                        in_=V(src_t, 128 * g, [[1, 128]], npart=npart), identity=ident_b[0:npart, 0:npart]),
                        reads=[src_b, B_identb], writes=[pb])
                evac_copy(V(dst_t, gb * 8 * ncol, [[ncol, 8], [1, ncol]]),
                          bass.AP(tensor=psb.tensor, offset=0, ap=[[1024, 128], [128, 8], [1, ncol]]), [pb], [dst_b], eng)

        vss_tiles = []
        for st in range(2):
            for tt in range(4):
                tile = TILES[st * 4 + tt]
                xt, xb = load_xt_input(tile, TT1, b_tt1) if (l == 0 and FUSE_IO) else load_xt(tile)
                rmsnorm(xt, xb, TW, 8 * l, out=(UTS, b_uts, tt * TW))
            for j in range(8):
                ps, pb = psA.get()
                psb = ps[:, :].bitcast(BF16)
                for q in range(8):
                    S.pe(lambda e, psb=psb, q=q, j=j: e.transpose(
                        out=psb[:, q * 128:(q + 1) * 128], in_=V(UTS, q * 1024 + j, [[8, 128]]), identity=ident_b[:, :]),
                        reads=[b_uts, B_identb], writes=[pb])
                evac_copy(V(Utok, j * 16, [[128, 64], [1, 16]]),
                          bass.AP(tensor=psb.tensor, offset=0, ap=[[1024, 128], [16, 64], [1, 16]]), [pb], [b_utok], "act")
            Ug, b_ug = ug_r.get()
            stage2(Utok, b_utok, Ug, b_ug, 128, 128, "act")
            VSS, _unused = vss_r.get()
            b_vs = [Buf() for _ in range(RB + 1)]
            for bk in range(16):
                ps, pb = psA.get()
                for ee in range(4):
                    en = bk * 4 + ee
                    ri, i = divmod(en, 32)
                    for h in range(2):
                        S.pe(lambda e, ps=ps, ee=ee, ri=ri, i=i, h=h, Ug=Ug: e.matmul(
                            out=ps[h * 64:(h + 1) * 64, ee * 128:(ee + 1) * 128],
                            lhsT=V(WVT, (i * 2 + ri) * 128 + h * 64, [[1, 64]]),
                            rhs=V(Ug, (2 * i + h) * 128, [[1, 128]]), start=True, stop=True),
                            reads=[b_fin, b_ug], writes=[pb])
                evac_copy(V(VSS, bk * 4 * 129 + 1, [[129, 4], [1, 128]]), V(ps, 0, [[128, 4], [1, 128]]), [pb], list(b_vs), "act")
            vss_tiles.append((Ug, b_ug, VSS, b_vs))
        state = (ZST, b_zst, 0)
        for st in range(2):
            state = scan_blocked(state, vss_tiles[st][2], vss_tiles[st][3], False, st)

        pt, pbuf, poff = state
        dbg("P1", V(pt, poff, [[1, 64]]), [128, 64], [pbuf])
        stop("s5a")
        b_ccin, b_ccout = Buf(), Buf()
        S.dma("pool", cc_in[l].ap(), V(pt, poff, [[1, 64]]), reads=[pbuf], writes=[b_ccin])
        S.cc(lambda e: e.collective_compute("AllGather", ALU.bypass, replica_groups=[[0, 1], [2, 3], [4, 5], [6, 7]],
                                            ins=[cc_in[l].ap().opt()], outs=[cc_out[l].ap().opt()]),
             reads=[b_ccin], writes=[b_ccout])
        S.dma("pool", SI[:, :], cc_out[l].ap()[0:128, :], reads=[b_ccout], writes=[b_si])
        S.dve(lambda e: e.tensor_scalar(out=SI[:, :], in0=SI[:, :], scalar1=FLAG[:, 0:1], scalar2=None, op0=ALU.mult),
              reads=[b_si, B_c2], writes=[b_si])

        dbg("SI", SI[:, :], [128, 64], [b_si])
        stop("xchg")
        tile = TILES[8]
        xt, xb = load_xt_input(tile, TT1, b_tt1) if (l == 0 and FUSE_IO) else load_xt(tile)
        uts, utsb = rmsnorm(xt, xb, TS, 8 * l)
        S.dve(lambda e: e.memset(Utok[0:TS, :, :], 0.0), writes=[b_utok])
        ps, pb = psA.get()
        psb = ps[:, :].bitcast(BF16)
        for q in range(8):
            S.pe(lambda e, psb=psb, q=q, uts=uts: e.transpose(
                out=psb[0:TS, q * 128:(q + 1) * 128], in_=uts[:, q, 0:TS], identity=ident_b[:, :]),
                reads=[utsb, B_identb], writes=[pb])
        S.act(lambda e, psb=psb: e.copy(out=V(Utok, 0, [[128, 64], [1, 16]], npart=TS),
                                        in_=bass.AP(tensor=psb.tensor, offset=0, ap=[[1024, TS], [16, 64], [1, 16]])),
              reads=[pb], writes=[b_utok])
        S.dve(lambda e, psb=psb: e.tensor_copy(out=V(Utok, 7 * 16, [[128, 64], [1, 16]], npart=TS),
                                               in_=bass.AP(tensor=psb.tensor, offset=0, ap=[[1024, TS], [16, 64], [1, 16]])),
              reads=[pb], writes=[b_utok])
        stage2(Utok, b_utok, Ugs, b_ugs, TS, TS)

        state = (SI, b_si, 0)
        tl = vss_tiles
        for st in range(2):
            Ug, b_ug, VSS, b_vs = tl[st]
            state = scan_blocked(state, VSS, b_vs, True, st)
        for st in range(2):
            Ug, b_ug, VSS, b_vs = tl[st]
            for h in range(2):
                for i0 in range(0, 32, 4):
                    banks = [psA.get() for _ in range(4)]
                    for ri in range(2):
                        for ii in range(4):
                            i = i0 + ii
                            ps, pb = banks[ii]
                            S.pe(lambda e, ps=ps, i=i, ri=ri, h=h, VSS=VSS: e.matmul(
                                out=ps[:, 0:128],
                                lhsT=V(VSS, (ri * 32 + i) * 129, [[1, 128]], p0=h * 64, npart=64),
                                rhs=V(WO, (i * 2 + ri) * 128, [[1, 128]], p0=h * 64, npart=64),
                                start=(ri == 0), stop=False), reads=list(b_vs) + [b_fin], writes=[pb])
                    for ii in range(4):
                        g = 2 * (i0 + ii) + h
                        ps, pb = banks[ii]
                        S.pe(lambda e, ps=ps, g=g, Ug=Ug: e.matmul(
                            out=ps[:, 0:128], lhsT=V(Ug, g * 128, [[1, 128]]),
                            rhs=V(WK, g * 128, [[1, 128]]), start=False, stop=True),
                            reads=[b_ug, b_fin], writes=[pb])
                    for ii in range(4):
                        g = 2 * (i0 + ii) + h
                        ps, pb = banks[ii]
                        S.act(lambda e, ps=ps, g=g: e.activation(
                            out=V(Utok, 16 * g, [[1024, 8], [1, 16]]),
                            in_=V(ps, 0, [[16, 8], [1, 16]]), func=AF.Gelu_apprx_tanh),
                            reads=[pb], writes=[b_utok])
            for j in range(8):
                ps, pb = psA.get()
                psb = ps[:, :].bitcast(BF16)
                for q in range(8):
                    S.pe(lambda e, psb=psb, q=q, j=j: e.transpose(
                        out=psb[:, q * 128:(q + 1) * 128], in_=Utok[:, j, q * 128:(q + 1) * 128], identity=ident_b[:, :]),
                        reads=[b_utok, B_identb], writes=[pb])
                evac_copy(V(UTS, j, [[1024, 8], [8, 128]]),
                          bass.AP(tensor=psb.tensor, offset=0, ap=[[1024, 128], [128, 8], [1, 128]]), [pb], [b_uts], "act")
            S.dma("pool", UT_d[:, :, st * 1024:(st + 1) * 1024], UTS[:, :, :], reads=[b_uts],
                  writes=[UT_b[i] for i in range(st * 8, st * 8 + 8)])

        stop("s5b")
        pt, pbuf, poff = state
        ps, pb = psA.get()
        S.pe(lambda e, ps=ps, pt=pt, poff=poff: e.transpose(out=ps[0:64, 0:128], in_=V(pt, poff, [[1, 64]]),
                                                            identity=ident_f[:, :]),
             reads=[pbuf, B_identf], writes=[pb])
        fin_o, b_fino = TB("fin_o", [64, 128], F32)
        S.dve(lambda e, ps=ps: e.tensor_copy(out=fin_o[:, :], in_=ps[0:64, 0:128]), reads=[pb], writes=[b_fino])
        S.dma("pool", sp_state[l], fin_o[:, :], reads=[b_fino])

        stop("fin")
        AR.release(m_work)
        S.barrier()
        S0IN, b_s0in = TB("S0IN", [TS, 4096], F32)
        S0, b_s0 = TB("S0", [128, 64, TS], F32)
        S0b, b_s0b = TB("S0b", [128, 64, TS], BF16)
        for half in range(2):
            S.dma("sp", S0IN[:, :], (st_re_d if half == 0 else st_im_d)[l], writes=[b_s0in])
            ps, pb = psA.get()
            for ee in range(32):
                en = half * 32 + ee
                S.pe(lambda e, ps=ps, ee=ee, en=en: e.transpose(
                    out=ps[:, ee * TS:(ee + 1) * TS], in_=V(S0IN, ee * 128, [[1, 128]], npart=TS),
                    identity=ident_f[0:TS, 0:TS]), reads=[b_s0in, B_identf], writes=[pb])
            S.dve(lambda e, ps=ps, half=half: e.tensor_copy(out=V(S0, half * 512, [[1, 512]]), in_=ps[:, :]),
                  reads=[pb], writes=[b_s0])
            S.act(lambda e, ps=ps, half=half: e.copy(out=V(S0b, half * 512, [[1, 512]]), in_=ps[:, :]),
                  reads=[pb], writes=[b_s0b])
        stop("s0")
        Gs, b_gs = TB("Gs", [TS, D], BF16)
        for h in range(2):
            ps, pb = psA.get()
            for i in range(32):
                g = 2 * i + h
                for ri in range(2):
                    S.pe(lambda e, ps=ps, i=i, ri=ri, h=h: e.matmul(
                        out=ps[0:TS, i * 16:(i + 1) * 16],
                        lhsT=V(S0b, (ri * 32 + i) * TS, [[1, TS]], p0=h * 64, npart=64),
                        rhs=V(WO, (i * 2 + ri) * 128, [[1, 16]], p0=h * 64, npart=64),
                        start=(ri == 0), stop=False), reads=[b_s0b, b_fin], writes=[pb])
                S.pe(lambda e, ps=ps, i=i, g=g: e.matmul(
                    out=ps[0:TS, i * 16:(i + 1) * 16], lhsT=V(Ugs, g * TS, [[1, TS]]),
                    rhs=V(WK, g * 128, [[1, 16]]), start=False, stop=True), reads=[b_ugs, b_fin], writes=[pb])
            S.act(lambda e, ps=ps, h=h: e.activation(
                out=V(Gs, 16 * h, [[32, 32], [1, 16]], npart=TS), in_=V(ps, 0, [[16, 32], [1, 16]], npart=TS),
                func=AF.Gelu_apprx_tanh), reads=[pb], writes=[b_gs])
        ps, pb = psA.get()
        psb = ps[:, :].bitcast(BF16)
        for q in range(8):
            S.pe(lambda e, psb=psb, q=q: e.transpose(out=psb[:, q * TS:(q + 1) * TS], in_=Gs[:, q * 128:(q + 1) * 128],
                                                     identity=ident_b[0:TS, 0:TS]), reads=[b_gs, B_identb], writes=[pb])
        gts, b_gts = TB("gts", [128, 8, TS], BF16)
        S.dve(lambda e, psb=psb: e.tensor_copy(out=V(gts, 0, [[1, 8 * TS]]), in_=psb[:, 0:8 * TS]), reads=[pb], writes=[b_gts])
        S.dma("pool", UT_d[:, :, TP:TP + TS], gts[:, :, :], reads=[b_gts], writes=[UT_b[16]])
        stop("ys")
        SN, b_sn = TB("SN", [128, 64, TS], F32)
        TA, b_ta = TB("TA", [128, 64, TS], F32)
        TBt, b_tb = TB("TBt", [128, 64, TS], F32)
        S.dve(lambda e: e.tensor_tensor(out=TA[:, :, :], in0=S0[:, :, :], in1=V(A1x, 0, [[1, 64], [0, TS]]), op=ALU.mult),
              reads=[b_s0, b_fin], writes=[b_ta])
        S.dve(lambda e: e.tensor_tensor(out=V(TBt, 0, [[32 * TS, 2], [TS, 32], [1, TS]]),
                                        in0=V(S0, 32 * TS, [[-32 * TS, 2], [TS, 32], [1, TS]]),
                                        in1=V(Bm1, 0, [[32, 2], [1, 32], [0, TS]]), op=ALU.mult),
              reads=[b_s0, b_fin], writes=[b_tb])
        S.dve(lambda e: e.tensor_tensor(out=TA[:, :, :], in0=TA[:, :, :], in1=TBt[:, :, :], op=ALU.add),
              reads=[b_ta, b_tb], writes=[b_ta])
        for half in range(2):
            ps, pb = psA.get()
            for ee in range(32):
                en = half * 32 + ee
                ri, i = divmod(en, 32)
                for h in range(2):
                    S.pe(lambda e, ps=ps, ee=ee, ri=ri, i=i, h=h: e.matmul(
                        out=ps[h * 64:(h + 1) * 64, ee * TS:(ee + 1) * TS],
                        lhsT=V(WVT, (i * 2 + ri) * 128 + h * 64, [[1, 64]], p0=64, npart=64),
                        rhs=V(Ugs, (2 * i + h) * TS, [[1, TS]], p0=64, npart=64), start=True, stop=True),
                        reads=[b_fin, b_ugs], writes=[pb])
            S.dve(lambda e, ps=ps, half=half: e.tensor_tensor(
                out=V(SN, half * 512, [[1, 512]]), in0=ps[:, :], in1=V(TA, half * 512, [[1, 512]]), op=ALU.add),
                reads=[pb, b_ta], writes=[b_sn])
        OUTS, b_outs = TB("OUTS", [TS, 32, 128], F32)
        for ri in range(2):
            for bk in range(8):
                ps, pb = psA.get()
                for ee in range(4):
                    i = bk * 4 + ee
                    S.pe(lambda e, ps=ps, ee=ee, i=i, ri=ri: e.transpose(
                        out=ps[0:TS, ee * 128:(ee + 1) * 128], in_=V(SN, (ri * 32 + i) * TS, [[1, TS]]),
                        identity=ident_f[:, :]), reads=[b_sn, B_identf], writes=[pb])
                evac_copy(V(OUTS, bk * 512, [[1, 512]], npart=TS), ps[0:TS, :], [pb], [b_outs])
            S.dma("pool", ss_state[l, ri], V(OUTS, 0, [[1, 4096]], npart=TS), reads=[b_outs])

        AR.release(m_layer)
        S.barrier()

    def glu(l, after_w=None):
        m = AR.mark()
        WG = AR.alloc("WG", [128, 8, 2048], BF16)
        b_wg = [Buf() for _ in range(4)]
        for j in range(4):
            S.dma("pool", WG[:, :, j * 512:(j + 1) * 512],
                  w_glu_d[l, :, j * 512:(j + 1) * 512].rearrange("(q p) f -> p q f", p=128), writes=[b_wg[j]])
        lazy = after_w() if after_w is not None else []
        sg_r = Rot(AR, "sg", [128, TW], F32, 2)
        gt_r = Rot(AR, "gt", [128, TW], F32, 2)
        for tile in TILES:
            c0, n, bl = tile
            xt, xb = load_xt(tile)
            ut, utb = ut_r.get()
            S.dma("sp", ut[:, :, 0:n], UT_d[:, :, c0:c0 + n], reads=[UT_b[i] for i in bl], writes=[utb])
            for m8 in range(8):
                ps, pb = psA.get()
                for k in range(2):
                    col = k * 1024 + m8 * 128
                    for q in range(8):
                        S.pe(lambda e, ps=ps, k=k, col=col, q=q, ut=ut, n=n: e.matmul(
                            out=ps[:, k * 256:k * 256 + n], lhsT=WG[:, q, col:col + 128], rhs=ut[:, q, 0:n],
                            start=(q == 0), stop=(q == 7)), reads=[utb, b_wg[col // 512]], writes=[pb])
                sg, sgb = sg_r.get()
                r1 = 72 + 16 * l + m8
                S.act(lambda e, ps=ps, sg=sg, r1=r1, n=n: e.activation(
                    out=sg[:, 0:n], in_=ps[:, 256:256 + n], func=AF.Sigmoid, bias=VEC[:, r1 + 8:r1 + 9]),
                    reads=[pb, B_vec], writes=[sgb])
                gt, gtb = gt_r.get()
                S.dve(lambda e, ps=ps, sg=sg, gt=gt, r1=r1, n=n: e.scalar_tensor_tensor(
                    out=gt[:, 0:n], in0=ps[:, 0:n], scalar=VEC[:, r1:r1 + 1], in1=sg[:, 0:n],
                    op0=ALU.add, op1=ALU.mult), reads=[pb, sgb, B_vec], writes=[gtb])
                S.pool(lambda e, xt=xt, gt=gt, m8=m8, n=n: e.tensor_tensor(
                    out=xt[:, m8, 0:n], in0=xt[:, m8, 0:n], in1=gt[:, 0:n], op=ALU.add),
                    reads=[gtb, xb], writes=[xb])
            store_xt(tile, xt, xb)
            if len(lazy) > 8:
                lazy.pop(0)()
        while len(lazy) > 8:
            lazy.pop(0)()
        phase_end(m)
        return lazy

    w_k_d = din("w_k", [D, 256])
    w_v_d = din("w_v", [D, 256])
    w_q_d = din("w_q", [2, D, D])
    w_o_d = din("w_o", [2, D, D])
    gqk_d = din("gqk", [128, 6])
    sinks_d = din("sinks", [128, 16])
    rot_d = din("rot", [128, 128])
    bones_d = din("bones", [128, 128])
    mask_d = din("masks", [3, 128, 128])
    cos_d = din("cos_t", [128, NT])
    sin_d = din("sin_t", [128, NT])
    cache_k_d = din("cache_k", [TS, 128, 256])
    cache_v_d = din("cache_v", [TS, 128, 256])
    k_last = dout("k_last", [128, 256])
    v_last = dout("v_last", [128, 256])
    ks_out = dout("ks_out", [TS, 128, 256])
    vs_out = dout("vs_out", [TS, 128, 256])
    kvx_in = nc.dram_tensor("kvx_in", [128, 384], F32)
    kvx_out = nc.dram_tensor("kvx_out", [256, 384], F32)

    KT_d = nc.dram_tensor("KT_d", [128, 4, 128 + NT], BF16).ap()
    V_d = nc.dram_tensor("V_d", [128, 18, 256], BF16).ap()
    KTs_d = nc.dram_tensor("KTs_d", [128, TS, 4, 128], BF16).ap()
    Vs_d = nc.dram_tensor("Vs_d", [128, TS, 256], BF16).ap()
    b_kvd = Buf("kv_dram")

    if True:
        MASK_F = AR.alloc("MASK_F", [128, 256], BF16)
        MASK_R = AR.alloc("MASK_R", [128, 256], BF16)
        ROT_b = AR.alloc("ROT_b", [128, 128], BF16)
        BONES_b = AR.alloc("BONES_b", [128, 128], BF16)
        GQK = AR.alloc("GQK", [128, 6], F32)
        ESINK = AR.alloc("ESINK", [128, 16], F32)
        EPSV = AR.alloc("EPSV", [128, 1], F32)
        b_c3 = Buf()
        S.dve(lambda e: e.memset(EPSV[:, :], EPS), writes=[b_c3])
        m_tmp = AR.mark()
        tmpc = AR.alloc("tmpc", [128, 256], F32)
        tmpm = AR.alloc("tmpm", [128, 3, 128], F32)
        for i3 in range(3):
            S.dma("sp", tmpm[:, i3, :], mask_d[i3], writes=[b_c3])
        S.dve(lambda e: e.tensor_copy(out=MASK_R[:, :], in_=V(tmpm, 0, [[1, 256]])), reads=[b_c3], writes=[b_c3])
        S.dve(lambda e: e.tensor_copy(out=MASK_F[:, 0:128], in_=tmpm[:, 2, :]), reads=[b_c3], writes=[b_c3])
        S.dve(lambda e: e.tensor_copy(out=MASK_F[:, 128:256], in_=tmpm[:, 1, :]), reads=[b_c3], writes=[b_c3])
        S.dma("sp", GQK[:, :], gqk_d, writes=[b_c3])
        S.dma("sp", ESINK[:, :], sinks_d, writes=[b_c3])
        S.act(lambda e: e.activation(out=ESINK[:, :], in_=ESINK[:, :], func=AF.Exp), reads=[b_c3], writes=[b_c3])
        S.dma("sp", tmpc[:, 0:128], rot_d, writes=[b_c3])
        S.dma("sp", tmpc[:, 128:256], bones_d, writes=[b_c3])
        S.dve(lambda e: e.tensor_copy(out=ROT_b[:, :], in_=tmpc[:, 0:128]), reads=[b_c3], writes=[b_c3])
        S.dve(lambda e: e.tensor_copy(out=BONES_b[:, :], in_=tmpc[:, 128:256]), reads=[b_c3], writes=[b_c3])
        AR.release(m_tmp)
        S.barrier()

    class QK:
        def __init__(self):
            self.cs_r = Rot(AR, "cs", [128, 2, TW], F32, 2)
            self.sqb_r = Rot(AR, "sqb", [128, TW], BF16, 2)
            self.kb_r = Rot(AR, "kbb", [128, TW], BF16, 2)
            self.rq_r = Rot(AR, "rq", [128, TW], F32, 2)
            self.ta_r = Rot(AR, "ta", [128, TW], F32, 2)
            self.tb_r = Rot(AR, "tbb", [128, TW], F32, 2)

    if True:
        def load_cs(H, tile):
            c0, n, bl = tile
            cs, csb = H.cs_r.get()
            S.dma("sp", cs[:, 0, 0:n], cos_d[:, c0:c0 + n], writes=[csb])
            S.dma("sp", cs[:, 1, 0:n], sin_d[:, c0:c0 + n], writes=[csb])
            return cs, csb

        def qk_post(H, ps, pb, n, gcol, cs, csb, out_ap, out_bs):
            sq, sqb = H.sqb_r.get()
            kb, kbb = H.kb_r.get()
            S.act(lambda e: e.activation(out=sq[:, 0:n], in_=ps[:, 0:n], func=AF.Square), reads=[pb], writes=[sqb])
            S.dve(lambda e: e.tensor_copy(out=kb[:, 0:n], in_=ps[:, 0:n]), reads=[pb], writes=[kbb])
            ps2, pb2 = psA.get()
            S.pe(lambda e: e.matmul(out=ps2[:, 0:n], lhsT=BONES_b[:, :], rhs=sq[:, 0:n], start=True, stop=True),
                 reads=[sqb, b_c3], writes=[pb2])
            S.pe(lambda e: e.matmul(out=ps2[:, 256:256 + n], lhsT=ROT_b[:, :], rhs=kb[:, 0:n], start=True, stop=True),
                 reads=[kbb, b_c3], writes=[pb2])
            rq, rqb = H.rq_r.get()
            S.act(lambda e: e.activation(out=rq[:, 0:n], in_=ps2[:, 0:n], func=AF.Ln, bias=EPSV[:, 0:1], scale=1.0 / 64),
                  reads=[pb2, b_c3], writes=[rqb])
            S.act(lambda e: e.activation(out=rq[:, 0:n], in_=rq[:, 0:n], func=AF.Exp, scale=-0.5),
                  reads=[rqb], writes=[rqb])
            ta, tab = H.ta_r.get()
            tb, tbb = H.tb_r.get()
            S.dve(lambda e: e.scalar_tensor_tensor(out=ta[:, 0:n], in0=ps[:, 0:n], scalar=GQK[:, gcol:gcol + 1],
                                                   in1=cs[:, 0, 0:n], op0=ALU.mult, op1=ALU.mult),
                  reads=[pb, csb, b_c3], writes=[tab])
            S.dve(lambda e: e.scalar_tensor_tensor(out=tb[:, 0:n], in0=ps2[:, 256:256 + n],
                                                   scalar=GQK[:, gcol + 1:gcol + 2], in1=cs[:, 1, 0:n],
                                                   op0=ALU.mult, op1=ALU.mult),
                  reads=[pb2, csb, b_c3], writes=[tbb])
            S.pool(lambda e: e.tensor_tensor(out=ta[:, 0:n], in0=ta[:, 0:n], in1=tb[:, 0:n], op=ALU.add),
                   reads=[tab, tbb], writes=[tab])
            S.dve(lambda e: e.tensor_tensor(out=out_ap, in0=ta[:, 0:n], in1=rq[:, 0:n], op=ALU.mult),
                  reads=[tab, rqb], writes=list(out_bs))

    def kv_phase(hook=None):
        m_kv = AR.mark()
        H = QK()
        KT_all = AR.alloc("KT_all", [128, 4, 128 + NT], BF16)
        V_all = AR.alloc("V_all", [128, 18, 256], BF16)
        KTs = AR.alloc("KTs", [128, TS, 4, 128], BF16)
        Vs = AR.alloc("Vs", [128, TS, 256], BF16)
        b_kt = [Buf() for _ in range(18)]
        b_vv = [Buf() for _ in range(18)]
        b_kts, b_vs2 = Buf(), Buf()
        b_kso, b_vso = Buf(), Buf()
        S.dma("sp", ks_out[:, 0:127, :], cache_k_d[:, 1:128, :], writes=[b_kso])
        S.dma("sp", vs_out[:, 0:127, :], cache_v_d[:, 1:128, :], writes=[b_vso])
        S.pool(lambda e: e.memset(V_all[:, 17, :], 0.0), writes=[b_vv[17]])
        WKd = AR.alloc("WKd", [128, 8, 4, 2, 64], BF16)
        WVt = AR.alloc("WVt", [128, 8, 256], BF16)
        b_wk, b_wv = Buf(), Buf()
        WKs = AR.alloc("WKs", [128, 8, 256], BF16)
        b_wks = Buf()
        S.dma("pool", WKs[:, :, :], w_k_d.rearrange("(q p) f -> p q f", p=128), writes=[b_wks])
        for q in range(8):
            S.pool(lambda e, q=q: e.tensor_copy(out=V(WKd, q * 512, [[128, 4], [64, 2], [1, 64]]),
                                                in_=V(WKs, q * 256, [[64, 4], [0, 2], [1, 64]])),
                   reads=[b_wks], writes=[b_wk])
        S.dma("pool", WVt[:, :, :], w_v_d.rearrange("(q p) f -> p q f", p=128), writes=[b_wv])
        if hook is not None:
            hook()
        for ti, tile in enumerate(TILES):
            c0, n, bl = tile
            xt, xb = load_xt(tile)
            ut, utb = rmsnorm(xt, xb, n, 64)
            cs, csb = load_cs(H, tile)
            kbufs = [b_kt[1 + i] for i in bl]
            for kvh in range(4):
                ps, pb = psA.get()
                for q in range(8):
                    S.pe(lambda e, ps=ps, q=q, kvh=kvh, ut=ut, n=n: e.matmul(
                        out=ps[:, 0:n], lhsT=V(WKd, (q * 4 + kvh) * 128, [[1, 128]]), rhs=ut[:, q, 0:n],
                        start=(q == 0), stop=(q == 7)), reads=[utb, b_wk], writes=[pb])
                qk_post(H, ps, pb, n, 4, cs, csb, KT_all[:, kvh, 128 + c0:128 + c0 + n], kbufs)
            for sb in range((n + 127) // 128):
                nn = min(128, n - sb * 128)
                blk = bl[sb]
                ps, pb = psA.get()
                for q in range(8):
                    S.pe(lambda e, ps=ps, q=q, ut=ut, sb=sb, nn=nn: e.matmul(
                        out=ps[0:nn, 0:256], lhsT=ut[:, q, sb * 128:sb * 128 + nn], rhs=WVt[:, q, :],
                        start=(q == 0), stop=(q == 7)), reads=[utb, b_wv], writes=[pb])
                evac_copy(V_all[0:nn, 1 + blk, :], ps[0:nn, 0:256], [pb], [b_vv[1 + blk]])

        kl, b_kl = TB("kl", [128, 4, 64], F32)
        vl, b_vl = TB("vl", [128, 256], F32)
        ps, pb = psA.get()
        psb = ps[:, :].bitcast(BF16)
        for kvh in range(4):
            S.pe(lambda e, psb=psb, kvh=kvh: e.transpose(
                out=psb[:, kvh * 128:(kvh + 1) * 128], in_=KT_all[:, kvh, 128 + TP - 128:128 + TP], identity=ident_b[:, :]),
                reads=[b_kt[16], B_identb], writes=[pb])
        S.dve(lambda e, psb=psb: e.tensor_copy(out=kl[:, :, :], in_=bass.AP(tensor=psb.tensor, offset=0,
                                                                           ap=[[1024, 128], [128, 4], [1, 64]])),
              reads=[pb], writes=[b_kl])
        S.dma("pool", k_last, V(kl, 0, [[1, 256]]), reads=[b_kl])
        S.act(lambda e: e.copy(out=vl[:, :], in_=V_all[:, 16, :]), reads=[b_vv[16]], writes=[b_vl])
        S.dma("pool", v_last, vl[:, :], reads=[b_vl])
        kn, b_kn = TB("kn", [TS, 4, 64], F32)
        vn, b_vn = TB("vn", [TS, 256], F32)
        ps, pb = psA.get()
        psb = ps[:, :].bitcast(BF16)
        for kvh in range(4):
            S.pe(lambda e, psb=psb, kvh=kvh: e.transpose(
                out=psb[0:TS, kvh * 128:(kvh + 1) * 128], in_=KT_all[:, kvh, 128 + TP:128 + TP + TS], identity=ident_b[:, :]),
                reads=[b_kt[17], B_identb], writes=[pb])
        S.dve(lambda e, psb=psb: e.tensor_copy(out=kn[:, :, :], in_=bass.AP(tensor=psb.tensor, offset=0,
                                                                           ap=[[1024, TS], [128, 4], [1, 64]])),
              reads=[pb], writes=[b_kn])
        S.act(lambda e: e.copy(out=vn[:, :], in_=V_all[0:TS, 17, :]), reads=[b_vv[17]], writes=[b_vn])
        S.dma("pool", ks_out[:, 127, :], V(kn, 0, [[1, 256]], npart=TS), reads=[b_kn], writes=[b_kso])
        S.dma("pool", vs_out[:, 127, :], vn[:, :], reads=[b_vn], writes=[b_vso])
        b_xin, b_xout = Buf(), Buf()
        xin_b = kvx_in.ap().bitcast(BF16)
        for kvh in range(4):
            S.dma("pool", xin_b[:, kvh * 128:(kvh + 1) * 128], KT_all[:, kvh, 128 + TP - 128:128 + TP],
                  reads=[b_kt[16]], writes=[b_xin])
        S.dma("pool", xin_b[:, 512:768], V_all[:, 16, :], reads=[b_vv[16]], writes=[b_xin])
        S.cc(lambda e: e.collective_compute("AllGather", ALU.bypass, replica_groups=[[0, 1], [2, 3], [4, 5], [6, 7]],
                                            ins=[kvx_in.ap().opt()], outs=[kvx_out.ap().opt()]),
             reads=[b_xin], writes=[b_xout])
        kc_r = Rot(AR, "kc", [128, 256], F32, 2)
        kc2_r = Rot(AR, "kc2", [128, 4, 2, 64], BF16, 2)
        for b in range(TS):
            kc, kcb = kc_r.get()
            S.dma("sp", kc[:, :], ks_out[b], reads=[b_kso], writes=[kcb])
            kc2, kc2b = kc2_r.get()
            S.dve(lambda e, kc=kc, kc2=kc2: e.tensor_copy(out=kc2[:, :, :, :], in_=V(kc, 0, [[64, 4], [0, 2], [1, 64]])),
                  reads=[kcb], writes=[kc2b])
            ps, pb = psA.get()
            psb = ps[:, :].bitcast(BF16)
            for kvh in range(4):
                S.pe(lambda e, psb=psb, kvh=kvh, kc2=kc2: e.transpose(
                    out=psb[:, kvh * 128:(kvh + 1) * 128], in_=V(kc2, kvh * 128, [[1, 128]]), identity=ident_b[:, :]),
                    reads=[kc2b, B_identb], writes=[pb])
            evac_copy(V(KTs, b * 512, [[1, 512]]), psb[:, 0:512], [pb], [b_kts])
            S.dma("pool", Vs[:, b, :], vs_out[b], reads=[b_vso], writes=[b_vs2])
        xout_b = kvx_out.ap().bitcast(BF16)
        for kvh in range(4):
            S.dma("sp", KT_all[:, kvh, 0:128], xout_b[0:128, kvh * 128:(kvh + 1) * 128], reads=[b_xout], writes=[b_kt[0]])
        S.dma("sp", V_all[:, 0, :], xout_b[0:128, 512:768], reads=[b_xout], writes=[b_vv[0]])
        S.dma("pool", KT_d, KT_all[:, :, :], reads=b_kt, writes=[b_kvd])
        S.dma("pool", V_d, V_all[:, :, :], reads=b_vv, writes=[b_kvd])
        S.dma("pool", KTs_d, KTs[:, :, :, :], reads=[b_kts], writes=[b_kvd])
        S.dma("pool", Vs_d, Vs[:, :, :], reads=[b_vs2], writes=[b_kvd])
        phase_end(m_kv)

    def attn_w_alloc(which):
        w = {}
        for nm in which:
            w[nm] = (AR.alloc("W%st" % nm, [128, 8, D], BF16), [Buf(), Buf()])
        return w

    def attn_w_load(j, w):
        for nm, srcd in (("Q", w_q_d), ("O", w_o_d)):
            if nm in w and not w.get(nm + "_loaded"):
                t, bs = w[nm]
                for hf in range(2):
                    S.dma("pool", t[:, :, hf * 512:(hf + 1) * 512],
                          srcd[j, :, hf * 512:(hf + 1) * 512].rearrange("(q p) f -> p q f", p=128), writes=[bs[hf]])
                w[nm + "_loaded"] = True

    def attn_layer(j, w=None, hook=None):
        if True:
            layer = 2 + j
            m_at = AR.mark()
            w = dict(w) if w is not None else {}
            H = QK()
            KT_all = AR.alloc("KT_all", [128, 4, 128 + NT], BF16)
            V_all = AR.alloc("V_all", [128, 18, 256], BF16)
            KTs = AR.alloc("KTs", [128, TS, 4, 128], BF16)
            Vs = AR.alloc("Vs", [128, TS, 256], BF16)
            b_kvs = Buf()
            b_kt = [b_kvs] * 18
            b_vv = [b_kvs] * 18
            b_kts = b_vs2 = b_kvs
            S.dma("sp", KT_all[:, :, :], KT_d, reads=[b_kvd], writes=[b_kvs])
            S.dma("sp", V_all[:, :, :], V_d, reads=[b_kvd], writes=[b_kvs])
            S.dma("sp", KTs[:, :, :, :], KTs_d, reads=[b_kvd], writes=[b_kvs])
            S.dma("sp", Vs[:, :, :], Vs_d, reads=[b_kvd], writes=[b_kvs])
            missing = [nm for nm in ("Q", "O") if nm not in w]
            w.update(attn_w_alloc(missing))
            attn_w_load(j, w)
            WQt, b_wq = w["Q"]
            WOt, b_wo = w["O"]
            if hook is not None:
                hook()
            qt_r = Rot(AR, "qt", [128, 8, TW], BF16, 2)
            ot_r = Rot(AR, "ot", [128, 8, TW], BF16, 2)
            pt_r = Rot(AR, "pt", [128, 256], BF16, 12)
            ln_r = Rot(AR, "lnr", [128, 128], F32, 4)
            for ti, tile in enumerate(TILES):
                c0, n, bl = tile
                xt, xb = load_xt(tile)
                ut, utb = rmsnorm(xt, xb, n, 8 * layer)
                cs, csb = load_cs(H, tile)
                qt, qtb = qt_r.get()
                pss = {}

                def projq(c):
                    ps, pb = psA.get()
                    for q in range(8):
                        S.pe(lambda e, ps=ps, q=q, c=c, ut=ut, n=n: e.matmul(
                            out=ps[:, 0:n], lhsT=WQt[:, q, c * 128:(c + 1) * 128], rhs=ut[:, q, 0:n],
                            start=(q == 0), stop=(q == 7)), reads=[utb, b_wq[c // 4]], writes=[pb])
                    pss[c] = (ps, pb)

                projq(0)
                projq(1)
                for c in range(8):
                    ps, pb = pss.pop(c)
                    qk_post(H, ps, pb, n, 2 * j, cs, csb, qt[:, c, 0:n], [qtb])
                    if c + 2 < 8:
                        projq(c + 2)
                ot, otb = ot_r.get()
                if ti < 8:
                    def stage_a(sb, c):
                        qb = 2 * ti + sb
                        MASK = MASK_F if qb == 0 else MASK_R
                        kvh = c // 2
                        pts = []
                        for hh in range(2):
                            psS, pbS = psA.get()
                            for kt in range(2):
                                S.pe(lambda e, psS=psS, hh=hh, kt=kt, kvh=kvh, qb=qb, c=c, qt=qt, sb=sb: e.matmul(
                                    out=psS[:, kt * 128:(kt + 1) * 128],
                                    lhsT=KT_all[hh * 64:(hh + 1) * 64, kvh, (qb + kt) * 128:(qb + kt + 1) * 128],
                                    rhs=qt[hh * 64:(hh + 1) * 64, c, sb * 128:(sb + 1) * 128], start=True, stop=True),
                                    reads=[b_kt[qb + kt], qtb], writes=[pbS])
                            pt, ptb = pt_r.get()
                            S.act(lambda e, psS=psS, pt=pt: e.activation(out=pt[:, :], in_=psS[:, 0:256], func=AF.Exp,
                                                                         scale=0.125), reads=[pbS], writes=[ptb])
                            (S.pool if hh == 0 else S.dve)(
                                lambda e, pt=pt, MASK=MASK: e.tensor_tensor(out=pt[:, :], in0=pt[:, :], in1=MASK[:, :],
                                                                            op=ALU.mult), reads=[ptb, b_c3], writes=[ptb])
                            pts.append((pt, ptb))
                        return pts

                    def stage_b(sb, c, pts):
                        qb = 2 * ti + sb
                        kvh = c // 2
                        psO, pbO = psA.get()
                        for part in range(2):
                            for hh in range(2):
                                pt, ptb = pts[hh]
                                for kt in range(2):
                                    if part == 0:
                                        lhs = V_all[:, qb + kt, kvh * 64:(kvh + 1) * 64]
                                        rd = [b_vv[qb + kt], ptb]
                                    else:
                                        lhs = ones_b[:, 0:64]
                                        rd = [B_ones, ptb]
                                    S.pe(lambda e, psO=psO, hh=hh, kt=kt, lhs=lhs, pt=pt, part=part: e.matmul(
                                        out=psO[hh * 64:(hh + 1) * 64, part * 128:(part + 1) * 128], lhsT=lhs,
                                        rhs=pt[:, kt * 128:(kt + 1) * 128], start=(kt == 0), stop=(kt == 1)),
                                        reads=rd, writes=[pbO])
                        ln, lnb = ln_r.get()
                        S.act(lambda e, psO=psO, ln=ln, c=c, j=j: e.activation(
                            out=ln[:, :], in_=psO[:, 128:256], func=AF.Ln, bias=ESINK[:, j * 8 + c:j * 8 + c + 1]),
                            reads=[pbO, b_c3], writes=[lnb])
                        S.act(lambda e, ln=ln: e.activation(out=ln[:, :], in_=ln[:, :], func=AF.Exp, scale=-1.0),
                              reads=[lnb], writes=[lnb])
                        S.dve(lambda e, psO=psO, ln=ln, ot=ot, c=c, sb=sb: e.tensor_tensor(
                            out=ot[:, c, sb * 128:(sb + 1) * 128], in0=psO[:, 0:128], in1=ln[:, :], op=ALU.mult),
                            reads=[pbO, lnb], writes=[otb])

                    its = [(sb, c) for sb in range(2) for c in range(8)]
                    pend = []
                    SKEW = 2
                    for it in range(len(its) + SKEW):
                        if it < len(its):
                            pend.append(stage_a(*its[it]))
                        if it >= SKEW:
                            stage_b(*its[it - SKEW], pend[it - SKEW])
                else:
                    pts = []
                    for hh in range(2):
                        psS, pbS = psA.get()
                        for b in range(TS):
                            for kvh in range(4):
                                S.pe(lambda e, psS=psS, hh=hh, b=b, kvh=kvh, qt=qt: e.matmul(
                                    out=psS[:, (b * 4 + kvh) * 2:(b * 4 + kvh) * 2 + 2],
                                    lhsT=V(KTs, (b * 4 + kvh) * 128, [[1, 128]], p0=hh * 64, npart=64),
                                    rhs=V(qt, (2 * kvh) * TW + b, [[TW, 2]], p0=hh * 64, npart=64), start=True, stop=True),
                                    reads=[b_kts, qtb], writes=[pbS])
                        pt, ptb = pt_r.get()
                        S.act(lambda e, psS=psS, pt=pt: e.activation(out=pt[:, 0:128], in_=psS[:, 0:128], func=AF.Exp,
                                                                     scale=0.125), reads=[pbS], writes=[ptb])
                        pts.append((pt, ptb))
                    psO, pbO = psA.get()
                    for part in range(2):
                        for hh in range(2):
                            pt, ptb = pts[hh]
                            for b in range(TS):
                                for kvh in range(4):
                                    col = (b * 4 + kvh) * 2
                                    if part == 0:
                                        lhs = Vs[:, b, kvh * 64:(kvh + 1) * 64]
                                        rd = [b_vs2, ptb]
                                    else:
                                        lhs = ones_b[:, 0:64]
                                        rd = [B_ones, ptb]
                                    S.pe(lambda e, psO=psO, hh=hh, lhs=lhs, pt=pt, part=part, col=col: e.matmul(
                                        out=psO[hh * 64:(hh + 1) * 64, part * 128 + col:part * 128 + col + 2], lhsT=lhs,
                                        rhs=pt[:, col:col + 2], start=True, stop=True), reads=rd, writes=[pbO])
                    ln, lnb = ln_r.get()
                    S.dve(lambda e, psO=psO, ln=ln, j=j: e.tensor_tensor(
                        out=V(ln, 0, [[8, TS], [1, 8]]), in0=V(psO, 128, [[8, TS], [1, 8]]),
                        in1=V(ESINK, j * 8, [[0, TS], [1, 8]]), op=ALU.add),
                        reads=[pbO, b_c3], writes=[lnb])
                    S.act(lambda e, ln=ln: e.activation(out=ln[:, :], in_=ln[:, :], func=AF.Ln), reads=[lnb], writes=[lnb])
                    S.act(lambda e, ln=ln: e.activation(out=ln[:, :], in_=ln[:, :], func=AF.Exp, scale=-1.0),
                          reads=[lnb], writes=[lnb])
                    S.dve(lambda e, psO=psO, ln=ln, ot=ot: e.tensor_tensor(
                        out=V(ot, 0, [[1, TS], [TW, 8]]), in0=V(psO, 0, [[8, TS], [1, 8]]),
                        in1=V(ln, 0, [[8, TS], [1, 8]]), op=ALU.mult), reads=[pbO, lnb], writes=[otb])
                for d2 in range(4):
                    ps, pb = psA.get()
                    for k in range(2):
                        dm = d2 * 2 + k
                        for c in range(8):
                            S.pe(lambda e, ps=ps, k=k, dm=dm, c=c, ot=ot, n=n: e.matmul(
                                out=ps[:, k * 256:k * 256 + n], lhsT=WOt[:, c, dm * 128:(dm + 1) * 128],
                                rhs=ot[:, c, 0:n], start=(c == 0), stop=(c == 7)),
                                reads=[otb, b_wo[dm // 4]], writes=[pb])
                    psv = V(ps, 0, [[256, 2], [1, n]])
                    S.dve(lambda e, xt=xt, d2=d2, psv=psv, n=n: e.tensor_tensor(
                        out=xt[:, 2 * d2:2 * d2 + 2, 0:n], in0=psv, in1=xt[:, 2 * d2:2 * d2 + 2, 0:n], op=ALU.add),
                        reads=[pb, xb], writes=[xb])
                store_xt(tile, xt, xb)
            phase_end(m_at)

    def mlp_alloc(win0=None):
        if win0 is None:
            win0 = (AR.alloc("win0", [128, 8, 2048], BF16), [Buf() for _ in range(4)])
        win_t = [win0[0], AR.alloc("win1", [128, 8, 2048], BF16)]
        wout_t = [AR.alloc("wout%d" % i, [128, 16, 1024], BF16) for i in range(2)]
        win_b = [win0[1], [Buf() for _ in range(4)]]
        wout_b = [[Buf() for _ in range(4)] for _ in range(2)]
        return win_t, wout_t, win_b, wout_b

    def win0_load(l, win0):
        t, bs = win0
        for j in range(4):
            S.dma("pool", t[:, :, j * 512:(j + 1) * 512],
                  w_mlp_in[l, :, j * 512:(j + 1) * 512].rearrange("(q p) f -> p q f", p=128), writes=[bs[j]])

    def mlp_weights(l, pre=None, lazy=None, skip_win0=False):
        win_t, wout_t, win_b, wout_b = pre if pre is not None else mlp_alloc()

        class _Q:
            def dma(self, *a, **k):
                if lazy is None:
                    S.dma(*a, **k)
                else:
                    lazy.append(lambda a=a, k=k: S.dma(*a, **k))
        Sx = _Q()
        for hf in range(2):
            for j in range(4):
                if hf == 0 and skip_win0:
                    continue
                Sx.dma("pool", win_t[hf][:, :, j * 512:(j + 1) * 512],
                      w_mlp_in[l, :, hf * 2048 + j * 512: hf * 2048 + (j + 1) * 512].rearrange("(q p) f -> p q f", p=128),
                      writes=[win_b[hf][j]])
            for j in range(4):
                Sx.dma("pool", wout_t[hf][:, j * 4:(j + 1) * 4, :],
                      w_mlp_out[l, hf * 2048 + j * 512: hf * 2048 + (j + 1) * 512, :].rearrange("(f p) d -> p f d", p=128),
                      writes=[wout_b[hf][j]])
        return win_t, wout_t, win_b, wout_b

    def mlp(l, pre=None, rest=(), win0=None, hook=None):
        m_mlp = AR.mark()
        if pre is None and win0 is not None:
            pre = mlp_weights(l, mlp_alloc(win0), None, True)
        win_t, wout_t, win_b, wout_b = pre if pre is not None else mlp_weights(l)
        if hook is not None:
            hook()
        for fn in rest:
            fn()
        hT_r = Rot(AR, "hT", [128, 16, TW], BF16, 2)
        rl_r = Rot(AR, "rl", [128, 512], F32, 2)
        if l == 3 and FUSE_IO:
            yo_r = Rot(AR, "yo", [128, D], F32, 2)
        for hf in range(2):
            for tile in TILES:
                c0, n, bl = tile
                xt, xb = load_xt(tile)
                if hf == 0:
                    ut, utb = rmsnorm(xt, xb, n, 32 + 8 * l)
                    S.dma("pool", UT_d[:, :, c0:c0 + n], ut[:, :, 0:n], reads=[utb], writes=[UT_b[i] for i in bl])
                else:
                    ut, utb = ut_r.get()
                    S.dma("sp", ut[:, :, 0:n], UT_d[:, :, c0:c0 + n], reads=[UT_b[i] for i in bl], writes=[utb])
                hT, hb = hT_r.get()
                for f2 in range(8):
                    ps, pb = psA.get()
                    for k in range(2):
                        fc = f2 * 2 + k
                        for q in range(8):
                            S.pe(lambda e, ps=ps, k=k, fc=fc, q=q, hf=hf, ut=ut, n=n: e.matmul(
                                out=ps[:, k * 256:k * 256 + n], lhsT=win_t[hf][:, q, fc * 128:(fc + 1) * 128],
                                rhs=ut[:, q, 0:n], start=(q == 0), stop=(q == 7)),
                                reads=[utb, win_b[hf][fc // 4]], writes=[pb])
                    rl, rlb = rl_r.get()
                    psv = V(ps, 0, [[256, 2], [1, n]])
                    rlv = V(rl, 0, [[256, 2], [1, n]])
                    S.act(lambda e, rlv=rlv, psv=psv: e.activation(out=rlv, in_=psv, func=AF.Relu), reads=[pb], writes=[rlb])
                    S.dve(lambda e, hT=hT, f2=f2, rlv=rlv, n=n: e.tensor_tensor(
                        out=hT[:, 2 * f2:2 * f2 + 2, 0:n], in0=rlv, in1=rlv, op=ALU.mult), reads=[rlb], writes=[hb])
                for d2 in range(4):
                    ps, pb = psA.get()
                    for k in range(2):
                        dm = d2 * 2 + k
                        for fc in range(16):
                            S.pe(lambda e, ps=ps, k=k, dm=dm, fc=fc, hf=hf, hT=hT, n=n: e.matmul(
                                out=ps[:, k * 256:k * 256 + n], lhsT=wout_t[hf][:, fc, dm * 128:(dm + 1) * 128],
                                rhs=hT[:, fc, 0:n], start=(fc == 0), stop=(fc == 15)),
                                reads=[hb, wout_b[hf][fc // 4]], writes=[pb])
                    psv = V(ps, 0, [[256, 2], [1, n]])
                    S.dve(lambda e, xt=xt, d2=d2, psv=psv, n=n: e.tensor_tensor(
                        out=xt[:, 2 * d2:2 * d2 + 2, 0:n], in0=psv, in1=xt[:, 2 * d2:2 * d2 + 2, 0:n], op=ALU.add),
                        reads=[pb, xb], writes=[xb])
                if l == 3 and hf == 1 and FUSE_IO:
                    for sb, blk in enumerate(bl):
                        nn = blk_rows(blk)
                        yt, yb = yo_r.get()
                        for half in range(2):
                            ps, pb = psA.get()
                            for qq in range(4):
                                q = half * 4 + qq
                                S.pe(lambda e, ps=ps, xt=xt, q=q, qq=qq, nn=nn, sb=sb: e.transpose(
                                    out=ps[0:nn, qq * 128:(qq + 1) * 128], in_=xt[:, q, sb * 128:sb * 128 + nn],
                                    identity=ident_f[:, :]), reads=[xb, B_identf], writes=[pb])
                            evac_copy(yt[0:nn, half * 512:(half + 1) * 512], ps[0:nn, :], [pb], [yb], "act")
                        dst = y_p[blk * 128:(blk + 1) * 128, :] if blk < 16 else y_s[:, :]
                        S.dma("pool", dst, yt[0:nn, :], reads=[yb])
                else:
                    store_xt(tile, xt, xb)
        phase_end(m_mlp)

    try:
        for l in range(4):
            if l < 2 and on("s5_%d" % l):
                s5_layer(l)
            pre = None
            rest = ()
            m_pre = AR.mark()
            if l < 2 and on("glu%d" % l):
                hook = None
                if on("mlp%d" % l):
                    pre = mlp_alloc()
                    def hook(l=l, pre=pre):
                        lz = []
                        mlp_weights(l, pre, lz)
                        return lz
                rest = glu(l, hook)
            if l >= 2 and ALL:
                if l == 2:
                    win0 = (AR.alloc("win0p", [128, 8, 2048], BF16), [Buf() for _ in range(4)])
                    m_l2 = AR.mark()
                    w0 = attn_w_alloc(["Q", "O"])
                    kv_phase(hook=lambda: attn_w_load(0, w0))
                    attn_layer(0, w0, hook=lambda: win0_load(2, win0))
                    AR.release(m_l2)
                    S.barrier()
                    w1 = attn_w_alloc(["Q"])
                    mlp(2, win0=win0, hook=lambda: attn_w_load(1, w1))
                else:
                    attn_layer(1, w1, hook=lambda: win0_load(3, win0))
                    AR.release(m_l2)
                    S.barrier()
                    mlp(3, win0=win0)
                continue
            if l == 2 and on("kv"):
                kv_phase()
            if l >= 2 and on("attn%d" % (l - 2)):
                attn_layer(l - 2)
            if on("mlp%d" % l):
                mlp(l, pre, rest)
            if pre is not None:
                phase_end(m_pre)
    except StopBuild:
        S.emit()
        return nc

    xfin = Rot(AR, "xfin", [128, 8, 128], F32, 2)
    yout = Rot(AR, "yout", [128, D], F32, 2)
    for b in range(0 if FUSE_IO else 17):
        n = blk_rows(b)
        t, tb = xfin.get()
        S.dma("sp", t[:, :, 0:n], XT_d[:, :, b * 128:b * 128 + n], reads=[XT_b[b]], writes=[tb])
        yt, yb = yout.get()
        for half in range(2):
            ps, pb = psA.get()
            for qq in range(4):
                q = half * 4 + qq
                S.pe(lambda e, ps=ps, t=t, q=q, qq=qq, n=n: e.transpose(
                    out=ps[0:n, qq * 128:(qq + 1) * 128], in_=t[:, q, 0:n],
                    identity=ident_f[:, :]), reads=[tb, B_identf], writes=[pb])
            evac_copy(yt[0:n, half * 512:(half + 1) * 512], ps[0:n, :], [pb], [yb])
        dst = y_p[b * 128:(b + 1) * 128, :] if b < 16 else y_s[:, :]
        S.dma("pool", dst, yt[0:n, :], reads=[yb])

    S.emit()
    return nc


def make_in_maps(inputs):
    f = lambda k: np.ascontiguousarray(inputs[k], dtype=np.float32)
    x_prompt = f("x_prompt")
    x_sample = f("x_sample")
    ident = np.eye(128, dtype=np.float32)
    shared = {k: f(k) for k in ("norm_mix", "norm_mlp", "norm_kv", "b_glu", "w_mlp_in", "w_mlp_out",
                                "ssm_a_re", "ssm_a_im", "ssm_log_dt", "ssm_b_re", "ssm_b_im",
                                "ssm_c_re", "ssm_c_im", "ssm_d", "w_glu", "w_k", "w_v", "w_q", "w_o")}
    ev = np.array([7, 6, 5, 4, 3, 2, 1, 0] + list(range(1, 9)) + [-k for k in range(1, 9)] + [64], np.float32)
    shared["ev"] = np.ascontiguousarray(np.broadcast_to(ev[None, :], (128, 25)))
    jj = np.arange(128) // 16
    shared["cmask"] = (jj[:, None] <= jj[None, :]).astype(np.float32)
    hd = np.arange(128) % 64
    hd_sw = (hd + 32) % 64
    qn = f("q_norm")
    kn = f("k_norm")
    shared["gqk"] = np.ascontiguousarray(np.stack([qn[0][hd], qn[0][hd_sw], qn[1][hd], qn[1][hd_sw],
                                                   kn[hd], kn[hd_sw]], axis=1))
    sk = f("attn_sinks")
    half = (np.arange(128) >= 64).astype(np.int64)
    sinks = np.zeros((128, 16), np.float32)
    for j in range(2):
        for c in range(8):
            sinks[:, j * 8 + c] = sk[j][2 * c + half]
    shared["sinks"] = sinks
    rot = np.zeros((128, 128), np.float32)
    for m in range(128):
        if m % 64 < 32:
            rot[m + 32, m] = -1.0
        else:
            rot[m - 32, m] = 1.0
    shared["rot"] = rot
    blk = np.arange(128) // 64
    shared["bones"] = (blk[:, None] == blk[None, :]).astype(np.float32)
    kj = np.arange(128)[:, None]
    qi = np.arange(128)[None, :]
    m_prev = (kj > qi).astype(np.float32)
    m_own = (kj <= qi).astype(np.float32)
    m_none = np.zeros((128, 128), np.float32)
    inv = (np.float32(10000.0) ** (-np.arange(32, dtype=np.float32) / np.float32(32))).astype(np.float32)
    fidx = (np.arange(128) % 64) % 32
    st_re = f("state_ssm_re").reshape(2, 128, 4096)
    st_im = f("state_ssm_im").reshape(2, 128, 4096)
    ck = f("cache_k").reshape(128, 128, 256)
    cv = f("cache_v").reshape(128, 128, 256)
    in_maps = []
    for c in range(NCORES):
        b, h = c // 2, c % 2
        pos = np.concatenate([np.arange(h * TP, (h + 1) * TP), np.full(TS, 8192)]).astype(np.float32)
        ang = (pos[None, :] * inv[fidx][:, None]).astype(np.float32)
        m = {
            "xp": np.ascontiguousarray(x_prompt[b, h * TP:(h + 1) * TP, :]),
            "xs": np.ascontiguousarray(x_sample[c * TS:(c + 1) * TS, 0, :]),
            "ident": ident,
            "flag": np.full((128, 1), float(h), np.float32),
            "st_re": np.ascontiguousarray(st_re[:, c * TS:(c + 1) * TS]),
            "st_im": np.ascontiguousarray(st_im[:, c * TS:(c + 1) * TS]),
            "masks": np.ascontiguousarray(np.stack([m_prev, m_own, m_prev if h == 1 else m_none])),
            "cos_t": np.cos(ang.astype(np.float64)).astype(np.float32),
            "sin_t": np.sin(ang.astype(np.float64)).astype(np.float32),
            "cache_k": np.ascontiguousarray(ck[c * TS:(c + 1) * TS]),
            "cache_v": np.ascontiguousarray(cv[c * TS:(c + 1) * TS]),
        }
        m.update(shared)
        in_maps.append(m)
    return in_maps


def kernel(_stages=("all",), **inputs):
    nc = build(_stages)
    in_maps = make_in_maps(inputs)
    res = run_bass_kernel_spmd(nc, in_maps, core_ids=list(range(NCORES)))
    r = res.results
    if any(s.startswith("dbg_") for s in _stages):
        return r
    y_prompt = np.zeros((4, 4096, D), np.float32)
    y_sample = np.zeros((128, 1, D), np.float32)
    sp_re = np.zeros((2, 4, 64, 64), np.float32)
    sp_im = np.zeros((2, 4, 64, 64), np.float32)
    ss_re = np.zeros((2, 128, 64, 64), np.float32)
    ss_im = np.zeros((2, 128, 64, 64), np.float32)
    kp = np.zeros((4, 128, 4, 64), np.float32)
    vp = np.zeros((4, 128, 4, 64), np.float32)
    ks = np.zeros((128, 128, 4, 64), np.float32)
    vs = np.zeros((128, 128, 4, 64), np.float32)
    for c in range(NCORES):
        b, h = c // 2, c % 2
        rc = r[c]
        y_prompt[b, h * TP:(h + 1) * TP] = rc["y_p"]
        y_sample[c * TS:(c + 1) * TS, 0] = rc["y_s"]
        if "sp_state" in rc:
            if h == 1:
                sp = rc["sp_state"]
                sp_re[:, b] = sp[:, 0:32].reshape(2, 64, 64)
                sp_im[:, b] = sp[:, 32:64].reshape(2, 64, 64)
            ss = rc["ss_state"]
            ss_re[:, c * TS:(c + 1) * TS] = ss[:, 0].reshape(2, TS, 64, 64)
            ss_im[:, c * TS:(c + 1) * TS] = ss[:, 1].reshape(2, TS, 64, 64)
        if "k_last" in rc:
            if h == 1:
                kp[b] = rc["k_last"].reshape(128, 4, 64)
                vp[b] = rc["v_last"].reshape(128, 4, 64)
            ks[c * TS:(c + 1) * TS] = rc["ks_out"].reshape(TS, 128, 4, 64)
            vs[c * TS:(c + 1) * TS] = rc["vs_out"].reshape(TS, 128, 4, 64)
    return y_prompt, y_sample, sp_re, sp_im, kp, vp, ss_re, ss_im, ks, vs
```

```python
import math
import numpy as np
import concourse.bass as bass
import concourse.mybir as mybir
from concourse.bass_utils import run_bass_kernel_spmd

F32 = mybir.dt.float32
BF16 = mybir.dt.bfloat16
I32 = mybir.dt.int32
AF = mybir.ActivationFunctionType
ALU = mybir.AluOpType

ENGS = ("pe", "act", "dve", "pool", "sp")

NCORES = 8
TP = 2048
TS = 16
NT = TP + TS
D = 1024
DFF = 4096
EPS = 1e-6


class Buf:
    __slots__ = ("w", "r", "name", "excl")

    def __init__(self, name="", excl=False):
        self.w = None
        self.r = {}
        self.name = name
        self.excl = excl


class Op:
    __slots__ = ("eng", "fn", "deps", "dma", "sig", "idx", "dsem", "dval")

    def __init__(self, eng, fn, dma):
        self.eng = eng
        self.fn = fn
        self.deps = []
        self.dma = dma
        self.sig = False
        self.idx = None
        self.dsem = None
        self.dval = None


class Sched:
    NDS = 24

    def __init__(self, nc):
        self.nc = nc
        self.ops = {e: [] for e in ENGS}
        self.ndma = {e: 0 for e in ENGS}
        self.bar = Buf("barrier")
        self.bar_t = None

    def barrier(self):
        if self.bar_t is None:
            self.bar_t = self.nc.alloc_sbuf_tensor_at("bar_t", [128, 8], F32, offset=SB_BASE)
        t = self.bar_t
        self.add("pool", lambda e: e.memset(t[:, :], 0.0), writes=[self.bar])

    def add(self, eng, fn, reads=(), writes=(), dma=False):
        op = Op(eng, fn, dma)
        deps = {}
        reads = list(reads) + [self.bar]

        def need(d, kind):
            if d is None:
                return
            if (not d.dma) and (not dma) and d.eng == eng:
                if kind != "raw" or eng == "pe":
                    return
            deps[id(d)] = d

        for b in reads:
            need(b.w, "raw")
            if b.excl:
                for r in b.r.values():
                    need(r, "war")
        for b in writes:
            need(b.w, "waw")
            for r in b.r.values():
                need(r, "war")
        for d in deps.values():
            d.sig = True
            op.deps.append(d)
        for b in writes:
            b.w = op
            b.r = {}
        for b in reads:
            if dma:
                b.r[("dma", id(op))] = op
            else:
                b.r[eng] = op
        if dma:
            op.sig = True
            k = self.ndma[eng]
            self.ndma[eng] += 1
            op.dsem = (eng, k % self.NDS)
            op.dval = 16 * (k // self.NDS + 1)
        self.ops[eng].append(op)
        return op

    def pe(self, fn, reads=(), writes=()):
        return self.add("pe", fn, reads, writes)

    def act(self, fn, reads=(), writes=()):
        return self.add("act", fn, reads, writes)

    def dve(self, fn, reads=(), writes=()):
        return self.add("dve", fn, reads, writes)

    def pool(self, fn, reads=(), writes=()):
        return self.add("pool", fn, reads, writes)

    def cc(self, fn, reads=(), writes=()):
        op = self.add("pool", fn, reads, writes, dma=True)
        self.ndma["pool"] -= 1
        self.ncc = getattr(self, "ncc", 0) + 1
        op.dsem = ("cc", self.ncc)
        op.dval = 1
        return op

    def dma(self, q, out, in_, reads=(), writes=(), **kw):
        return self.add(q, lambda e: e.dma_start(out=out, in_=in_, **kw), reads, writes, dma=True)

    def emit(self):
        nc = self.nc
        for e in ENGS:
            k = 0
            for op in self.ops[e]:
                if op.sig and not op.dma:
                    k += 1
                    op.idx = k
        esem = {e: nc.alloc_semaphore("es_" + e) for e in ENGS}
        dsem = {}
        for e in ENGS:
            for j in range(min(self.NDS, self.ndma[e])):
                dsem[(e, j)] = nc.alloc_semaphore("ds_%s_%d" % (e, j))
        for j in range(1, getattr(self, "ncc", 0) + 1):
            dsem[("cc", j)] = nc.alloc_semaphore("cc_%d" % j)
        handles = {"pe": "tensor", "act": "scalar", "dve": "vector", "pool": "gpsimd", "sp": "sync"}
        with nc.Block() as block:
            for e in ENGS:
                ops = self.ops[e]
                if not ops:
                    continue

                def body(h, e=e, ops=ops):
                    waited = {}

                    def wait(key, sem, val):
                        if waited.get(key, 0) >= val:
                            return
                        waited[key] = val
                        h.wait_ge(sem, val)

                    for op in ops:
                        for d in op.deps:
                            if d.dma:
                                wait(d.dsem, dsem[d.dsem], d.dval)
                            else:
                                wait(d.eng, esem[d.eng], d.idx)
                        if op.dma and op.dsem[0] == "cc":
                            op.fn(h).then_inc(dsem[op.dsem], 1)
                        elif op.dma:
                            if op.dval > 16:
                                wait(op.dsem, dsem[op.dsem], op.dval - 16)
                            op.fn(h).then_inc(dsem[op.dsem], 16)
                        else:
                            ins = op.fn(h)
                            if op.sig:
                                ins.then_inc(esem[e], 1)
                    last = {}
                    for op in ops:
                        if op.dma:
                            last[op.dsem] = op.dval
                    for k, v in last.items():
                        wait(k, dsem[k], v)

                getattr(block, handles[e])(body)


def V(t, off, dims, p0=0, npart=128):
    shp = list(t.shape)
    ps = 1
    for s in shp[1:]:
        ps *= int(s)
    return bass.AP(tensor=t, offset=p0 * ps + off, ap=[[ps, npart]] + [list(d) for d in dims])


SB_BASE = 16512
SB_TOP = 229344
_DT_SIZE = {F32: 4, BF16: 2, I32: 4}


class Arena:
    def __init__(self, nc):
        self.nc = nc
        self.p = SB_BASE + 32
        self.k = 0

    def alloc(self, name, shape, dtype):
        n = _DT_SIZE[dtype]
        for s in shape[1:]:
            n *= int(s)
        off = self.p
        self.p = (off + n + 31) // 32 * 32
        assert self.p <= SB_TOP, "SBUF arena overflow %s: %d > %d" % (name, self.p, SB_TOP)
        self.k += 1
        return self.nc.alloc_sbuf_tensor_at("%s_%d" % (name, self.k), list(shape), dtype, offset=off)

    def mark(self):
        return self.p

    def release(self, m):
        self.p = m


class Rot:
    def __init__(self, nc, name, shape, dtype, n, psum=False):
        self.items = []
        for i in range(n):
            if psum:
                t = nc.alloc_psum_tensor("%s%d" % (name, i), shape, dtype)
            else:
                t = nc.alloc("%s%d" % (name, i), shape, dtype)
            self.items.append((t, Buf("%s%d" % (name, i), excl=psum)))
        self.i = 0

    def get(self):
        it = self.items[self.i % len(self.items)]
        self.i += 1
        return it


TW = 256
TILES = [(i * TW, TW, (2 * i, 2 * i + 1)) for i in range(TP // TW)] + [(TP, TS, (16,))]


def build(stages=("all",)):
    nc = bass.Bass("TRN2", target_bir_lowering=False)
    S = Sched(nc)
    AR = Arena(nc)
    ALL = "all" in stages

    def phase_end(m):
        AR.release(m)
        S.barrier()

    class StopBuild(Exception):
        pass

    def dbg(name, sb_ap, shape, reads, dt=F32):
        if ("dbg_" + name) in stages or "dbg_all" in stages:
            o = dout("dbg_" + name, shape, dt)
            S.dma("sp", o, sb_ap, reads=reads)

    def stop(name):
        if ("stop_" + name) in stages:
            raise StopBuild()

    def on(s):
        return ALL or s in stages

    def din(name, shape, dt=F32):
        return nc.dram_tensor(name, list(shape), dt, kind="ExternalInput").ap()

    def dout(name, shape, dt=F32):
        return nc.dram_tensor(name, list(shape), dt, kind="ExternalOutput").ap()

    xp = din("xp", [TP, D])
    xs = din("xs", [TS, D])
    ident_d = din("ident", [128, 128])
    norm_mix = din("norm_mix", [4, D])
    norm_mlp = din("norm_mlp", [4, D])
    norm_kv = din("norm_kv", [D])
    b_glu = din("b_glu", [2, 2 * D])
    w_mlp_in = din("w_mlp_in", [4, D, DFF])
    w_mlp_out = din("w_mlp_out", [4, DFF, D])
    y_p = dout("y_p", [TP, D])
    y_s = dout("y_s", [TS, D])

    XT_d = nc.dram_tensor("XT_d", [128, 8, NT], F32).ap()
    UT_d = nc.dram_tensor("UT_d", [128, 8, NT], BF16).ap()
    XT_b = [Buf("XT_%d" % i) for i in range(17)]
    UT_b = [Buf("UT_%d" % i) for i in range(17)]

    ident_f = AR.alloc("ident_f", [128, 128], F32)
    ident_b = AR.alloc("ident_b", [128, 128], BF16)
    ones_b = AR.alloc("ones_b", [128, 128], BF16)
    B_identf, B_identb, B_ones = Buf(), Buf(), Buf()
    S.dma("sp", ident_f[:, :], ident_d, writes=[B_identf])
    S.dve(lambda e: e.tensor_copy(out=ident_b[:, :], in_=ident_f[:, :]), reads=[B_identf], writes=[B_identb])
    S.dve(lambda e: e.memset(ones_b[:, :], 1.0), writes=[B_ones])

    psA = Rot(nc, "psA", [128, 512], F32, 8, psum=True)

    vec_in = AR.alloc("vec_in", [104, 128], F32)
    VEC = AR.alloc("VEC", [128, 104], F32)
    B_vecin, B_vec = Buf(), Buf()
    S.dma("sp", vec_in[0:32, :], norm_mix.rearrange("l (q p) -> (l q) p", p=128), writes=[B_vecin])
    S.dma("sp", vec_in[32:64, :], norm_mlp.rearrange("l (q p) -> (l q) p", p=128), writes=[B_vecin])
    S.dma("sp", vec_in[64:72, :], norm_kv.rearrange("(q p) -> q p", p=128), writes=[B_vecin])
    S.dma("sp", vec_in[72:104, :], b_glu.rearrange("l (q p) -> (l q) p", p=128), writes=[B_vecin])
    ps, pb = psA.get()
    S.pe(lambda e, ps=ps: e.transpose(out=ps[:, 0:104], in_=vec_in[:, :], identity=ident_f[0:104, 0:104]),
         reads=[B_vecin, B_identf], writes=[pb])
    S.dve(lambda e, ps=ps: e.tensor_copy(out=VEC[:, :], in_=ps[:, 0:104]), reads=[pb], writes=[B_vec])

    def blk_rows(b):
        return 128 if b < 16 else TS

    evq = [0]

    def evac_copy(out, in_, reads, writes, eng=None):
        evq[0] += 1
        if eng == "act" or (eng is None and evq[0] % 2):
            S.act(lambda e: e.copy(out=out, in_=in_), reads, writes)
        else:
            S.dve(lambda e: e.tensor_copy(out=out, in_=in_), reads, writes)

    FUSE_IO = ALL
    m0 = AR.mark()
    xin = Rot(AR, "xin", [128, D], F32, 2)
    xst = Rot(AR, "xst", [128, 8, 128], F32, 2)
    for b in range(0 if FUSE_IO else 17):
        n = blk_rows(b)
        t, tb = xin.get()
        src = xp[b * 128:(b + 1) * 128, :] if b < 16 else xs[:, :]
        S.dma("sp", t[0:n, :], src, writes=[tb])
        st, stb = xst.get()
        for half in range(2):
            ps, pb = psA.get()
            for qq in range(4):
                q = half * 4 + qq
                S.pe(lambda e, ps=ps, t=t, q=q, qq=qq, n=n: e.transpose(
                    out=ps[:, qq * 128:qq * 128 + n], in_=t[0:n, q * 128:(q + 1) * 128],
                    identity=ident_f[0:n, 0:n]), reads=[tb, B_identf], writes=[pb])
            evac_copy(st[:, half * 4:half * 4 + 4, 0:n],
                      V(ps, 0, [[128, 4], [1, n]]), [pb], [stb])
        S.dma("pool", XT_d[:, :, b * 128:b * 128 + n], st[:, :, 0:n], reads=[stb], writes=[XT_b[b]])

    phase_end(m0)

    xt_r = Rot(AR, "xt", [128, 8, TW], F32, 2)
    ut_r = Rot(AR, "ut", [128, 8, TW], BF16, 2)
    sq_r = Rot(AR, "sq", [128, 8, TW], BF16, 1)
    rs_r = Rot(AR, "rs", [128, TW], F32, 2)
    rstd_r = Rot(AR, "rstd", [128, TW], F32, 2)

    def load_xt(tile):
        c0, n, bl = tile
        t, tb = xt_r.get()
        S.dma("sp", t[:, :, 0:n], XT_d[:, :, c0:c0 + n], reads=[XT_b[i] for i in bl], writes=[tb])
        return t, tb

    def load_xt_input(tile, xin1, b_xin1):
        c0, n, bl = tile
        t, tb = xt_r.get()
        for sb, blk in enumerate(bl):
            nn = blk_rows(blk)
            srcr = xp[blk * 128:(blk + 1) * 128, :] if blk < 16 else xs[:, :]
            S.dma("sp", V(xin1, 0, [[1, D]], npart=nn), srcr, writes=[b_xin1])
            for half in range(2):
                ps, pb = psA.get()
                for qq in range(4):
                    q = half * 4 + qq
                    S.pe(lambda e, ps=ps, q=q, qq=qq, nn=nn: e.transpose(
                        out=ps[:, qq * 128:qq * 128 + nn], in_=V(xin1, q * 128, [[1, 128]], npart=nn),
                        identity=ident_f[0:nn, 0:nn]), reads=[b_xin1, B_identf], writes=[pb])
                evac_copy(t[:, half * 4:half * 4 + 4, sb * 128:sb * 128 + nn],
                          V(ps, 0, [[128, 4], [1, nn]]), [pb], [tb])
        store_xt(tile, t, tb)
        return t, tb

    def store_xt(tile, t, tb):
        c0, n, bl = tile
        S.dma("pool", XT_d[:, :, c0:c0 + n], t[:, :, 0:n], reads=[tb], writes=[XT_b[i] for i in bl])

    def rmsnorm(xt, xb, n, grow, out=None, lnexp=False):
        sq, sqb = sq_r.get()
        S.act(lambda e: e.activation(out=sq[:, :, 0:n], in_=xt[:, :, 0:n], func=AF.Square), reads=[xb], writes=[sqb])
        ps, pb = psA.get()
        for q in range(8):
            S.pe(lambda e, q=q: e.matmul(out=ps[:, 0:n], lhsT=ones_b[:, :], rhs=sq[:, q, 0:n],
                                         start=(q == 0), stop=(q == 7)), reads=[sqb, B_ones], writes=[pb])
        rs, rsb = rs_r.get()
        rstd, rstdb = rstd_r.get()
        if lnexp:
            S.act(lambda e: e.activation(out=rs[:, 0:n], in_=ps[:, 0:n], func=AF.Ln, bias=EPSV[:, 0:1], scale=1.0 / D),
                  reads=[pb, b_c3], writes=[rsb])
            S.act(lambda e: e.activation(out=rstd[:, 0:n], in_=rs[:, 0:n], func=AF.Exp, scale=-0.5),
                  reads=[rsb], writes=[rstdb])
        else:
            S.act(lambda e: e.activation(out=rs[:, 0:n], in_=ps[:, 0:n], func=AF.Sqrt, bias=EPS, scale=1.0 / D),
                  reads=[pb], writes=[rsb])
            S.dve(lambda e: e.reciprocal(out=rstd[:, 0:n], in_=rs[:, 0:n]), reads=[rsb], writes=[rstdb])
        if out is None:
            ut, utb = ut_r.get()
            c0 = 0
        else:
            ut, utb, c0 = out
        for q in range(8):
            S.dve(lambda e, q=q: e.scalar_tensor_tensor(
                out=ut[:, q, c0:c0 + n], in0=xt[:, q, 0:n], scalar=VEC[:, grow + q:grow + q + 1], in1=rstd[:, 0:n],
                op0=ALU.mult, op1=ALU.mult), reads=[xb, rstdb, B_vec], writes=[utb])
        return ut, utb

    a_re_d = din("ssm_a_re", [2, 64, 64])
    a_im_d = din("ssm_a_im", [2, 64, 64])
    ldt_d = din("ssm_log_dt", [2, 64, 64])
    b_re_d = din("ssm_b_re", [2, 64, 64, 16])
    b_im_d = din("ssm_b_im", [2, 64, 64, 16])
    c_re_d = din("ssm_c_re", [2, 64, 16, 64])
    c_im_d = din("ssm_c_im", [2, 64, 16, 64])
    d_d = din("ssm_d", [2, D])
    w_glu_d = din("w_glu", [2, D, 2 * D])
    st_re_d = din("st_re", [2, TS, 4096])
    st_im_d = din("st_im", [2, TS, 4096])
    ev_d = din("ev", [128, 25])
    cmask_d = din("cmask", [128, 128])
    flag_d = din("flag", [128, 1])
    sp_state = dout("sp_state", [2, 64, 128])
    ss_state = dout("ss_state", [2, 2, TS, 4096])
    UG_d = nc.dram_tensor("UG_d", [2, 128, 64, 128], BF16).ap()
    VS_d = nc.dram_tensor("VS_d", [2, 128, 64, 128], BF16).ap()
    cc_in = [nc.dram_tensor("cc_in%d" % i, [128, 64], F32) for i in range(2)]
    cc_out = [nc.dram_tensor("cc_out%d" % i, [256, 64], F32) for i in range(2)]
    UG_b = [Buf(), Buf()]
    VS_b = [Buf(), Buf()]

    EV = AR.alloc("EV", [128, 25], F32)
    CMASK = AR.alloc("CMASK", [128, 128], F32)
    FLAG = AR.alloc("FLAG", [128, 1], F32)
    B_c2 = Buf()
    S.dma("sp", EV[:, :], ev_d, writes=[B_c2])
    S.dma("sp", CMASK[:, :], cmask_d, writes=[B_c2])
    S.dma("sp", FLAG[:, :], flag_d, writes=[B_c2])

    TWO_PI = 2.0 * math.pi
    PI_LO = 3.1415925

    def TB(name, shape, dt):
        return AR.alloc(name, shape, dt), Buf(name)

    def s5_layer(l):
        m_layer = AR.mark()
        WVT = AR.alloc("WVT", [128, 32, 2, 128], BF16)
        WO = AR.alloc("WO", [128, 32, 2, 128], BF16)
        WK = AR.alloc("WK", [128, 64, 128], BF16)
        A8x = AR.alloc("A8x", [128, 64], F32)
        Bm8 = AR.alloc("Bm8", [128, 64], F32)
        A1x = AR.alloc("A1x", [128, 64], F32)
        Bm1 = AR.alloc("Bm1", [128, 64], F32)
        ARx = AR.alloc("ARx", [128, 64], F32)
        BmR = AR.alloc("BmR", [128, 64], F32)
        b_fin = Buf("fin")
        m_prep = AR.mark()
        bp = Buf("prep_small")

        def dv(fn, extra_r=(), extra_w=()):
            S.dve(fn, reads=[bp] + list(extra_r), writes=[bp] + list(extra_w))

        def ac(fn, extra_r=(), extra_w=()):
            S.act(fn, reads=[bp] + list(extra_r), writes=[bp] + list(extra_w))

        par_in = AR.alloc("par_in", [96, 128], F32)
        S.dma("sp", par_in[0:32, :], a_re_d[l].rearrange("(i h) p -> i (h p)", h=2), writes=[bp])
        S.dma("sp", par_in[32:64, :], a_im_d[l].rearrange("(i h) p -> i (h p)", h=2), writes=[bp])
        S.dma("sp", par_in[64:96, :], ldt_d[l].rearrange("(i h) p -> i (h p)", h=2), writes=[bp])
        BRE = AR.alloc("BRE", [128, 32, 16], F32)
        BIM = AR.alloc("BIM", [128, 32, 16], F32)
        BBR = AR.alloc("BBR", [128, 32, 16], F32)
        BBI = AR.alloc("BBI", [128, 32, 16], F32)
        BT = AR.alloc("BT", [128, 32, 16], F32)
        b_bld = Buf("bload")
        S.dma("sp", BRE[:, :, :], b_re_d[l].rearrange("(i h) p c -> (h p) i c", h=2), writes=[b_bld])
        S.dma("sp", BIM[:, :, :], b_im_d[l].rearrange("(i h) p c -> (h p) i c", h=2), writes=[b_bld])
        b_cev = Buf('cev')
        b_cld = Buf("cload")
        CRE = AR.alloc("CRE", [128, 32, 16], F32)
        CIM = AR.alloc("CIM", [128, 32, 16], F32)
        CINs = []
        for (srcd, nm) in ((c_re_d, "cr"), (c_im_d, "ci")):
            CIN = AR.alloc("CIN" + nm, [128, 4, 2, 64], F32)
            sv = srcd[l].rearrange("(i4 i8 h) c p -> i8 c i4 h p", i8=8, h=2)
            for i8 in range(8):
                for hh in range(2):
                    S.dma("sp", CIN[i8 * 16:(i8 + 1) * 16, :, hh, :], sv[i8][:, :, hh, :], writes=[b_cld])
            CINs.append(CIN)
        ps, pb = psA.get()
        S.pe(lambda e, ps=ps: e.transpose(out=ps[:, 0:96], in_=par_in[:, :], identity=ident_f[0:96, 0:96]),
             reads=[bp, B_identf], writes=[pb])
        PAR = AR.alloc("PAR", [128, 96], F32)
        dv(lambda e, ps=ps: e.tensor_copy(out=PAR[:, :], in_=ps[:, 0:96]), [pb])
        for (CIN, dst) in zip(CINs, (CRE, CIM)):
            ps, pb = psA.get()
            for i4 in range(4):
                S.pe(lambda e, ps=ps, i4=i4, CIN=CIN: e.transpose(
                    out=ps[:, i4 * 128:(i4 + 1) * 128], in_=V(CIN, i4 * 128, [[1, 128]]), identity=ident_f[:, :]),
                    reads=[b_cld, B_identf], writes=[pb])
            S.act(lambda e, ps=ps, dst=dst: e.copy(out=V(dst, 0, [[1, 512]]), in_=ps[:, :]), reads=[pb], writes=[b_cev])
        ARE = PAR[:, 0:32]
        AIM = PAR[:, 32:64]
        DT = AR.alloc("DT", [128, 32], F32)
        XR = AR.alloc("XR", [128, 32], F32)
        XI = AR.alloc("XI", [128, 32], F32)
        def exp_taylor(dst, src, deg):
            dv(lambda e: e.tensor_scalar(out=dst, in0=src, scalar1=1.0 / deg, scalar2=1.0, op0=ALU.mult, op1=ALU.add))
            for k in range(deg - 1, 0, -1):
                dv(lambda e, k=k: e.scalar_tensor_tensor(out=dst, in0=dst, scalar=1.0 / k, in1=src,
                                                         op0=ALU.mult, op1=ALU.mult))
                dv(lambda e: e.tensor_scalar(out=dst, in0=dst, scalar1=1.0, scalar2=None, op0=ALU.add))

        NI = AR.alloc("NI", [128, 32], I32)
        NF = AR.alloc("NF", [128, 32], F32)
        RX = AR.alloc("RX", [128, 32], F32)
        I2 = AR.alloc("I2", [128, 32], I32)
        LDT = PAR[:, 64:96]
        dv(lambda e: e.tensor_scalar(out=NI[:, :], in0=LDT, scalar1=1.0 / math.log(2.0), scalar2=None, op0=ALU.mult))
        dv(lambda e: e.tensor_copy(out=NF[:, :], in_=NI[:, :]))
        dv(lambda e: e.scalar_tensor_tensor(out=RX[:, :], in0=NF[:, :], scalar=-0.693359375, in1=LDT,
                                            op0=ALU.mult, op1=ALU.add))
        dv(lambda e: e.scalar_tensor_tensor(out=RX[:, :], in0=NF[:, :], scalar=2.12194440e-4, in1=RX[:, :],
                                            op0=ALU.mult, op1=ALU.add))
        exp_taylor(DT[:, :], RX[:, :], 12)
        dv(lambda e: e.tensor_scalar(out=NI[:, :], in0=NF[:, :], scalar1=127.0, scalar2=None, op0=ALU.add))
        dv(lambda e: e.tensor_scalar(out=I2[:, :], in0=NI[:, :], scalar1=23, scalar2=None, op0=ALU.logical_shift_left))
        dv(lambda e: e.tensor_tensor(out=DT[:, :], in0=DT[:, :], in1=I2[:, :].bitcast(F32), op=ALU.mult))
        dv(lambda e: e.tensor_tensor(out=XR[:, :], in0=ARE, in1=DT[:, :], op=ALU.mult))
        dv(lambda e: e.tensor_tensor(out=XI[:, :], in0=AIM, in1=DT[:, :], op=ALU.mult))
        NE = 25
        ANG = AR.alloc("ANG", [128, NE, 32], F32)
        MAG = AR.alloc("MAG", [128, NE, 32], F32)
        QF = AR.alloc("QF", [128, NE, 32], F32)
        QI = AR.alloc("QI", [128, NE, 32], I32)
        RR = AR.alloc("RR", [128, NE, 32], F32)
        W1 = AR.alloc("W1", [128, NE, 32], F32)
        W2 = AR.alloc("W2", [128, NE, 32], F32)
        LR = AR.alloc("LR", [128, NE, 32], F32)
        LI = AR.alloc("LI", [128, NE, 32], F32)
        ev_b = V(EV, 0, [[1, NE], [0, 32]])
        dv(lambda e: e.tensor_tensor(out=ANG[:, :, :], in0=V(XI, 0, [[0, NE], [1, 32]]), in1=ev_b, op=ALU.mult), [B_c2])
        dv(lambda e: e.tensor_tensor(out=MAG[:, :, :], in0=V(XR, 0, [[0, NE], [1, 32]]), in1=ev_b, op=ALU.mult), [B_c2])
        dv(lambda e: e.tensor_scalar(out=MAG[:, :, :], in0=MAG[:, :, :], scalar1=0.125, scalar2=None, op0=ALU.mult))
        exp_taylor(QF[:, :, :], MAG[:, :, :], 10)
        dv(lambda e: e.tensor_tensor(out=MAG[:, :, :], in0=QF[:, :, :], in1=QF[:, :, :], op=ALU.mult))
        dv(lambda e: e.tensor_tensor(out=QF[:, :, :], in0=MAG[:, :, :], in1=MAG[:, :, :], op=ALU.mult))
        dv(lambda e: e.tensor_tensor(out=MAG[:, :, :], in0=QF[:, :, :], in1=QF[:, :, :], op=ALU.mult))
        dv(lambda e: e.tensor_scalar(out=QI[:, :, :], in0=ANG[:, :, :], scalar1=1.0 / TWO_PI, scalar2=None, op0=ALU.mult))
        dv(lambda e: e.tensor_copy(out=QF[:, :, :], in_=QI[:, :, :]))
        dv(lambda e: e.scalar_tensor_tensor(out=RR[:, :, :], in0=QF[:, :, :], scalar=-TWO_PI, in1=ANG[:, :, :],
                                            op0=ALU.mult, op1=ALU.add))

        def wrap_sin(dst, shift):
            dv(lambda e: e.tensor_scalar(out=W1[:, :, :], in0=RR[:, :, :], scalar1=shift, scalar2=None, op0=ALU.add))
            dv(lambda e: e.tensor_scalar(out=W2[:, :, :], in0=W1[:, :, :], scalar1=math.pi, scalar2=TWO_PI,
                                         op0=ALU.is_gt, op1=ALU.mult))
            dv(lambda e: e.tensor_tensor(out=W1[:, :, :], in0=W1[:, :, :], in1=W2[:, :, :], op=ALU.subtract))
            dv(lambda e: e.tensor_scalar(out=W2[:, :, :], in0=W1[:, :, :], scalar1=-math.pi, scalar2=TWO_PI,
                                         op0=ALU.is_lt, op1=ALU.mult))
            dv(lambda e: e.tensor_tensor(out=W1[:, :, :], in0=W1[:, :, :], in1=W2[:, :, :], op=ALU.add))
            dv(lambda e: e.tensor_scalar(out=W1[:, :, :], in0=W1[:, :, :], scalar1=PI_LO, scalar2=-PI_LO,
                                         op0=ALU.min, op1=ALU.max))
            ac(lambda e: e.activation(out=W2[:, :, :], in_=W1[:, :, :], func=AF.Sin))
            dv(lambda e: e.tensor_tensor(out=dst[:, :, :], in0=W2[:, :, :], in1=MAG[:, :, :], op=ALU.mult))

        wrap_sin(LI, 0.0)
        wrap_sin(LR, math.pi / 2)

        sm = [AR.alloc("sm%d" % i, [128, 32], F32) for i in range(8)]
        LBR = LR[:, 8, :]
        LBI = LI[:, 8, :]
        NR, DEN, U1, U2, FR, FI, U3, U4 = sm
        dv(lambda e: e.tensor_scalar(out=NR[:, :], in0=LBR, scalar1=-1.0, scalar2=None, op0=ALU.add))
        dv(lambda e: e.tensor_tensor(out=U1[:, :], in0=ARE, in1=ARE, op=ALU.mult))
        dv(lambda e: e.tensor_tensor(out=U2[:, :], in0=AIM, in1=AIM, op=ALU.mult))
        dv(lambda e: e.tensor_tensor(out=DEN[:, :], in0=U1[:, :], in1=U2[:, :], op=ALU.add))
        dv(lambda e: e.reciprocal(out=DEN[:, :], in_=DEN[:, :]))
        dv(lambda e: e.tensor_tensor(out=U1[:, :], in0=NR[:, :], in1=ARE, op=ALU.mult))
        dv(lambda e: e.tensor_tensor(out=U2[:, :], in0=LBI, in1=AIM, op=ALU.mult))
        dv(lambda e: e.tensor_tensor(out=U1[:, :], in0=U1[:, :], in1=U2[:, :], op=ALU.add))
        dv(lambda e: e.tensor_tensor(out=FR[:, :], in0=U1[:, :], in1=DEN[:, :], op=ALU.mult))
        dv(lambda e: e.tensor_tensor(out=U3[:, :], in0=LBI, in1=ARE, op=ALU.mult))
        dv(lambda e: e.tensor_tensor(out=U4[:, :], in0=NR[:, :], in1=AIM, op=ALU.mult))
        dv(lambda e: e.tensor_tensor(out=U3[:, :], in0=U3[:, :], in1=U4[:, :], op=ALU.subtract))
        dv(lambda e: e.tensor_tensor(out=FI[:, :], in0=U3[:, :], in1=DEN[:, :], op=ALU.mult))

        for (Ax, Bm, m) in ((A8x, Bm8, 15), (A1x, Bm1, 8), (ARx, BmR, 24)):
            dv(lambda e, Ax=Ax, m=m: e.tensor_copy(out=V(Ax, 0, [[32, 2], [1, 32]]), in_=V(LR, m * 32, [[0, 2], [1, 32]])),
               extra_w=[b_fin])
            dv(lambda e, Bm=Bm, m=m: e.tensor_scalar(out=Bm[:, 0:32], in0=LI[:, m, :], scalar1=-1.0, scalar2=None,
                                                     op0=ALU.mult), extra_w=[b_fin])
            dv(lambda e, Bm=Bm, m=m: e.tensor_copy(out=Bm[:, 32:64], in_=LI[:, m, :]), extra_w=[b_fin])

        fr_b = V(FR, 0, [[1, 32], [0, 16]])
        fi_b = V(FI, 0, [[1, 32], [0, 16]])
        dv(lambda e: e.tensor_tensor(out=BBR[:, :, :], in0=BRE[:, :, :], in1=fr_b, op=ALU.mult), [b_bld])
        dv(lambda e: e.tensor_tensor(out=BT[:, :, :], in0=BIM[:, :, :], in1=fi_b, op=ALU.mult))
        dv(lambda e: e.tensor_tensor(out=BBR[:, :, :], in0=BBR[:, :, :], in1=BT[:, :, :], op=ALU.subtract))
        dv(lambda e: e.tensor_tensor(out=BBI[:, :, :], in0=BIM[:, :, :], in1=fr_b, op=ALU.mult))
        dv(lambda e: e.tensor_tensor(out=BT[:, :, :], in0=BRE[:, :, :], in1=fi_b, op=ALU.mult))
        dv(lambda e: e.tensor_tensor(out=BBI[:, :, :], in0=BBI[:, :, :], in1=BT[:, :, :], op=ALU.add))

        tmpA = AR.alloc("tmpA", [128, 32, 8, 16], F32)
        tmpB = AR.alloc("tmpB", [128, 32, 8, 16], F32)
        XK = AR.alloc("XK", [128, 32, 2, 128], BF16)
        m_xv = AR.mark()
        XV = AR.alloc("XV", [128, 32, 2, 128], BF16)

        def lam_v(T, m0):
            return V(T, m0 * 32, [[1, 32], [32, 8], [0, 16]])

        def coef_v(T):
            return V(T, 0, [[16, 32], [0, 8], [1, 16]])

        def xout(T, ri):
            return V(T, ri * 128, [[256, 32], [16, 8], [1, 16]])

        def build_x(dst, m0, Pre, Pim, conj_neg):
            dv(lambda e: e.tensor_tensor(out=tmpA[:, :, :, :], in0=lam_v(LR, m0), in1=coef_v(Pre), op=ALU.mult), [b_cev])
            dv(lambda e: e.tensor_tensor(out=tmpB[:, :, :, :], in0=lam_v(LI, m0), in1=coef_v(Pim), op=ALU.mult))
            dv(lambda e: e.tensor_tensor(out=xout(dst, 0), in0=tmpA[:, :, :, :], in1=tmpB[:, :, :, :], op=ALU.subtract),
               extra_w=[b_fin])
            dv(lambda e: e.tensor_tensor(out=tmpA[:, :, :, :], in0=lam_v(LI, m0), in1=coef_v(Pre), op=ALU.mult))
            dv(lambda e: e.tensor_tensor(out=tmpB[:, :, :, :], in0=lam_v(LR, m0), in1=coef_v(Pim), op=ALU.mult))
            if conj_neg:
                dv(lambda e: e.scalar_tensor_tensor(out=xout(dst, 1), in0=tmpA[:, :, :, :], scalar=-1.0,
                                                    in1=tmpB[:, :, :, :], op0=ALU.mult, op1=ALU.subtract), extra_w=[b_fin])
            else:
                dv(lambda e: e.tensor_tensor(out=xout(dst, 1), in0=tmpA[:, :, :, :], in1=tmpB[:, :, :, :], op=ALU.add),
                   extra_w=[b_fin])

        build_x(XV, 0, BBR, BBI, False)
        build_x(XK, 16, BBR, BBI, False)
        build_x(WO, 8, CRE, CIM, True)

        for bk in range(8):
            ps, pb = psA.get()
            psb = ps[:, :].bitcast(BF16)
            for n in range(8):
                blk = bk * 8 + n
                S.pe(lambda e, psb=psb, n=n, blk=blk: e.transpose(
                    out=psb[:, n * 128:(n + 1) * 128], in_=V(XV, blk * 128, [[1, 128]]), identity=ident_b[:, :]),
                    reads=[bp, B_identb], writes=[pb])
            evac_copy(V(WVT, bk * 1024, [[1, 1024]]), psb[:, :], [pb], [b_fin])

        AR.release(m_xv)
        S.barrier()
        D8a = AR.alloc("D8a", [64, 16], F32)
        D8 = AR.alloc("D8", [64, 8, 16], F32)
        DPART = AR.alloc("DPART", [128, 64], F32)
        DIAG = AR.alloc("DIAG", [128, 64, 128], BF16)
        S.dma("sp", D8a[:, :], d_d[l].rearrange("(g c) -> g c", c=16), writes=[bp])
        dv(lambda e: e.tensor_copy(out=D8[:, :, :], in_=V(D8a, 0, [[0, 8], [1, 16]], npart=64)))
        ps, pb = psA.get()
        S.pe(lambda e, ps=ps: e.transpose(out=ps[:, 0:64], in_=V(D8, 0, [[1, 128]], npart=64), identity=ident_f[0:64, 0:64]),
             reads=[bp, B_identf], writes=[pb])
        dv(lambda e, ps=ps: e.tensor_copy(out=DPART[:, :], in_=ps[:, 0:64]), [pb])
        dv(lambda e: e.tensor_tensor(out=DIAG[:, :, :], in0=V(ident_f, 0, [[0, 64], [1, 128]]),
                                     in1=V(DPART, 0, [[1, 64], [0, 128]]), op=ALU.mult), [B_identf])
        for h in range(2):
            for i0 in range(0, 32, 4):
                banks = [psA.get() for _ in range(4)]
                for ri in range(2):
                    for ii in range(4):
                        i = i0 + ii
                        ps, pb = banks[ii]
                        S.pe(lambda e, ps=ps, i=i, ri=ri, h=h: e.matmul(
                            out=ps[:, 0:128],
                            lhsT=V(XK, (i * 2 + ri) * 128, [[1, 128]], p0=h * 64, npart=64),
                            rhs=V(WO, (i * 2 + ri) * 128, [[1, 128]], p0=h * 64, npart=64),
                            start=(ri == 0), stop=False), reads=[bp, b_fin], writes=[pb])
                for ii in range(4):
                    g = 2 * (i0 + ii) + h
                    ps, pb = banks[ii]
                    S.pe(lambda e, ps=ps, g=g: e.matmul(
                        out=ps[:, 0:128], lhsT=V(DIAG, g * 128, [[1, 128]]), rhs=ident_b[:, :],
                        start=False, stop=True), reads=[bp, B_identb], writes=[pb])
                for ii in range(4):
                    g = 2 * (i0 + ii) + h
                    ps, pb = banks[ii]
                    dv(lambda e, ps=ps, g=g: e.tensor_tensor(
                        out=V(WK, g * 128, [[1, 128]]), in0=ps[:, 0:128], in1=CMASK[:, :], op=ALU.mult),
                        [pb, B_c2], [b_fin])

        dbg("LR", LR[:, :, :], [128, NE, 32], [bp])
        dbg("LI", LI[:, :, :], [128, NE, 32], [bp])
        dbg("FR", FR[:, :], [128, 32], [bp])
        dbg("DT", DT[:, :], [128, 32], [bp])
        dbg("WVT", WVT[:, :, :, :], [128, 32, 2, 128], [b_fin], BF16)
        dbg("WO", WO[:, :, :, :], [128, 32, 2, 128], [b_fin], BF16)
        dbg("WK", WK[:, :, :], [128, 64, 128], [b_fin], BF16)
        dbg("A8x", A8x[:, :], [128, 64], [b_fin])
        dbg("Bm8", Bm8[:, :], [128, 64], [b_fin])
        stop("prep")
        AR.release(m_prep)
        S.barrier()

        T1, b_t1 = TB("T1", [128, 64], F32)
        T2, b_t2 = TB("T2", [128, 64], F32)
        T3, b_t3 = TB("T3", [128, 64], F32)
        ZST, b_zst = TB("ZST", [128, 64], F32)
        SI, b_si = TB("SI", [128, 64], F32)
        Ugs, b_ugs = TB("Ugs", [128, 64, 16], BF16)
        m_work = AR.mark()
        UTS, b_uts = TB("UTS", [128, 8, 1024], BF16)
        Utok, b_utok = TB("Utok", [128, 8, 1024], BF16)
        ug_r = Rot(AR, "Ug", [128, 64, 128], BF16, 2)
        vss_r = Rot(AR, "VSS", [128, 64, 129], BF16, 2)
        RB = 8
        MB = 128 // RB
        SINI, b_sini = TB("SINI", [128, 64], F32)
        BSUM = [TB("BSUM%d" % i, [128, 64, MB], F32) for i in range(2)]
        TT1, b_tt1 = TB("TT1", [128, 64, MB], F32)
        TT2, b_tt2 = TB("TT2", [128, 64, MB], F32)
        zh_r = Rot(AR, "ZH", [128, MB, 64], F32, 1)
        S.dve(lambda e: e.memset(ZST[:, :], 0.0), writes=[b_zst])

        def cmul_add(P, b_p, VSt, b_v, col0):
            S.dve(lambda e: e.tensor_tensor(out=TT1[:, :, :], in0=P[:, :, :], in1=V(A8x, 0, [[1, 64], [0, MB]]), op=ALU.mult),
                  reads=[b_p, b_fin], writes=[b_tt1])
            S.dve(lambda e: e.tensor_tensor(out=V(TT2, 0, [[32 * MB, 2], [MB, 32], [1, MB]]),
                                            in0=V(P, 32 * MB, [[-32 * MB, 2], [MB, 32], [1, MB]]),
                                            in1=V(Bm8, 0, [[32, 2], [1, 32], [0, MB]]), op=ALU.mult),
                  reads=[b_p, b_fin], writes=[b_tt2])
            S.dve(lambda e: e.tensor_tensor(out=TT1[:, :, :], in0=TT1[:, :, :], in1=TT2[:, :, :], op=ALU.add),
                  reads=[b_tt1, b_tt2], writes=[b_tt1])
            S.dve(lambda e: e.tensor_tensor(out=P[:, :, :], in0=TT1[:, :, :], in1=V(VSt, col0, [[129, 64], [RB, MB]]),
                                            op=ALU.add), reads=[b_tt1, b_v], writes=[b_p])

        def scan_blocked(prev, VSt, b_v, hist, st):
            pt, pbuf, poff = prev
            ipt, ipbuf, ipoff = prev
            BS, b_bs = BSUM[st]
            if hist:
                S.dve(lambda e, pt=pt, poff=poff: e.tensor_copy(out=SINI[:, :], in_=V(pt, poff, [[1, 64]])),
                      reads=[pbuf], writes=[b_sini])
                ipt, ipbuf, ipoff = SINI, b_sini, 0
            if not hist:
                S.dve(lambda e: e.tensor_copy(out=BS[:, :, :], in_=V(VSt, 1, [[129, 64], [RB, MB]])), reads=[b_v[0]], writes=[b_bs])
                for s in range(1, RB):
                    cmul_add(BS, b_bs, VSt, b_v[s], 1 + s)
            ZH, zb = zh_r.get()
            for m in range(MB):
                S.dve(lambda e, pt=pt, poff=poff: e.tensor_tensor(
                    out=T1[:, :], in0=V(pt, poff, [[1, 64]]), in1=ARx[:, :], op=ALU.mult),
                    reads=[pbuf, b_fin], writes=[b_t1])
                S.dve(lambda e, pt=pt, poff=poff: e.tensor_tensor(
                    out=V(T2, 0, [[32, 2], [1, 32]]), in0=V(pt, poff + 32, [[-32, 2], [1, 32]]),
                    in1=V(BmR, 0, [[32, 2], [1, 32]]), op=ALU.mult), reads=[pbuf, b_fin], writes=[b_t2])
                S.dve(lambda e: e.tensor_tensor(out=T3[:, :], in0=T1[:, :], in1=T2[:, :], op=ALU.add),
                      reads=[b_t1, b_t2], writes=[b_t3])
                S.dve(lambda e, ZH=ZH, m=m: e.tensor_tensor(
                    out=ZH[:, m, :], in0=T3[:, :], in1=V(BS, m, [[MB, 64]]), op=ALU.add),
                    reads=[b_t3, b_bs], writes=[zb])
                pt, pbuf, poff = ZH, zb, m * 64
            if hist:
                S.pool(lambda e: e.tensor_copy(out=V(VSt, 0, [[129, 64]]), in_=V(ipt, ipoff, [[1, 64]])),
                       reads=[ipbuf], writes=[b_v[RB]])
                S.dve(lambda e: e.tensor_copy(out=V(BS, 0, [[MB, 64]]), in_=V(ipt, ipoff, [[1, 64]])),
                      reads=[ipbuf], writes=[b_bs])
                S.dve(lambda e, ZH=ZH: e.tensor_copy(out=V(BS, 1, [[MB, 64], [1, MB - 1]]),
                                                     in_=V(ZH, 0, [[1, 64], [64, MB - 1]])), reads=[zb], writes=[b_bs])
                for s in range(RB - 1):
                    cmul_add(BS, b_bs, VSt, b_v[s], 1 + s)
                    S.dve(lambda e, s=s: e.tensor_copy(out=V(VSt, 1 + s, [[129, 64], [RB, MB]]), in_=BS[:, :, :]),
                          reads=[b_bs], writes=[b_v[s]])
                S.pool(lambda e, ZH=ZH: e.tensor_copy(out=V(VSt, RB, [[129, 64], [RB, MB]]), in_=V(ZH, 0, [[1, 64], [64, MB]])),
                       reads=[zb], writes=[b_v[RB - 1]])
            return pt, pbuf, poff

        def scan(prev, VSt, b_v, nsteps, hist):
            pt, pbuf, poff = prev
            if hist:
                S.pool(lambda e, pt=pt, poff=poff: e.tensor_copy(out=V(VSt, 0, [[129, 64]]), in_=V(pt, poff, [[1, 64]])),
                       reads=[pbuf], writes=[b_v])
            ring = rb = None
            for k in range(nsteps):
                r = k % 32
                if r == 0:
                    ring, rb = zh_r.get()
                S.dve(lambda e, pt=pt, poff=poff: e.tensor_tensor(
                    out=T1[:, :], in0=V(pt, poff, [[1, 64]]), in1=A8x[:, :], op=ALU.mult),
                    reads=[pbuf, b_fin], writes=[b_t1])
                S.dve(lambda e, pt=pt, poff=poff: e.tensor_tensor(
                    out=V(T2, 0, [[32, 2], [1, 32]]), in0=V(pt, poff + 32, [[-32, 2], [1, 32]]),
                    in1=V(Bm8, 0, [[32, 2], [1, 32]]), op=ALU.mult), reads=[pbuf, b_fin], writes=[b_t2])
                S.dve(lambda e: e.tensor_tensor(out=T3[:, :], in0=T1[:, :], in1=T2[:, :], op=ALU.add),
                      reads=[b_t1, b_t2], writes=[b_t3])
                S.dve(lambda e, ring=ring, r=r, k=k: e.tensor_tensor(
                    out=ring[:, r, :], in0=T3[:, :], in1=V(VSt, k + 1, [[129, 64]]), op=ALU.add),
                    reads=[b_t3, b_v], writes=[rb])
                pt, pbuf, poff = ring, rb, r * 64
                if hist and r == 31:
                    k0 = k - 31
                    S.pool(lambda e, ring=ring, k0=k0: e.tensor_copy(
                        out=V(VSt, k0 + 1, [[129, 64], [1, 32]]), in_=V(ring, 0, [[1, 64], [64, 32]])),
                        reads=[rb], writes=[b_v])
            return pt, pbuf, poff

        def stage2(src_t, src_b, dst_t, dst_b, ncol, npart, eng=None):
            for gb in range(8):
                ps, pb = psA.get()
                psb = ps[:, :].bitcast(BF16)
                for gg in range(8):
                    g = gb * 8 + gg
                    S.pe(lambda e, psb=psb, gg=gg, g=g: e.transpose(
                        out=psb[:, gg * 128:gg * 128 + ncol],
                        in_=V(src_t, 128 * g, [[1, 128]], npart=npart), identity=ident_b[0:npart, 0:npart]),
                        reads=[src_b, B_identb], writes=[pb])
                evac_copy(V(dst_t, gb * 8 * ncol, [[ncol, 8], [1, ncol]]),
                          bass.AP(tensor=psb.tensor, offset=0, ap=[[1024, 128], [128, 8], [1, ncol]]), [pb], [dst_b], eng)

        vss_tiles = []
        for st in range(2):
            for tt in range(4):
                tile = TILES[st * 4 + tt]
                xt, xb = load_xt_input(tile, TT1, b_tt1) if (l == 0 and FUSE_IO) else load_xt(tile)
                rmsnorm(xt, xb, TW, 8 * l, out=(UTS, b_uts, tt * TW))
            for j in range(8):
                ps, pb = psA.get()
                psb = ps[:, :].bitcast(BF16)
                for q in range(8):
                    S.pe(lambda e, psb=psb, q=q, j=j: e.transpose(
                        out=psb[:, q * 128:(q + 1) * 128], in_=V(UTS, q * 1024 + j, [[8, 128]]), identity=ident_b[:, :]),
                        reads=[b_uts, B_identb], writes=[pb])
                evac_copy(V(Utok, j * 16, [[128, 64], [1, 16]]),
                          bass.AP(tensor=psb.tensor, offset=0, ap=[[1024, 128], [16, 64], [1, 16]]), [pb], [b_utok], "act")
            Ug, b_ug = ug_r.get()
            stage2(Utok, b_utok, Ug, b_ug, 128, 128, "act")
            VSS, _unused = vss_r.get()
            b_vs = [Buf() for _ in range(RB + 1)]
            for bk in range(16):
                ps, pb = psA.get()
                for ee in range(4):
                    en = bk * 4 + ee
                    ri, i = divmod(en, 32)
                    for h in range(2):
                        S.pe(lambda e, ps=ps, ee=ee, ri=ri, i=i, h=h, Ug=Ug: e.matmul(
                            out=ps[h * 64:(h + 1) * 64, ee * 128:(ee + 1) * 128],
                            lhsT=V(WVT, (i * 2 + ri) * 128 + h * 64, [[1, 64]]),
                            rhs=V(Ug, (2 * i + h) * 128, [[1, 128]]), start=True, stop=True),
                            reads=[b_fin, b_ug], writes=[pb])
                evac_copy(V(VSS, bk * 4 * 129 + 1, [[129, 4], [1, 128]]), V(ps, 0, [[128, 4], [1, 128]]), [pb], list(b_vs), "act")
            vss_tiles.append((Ug, b_ug, VSS, b_vs))
        state = (ZST, b_zst, 0)
        for st in range(2):
            state = scan_blocked(state, vss_tiles[st][2], vss_tiles[st][3], False, st)

        pt, pbuf, poff = state
        dbg("P1", V(pt, poff, [[1, 64]]), [128, 64], [pbuf])
        stop("s5a")
        b_ccin, b_ccout = Buf(), Buf()
        S.dma("pool", cc_in[l].ap(), V(pt, poff, [[1, 64]]), reads=[pbuf], writes=[b_ccin])
        S.cc(lambda e: e.collective_compute("AllGather", ALU.bypass, replica_groups=[[0, 1], [2, 3], [4, 5], [6, 7]],
                                            ins=[cc_in[l].ap().opt()], outs=[cc_out[l].ap().opt()]),
             reads=[b_ccin], writes=[b_ccout])
        S.dma("pool", SI[:, :], cc_out[l].ap()[0:128, :], reads=[b_ccout], writes=[b_si])
        S.dve(lambda e: e.tensor_scalar(out=SI[:, :], in0=SI[:, :], scalar1=FLAG[:, 0:1], scalar2=None, op0=ALU.mult),
              reads=[b_si, B_c2], writes=[b_si])

        dbg("SI", SI[:, :], [128, 64], [b_si])
        stop("xchg")
        tile = TILES[8]
        xt, xb = load_xt_input(tile, TT1, b_tt1) if (l == 0 and FUSE_IO) else load_xt(tile)
        uts, utsb = rmsnorm(xt, xb, TS, 8 * l)
        S.dve(lambda e: e.memset(Utok[0:TS, :, :], 0.0), writes=[b_utok])
        ps, pb = psA.get()
        psb = ps[:, :].bitcast(BF16)
        for q in range(8):
            S.pe(lambda e, psb=psb, q=q, uts=uts: e.transpose(
                out=psb[0:TS, q * 128:(q + 1) * 128], in_=uts[:, q, 0:TS], identity=ident_b[:, :]),
                reads=[utsb, B_identb], writes=[pb])
        S.act(lambda e, psb=psb: e.copy(out=V(Utok, 0, [[128, 64], [1, 16]], npart=TS),
                                        in_=bass.AP(tensor=psb.tensor, offset=0, ap=[[1024, TS], [16, 64], [1, 16]])),
              reads=[pb], writes=[b_utok])
        S.dve(lambda e, psb=psb: e.tensor_copy(out=V(Utok, 7 * 16, [[128, 64], [1, 16]], npart=TS),
                                               in_=bass.AP(tensor=psb.tensor, offset=0, ap=[[1024, TS], [16, 64], [1, 16]])),
              reads=[pb], writes=[b_utok])
        stage2(Utok, b_utok, Ugs, b_ugs, TS, TS)

        state = (SI, b_si, 0)
        tl = vss_tiles
        for st in range(2):
            Ug, b_ug, VSS, b_vs = tl[st]
            state = scan_blocked(state, VSS, b_vs, True, st)
        for st in range(2):
            Ug, b_ug, VSS, b_vs = tl[st]
            for h in range(2):
                for i0 in range(0, 32, 4):
                    banks = [psA.get() for _ in range(4)]
                    for ri in range(2):
                        for ii in range(4):
                            i = i0 + ii
                            ps, pb = banks[ii]
                            S.pe(lambda e, ps=ps, i=i, ri=ri, h=h, VSS=VSS: e.matmul(
                                out=ps[:, 0:128],
                                lhsT=V(VSS, (ri * 32 + i) * 129, [[1, 128]], p0=h * 64, npart=64),
                                rhs=V(WO, (i * 2 + ri) * 128, [[1, 128]], p0=h * 64, npart=64),
                                start=(ri == 0), stop=False), reads=list(b_vs) + [b_fin], writes=[pb])
                    for ii in range(4):
                        g = 2 * (i0 + ii) + h
                        ps, pb = banks[ii]
                        S.pe(lambda e, ps=ps, g=g, Ug=Ug: e.matmul(
                            out=ps[:, 0:128], lhsT=V(Ug, g * 128, [[1, 128]]),
                            rhs=V(WK, g * 128, [[1, 128]]), start=False, stop=True),
                            reads=[b_ug, b_fin], writes=[pb])
                    for ii in range(4):
                        g = 2 * (i0 + ii) + h
                        ps, pb = banks[ii]
                        S.act(lambda e, ps=ps, g=g: e.activation(
                            out=V(Utok, 16 * g, [[1024, 8], [1, 16]]),
                            in_=V(ps, 0, [[16, 8], [1, 16]]), func=AF.Gelu_apprx_tanh),
                            reads=[pb], writes=[b_utok])
            for j in range(8):
                ps, pb = psA.get()
                psb = ps[:, :].bitcast(BF16)
                for q in range(8):
                    S.pe(lambda e, psb=psb, q=q, j=j: e.transpose(
                        out=psb[:, q * 128:(q + 1) * 128], in_=Utok[:, j, q * 128:(q + 1) * 128], identity=ident_b[:, :]),
                        reads=[b_utok, B_identb], writes=[pb])
                evac_copy(V(UTS, j, [[1024, 8], [8, 128]]),
                          bass.AP(tensor=psb.tensor, offset=0, ap=[[1024, 128], [128, 8], [1, 128]]), [pb], [b_uts], "act")
            S.dma("pool", UT_d[:, :, st * 1024:(st + 1) * 1024], UTS[:, :, :], reads=[b_uts],
                  writes=[UT_b[i] for i in range(st * 8, st * 8 + 8)])

        stop("s5b")
        pt, pbuf, poff = state
        ps, pb = psA.get()
        S.pe(lambda e, ps=ps, pt=pt, poff=poff: e.transpose(out=ps[0:64, 0:128], in_=V(pt, poff, [[1, 64]]),
                                                            identity=ident_f[:, :]),
             reads=[pbuf, B_identf], writes=[pb])
        fin_o, b_fino = TB("fin_o", [64, 128], F32)
        S.dve(lambda e, ps=ps: e.tensor_copy(out=fin_o[:, :], in_=ps[0:64, 0:128]), reads=[pb], writes=[b_fino])
        S.dma("pool", sp_state[l], fin_o[:, :], reads=[b_fino])

        stop("fin")
        AR.release(m_work)
        S.barrier()
        S0IN, b_s0in = TB("S0IN", [TS, 4096], F32)
        S0, b_s0 = TB("S0", [128, 64, TS], F32)
        S0b, b_s0b = TB("S0b", [128, 64, TS], BF16)
        for half in range(2):
            S.dma("sp", S0IN[:, :], (st_re_d if half == 0 else st_im_d)[l], writes=[b_s0in])
            ps, pb = psA.get()
            for ee in range(32):
                en = half * 32 + ee
                S.pe(lambda e, ps=ps, ee=ee, en=en: e.transpose(
                    out=ps[:, ee * TS:(ee + 1) * TS], in_=V(S0IN, ee * 128, [[1, 128]], npart=TS),
                    identity=ident_f[0:TS, 0:TS]), reads=[b_s0in, B_identf], writes=[pb])
            S.dve(lambda e, ps=ps, half=half: e.tensor_copy(out=V(S0, half * 512, [[1, 512]]), in_=ps[:, :]),
                  reads=[pb], writes=[b_s0])
            S.act(lambda e, ps=ps, half=half: e.copy(out=V(S0b, half * 512, [[1, 512]]), in_=ps[:, :]),
                  reads=[pb], writes=[b_s0b])
        stop("s0")
        Gs, b_gs = TB("Gs", [TS, D], BF16)
        for h in range(2):
            ps, pb = psA.get()
            for i in range(32):
                g = 2 * i + h
                for ri in range(2):
                    S.pe(lambda e, ps=ps, i=i, ri=ri, h=h: e.matmul(
                        out=ps[0:TS, i * 16:(i + 1) * 16],
                        lhsT=V(S0b, (ri * 32 + i) * TS, [[1, TS]], p0=h * 64, npart=64),
                        rhs=V(WO, (i * 2 + ri) * 128, [[1, 16]], p0=h * 64, npart=64),
                        start=(ri == 0), stop=False), reads=[b_s0b, b_fin], writes=[pb])
                S.pe(lambda e, ps=ps, i=i, g=g: e.matmul(
                    out=ps[0:TS, i * 16:(i + 1) * 16], lhsT=V(Ugs, g * TS, [[1, TS]]),
                    rhs=V(WK, g * 128, [[1, 16]]), start=False, stop=True), reads=[b_ugs, b_fin], writes=[pb])
            S.act(lambda e, ps=ps, h=h: e.activation(
                out=V(Gs, 16 * h, [[32, 32], [1, 16]], npart=TS), in_=V(ps, 0, [[16, 32], [1, 16]], npart=TS),
                func=AF.Gelu_apprx_tanh), reads=[pb], writes=[b_gs])
        ps, pb = psA.get()
        psb = ps[:, :].bitcast(BF16)
        for q in range(8):
            S.pe(lambda e, psb=psb, q=q: e.transpose(out=psb[:, q * TS:(q + 1) * TS], in_=Gs[:, q * 128:(q + 1) * 128],
                                                     identity=ident_b[0:TS, 0:TS]), reads=[b_gs, B_identb], writes=[pb])
        gts, b_gts = TB("gts", [128, 8, TS], BF16)
        S.dve(lambda e, psb=psb: e.tensor_copy(out=V(gts, 0, [[1, 8 * TS]]), in_=psb[:, 0:8 * TS]), reads=[pb], writes=[b_gts])
        S.dma("pool", UT_d[:, :, TP:TP + TS], gts[:, :, :], reads=[b_gts], writes=[UT_b[16]])
        stop("ys")
        SN, b_sn = TB("SN", [128, 64, TS], F32)
        TA, b_ta = TB("TA", [128, 64, TS], F32)
        TBt, b_tb = TB("TBt", [128, 64, TS], F32)
        S.dve(lambda e: e.tensor_tensor(out=TA[:, :, :], in0=S0[:, :, :], in1=V(A1x, 0, [[1, 64], [0, TS]]), op=ALU.mult),
              reads=[b_s0, b_fin], writes=[b_ta])
        S.dve(lambda e: e.tensor_tensor(out=V(TBt, 0, [[32 * TS, 2], [TS, 32], [1, TS]]),
                                        in0=V(S0, 32 * TS, [[-32 * TS, 2], [TS, 32], [1, TS]]),
                                        in1=V(Bm1, 0, [[32, 2], [1, 32], [0, TS]]), op=ALU.mult),
              reads=[b_s0, b_fin], writes=[b_tb])
        S.dve(lambda e: e.tensor_tensor(out=TA[:, :, :], in0=TA[:, :, :], in1=TBt[:, :, :], op=ALU.add),
              reads=[b_ta, b_tb], writes=[b_ta])
        for half in range(2):
            ps, pb = psA.get()
            for ee in range(32):
                en = half * 32 + ee
                ri, i = divmod(en, 32)
                for h in range(2):
                    S.pe(lambda e, ps=ps, ee=ee, ri=ri, i=i, h=h: e.matmul(
                        out=ps[h * 64:(h + 1) * 64, ee * TS:(ee + 1) * TS],
                        lhsT=V(WVT, (i * 2 + ri) * 128 + h * 64, [[1, 64]], p0=64, npart=64),
                        rhs=V(Ugs, (2 * i + h) * TS, [[1, TS]], p0=64, npart=64), start=True, stop=True),
                        reads=[b_fin, b_ugs], writes=[pb])
            S.dve(lambda e, ps=ps, half=half: e.tensor_tensor(
                out=V(SN, half * 512, [[1, 512]]), in0=ps[:, :], in1=V(TA, half * 512, [[1, 512]]), op=ALU.add),
                reads=[pb, b_ta], writes=[b_sn])
        OUTS, b_outs = TB("OUTS", [TS, 32, 128], F32)
        for ri in range(2):
            for bk in range(8):
                ps, pb = psA.get()
                for ee in range(4):
                    i = bk * 4 + ee
                    S.pe(lambda e, ps=ps, ee=ee, i=i, ri=ri: e.transpose(
                        out=ps[0:TS, ee * 128:(ee + 1) * 128], in_=V(SN, (ri * 32 + i) * TS, [[1, TS]]),
                        identity=ident_f[:, :]), reads=[b_sn, B_identf], writes=[pb])
                evac_copy(V(OUTS, bk * 512, [[1, 512]], npart=TS), ps[0:TS, :], [pb], [b_outs])
            S.dma("pool", ss_state[l, ri], V(OUTS, 0, [[1, 4096]], npart=TS), reads=[b_outs])

        AR.release(m_layer)
        S.barrier()

    def glu(l, after_w=None):
        m = AR.mark()
        WG = AR.alloc("WG", [128, 8, 2048], BF16)
        b_wg = [Buf() for _ in range(4)]
        for j in range(4):
            S.dma("pool", WG[:, :, j * 512:(j + 1) * 512],
                  w_glu_d[l, :, j * 512:(j + 1) * 512].rearrange("(q p) f -> p q f", p=128), writes=[b_wg[j]])
        lazy = after_w() if after_w is not None else []
        sg_r = Rot(AR, "sg", [128, TW], F32, 2)
        gt_r = Rot(AR, "gt", [128, TW], F32, 2)
        for tile in TILES:
            c0, n, bl = tile
            xt, xb = load_xt(tile)
            ut, utb = ut_r.get()
            S.dma("sp", ut[:, :, 0:n], UT_d[:, :, c0:c0 + n], reads=[UT_b[i] for i in bl], writes=[utb])
            for m8 in range(8):
                ps, pb = psA.get()
                for k in range(2):
                    col = k * 1024 + m8 * 128
                    for q in range(8):
                        S.pe(lambda e, ps=ps, k=k, col=col, q=q, ut=ut, n=n: e.matmul(
                            out=ps[:, k * 256:k * 256 + n], lhsT=WG[:, q, col:col + 128], rhs=ut[:, q, 0:n],
                            start=(q == 0), stop=(q == 7)), reads=[utb, b_wg[col // 512]], writes=[pb])
                sg, sgb = sg_r.get()
                r1 = 72 + 16 * l + m8
                S.act(lambda e, ps=ps, sg=sg, r1=r1, n=n: e.activation(
                    out=sg[:, 0:n], in_=ps[:, 256:256 + n], func=AF.Sigmoid, bias=VEC[:, r1 + 8:r1 + 9]),
                    reads=[pb, B_vec], writes=[sgb])
                gt, gtb = gt_r.get()
                S.dve(lambda e, ps=ps, sg=sg, gt=gt, r1=r1, n=n: e.scalar_tensor_tensor(
                    out=gt[:, 0:n], in0=ps[:, 0:n], scalar=VEC[:, r1:r1 + 1], in1=sg[:, 0:n],
                    op0=ALU.add, op1=ALU.mult), reads=[pb, sgb, B_vec], writes=[gtb])
                S.pool(lambda e, xt=xt, gt=gt, m8=m8, n=n: e.tensor_tensor(
                    out=xt[:, m8, 0:n], in0=xt[:, m8, 0:n], in1=gt[:, 0:n], op=ALU.add),
                    reads=[gtb, xb], writes=[xb])
            store_xt(tile, xt, xb)
            if len(lazy) > 8:
                lazy.pop(0)()
        while len(lazy) > 8:
            lazy.pop(0)()
        phase_end(m)
        return lazy

    w_k_d = din("w_k", [D, 256])
    w_v_d = din("w_v", [D, 256])
    w_q_d = din("w_q", [2, D, D])
    w_o_d = din("w_o", [2, D, D])
    gqk_d = din("gqk", [128, 6])
    sinks_d = din("sinks", [128, 16])
    rot_d = din("rot", [128, 128])
    bones_d = din("bones", [128, 128])
    mask_d = din("masks", [3, 128, 128])
    cos_d = din("cos_t", [128, NT])
    sin_d = din("sin_t", [128, NT])
    cache_k_d = din("cache_k", [TS, 128, 256])
    cache_v_d = din("cache_v", [TS, 128, 256])
    k_last = dout("k_last", [128, 256])
    v_last = dout("v_last", [128, 256])
    ks_out = dout("ks_out", [TS, 128, 256])
    vs_out = dout("vs_out", [TS, 128, 256])
    kvx_in = nc.dram_tensor("kvx_in", [128, 384], F32)
    kvx_out = nc.dram_tensor("kvx_out", [256, 384], F32)

    KT_d = nc.dram_tensor("KT_d", [128, 4, 128 + NT], BF16).ap()
    V_d = nc.dram_tensor("V_d", [128, 18, 256], BF16).ap()
    KTs_d = nc.dram_tensor("KTs_d", [128, TS, 4, 128], BF16).ap()
    Vs_d = nc.dram_tensor("Vs_d", [128, TS, 256], BF16).ap()
    b_kvd = Buf("kv_dram")

    if True:
        MASK_F = AR.alloc("MASK_F", [128, 256], BF16)
        MASK_R = AR.alloc("MASK_R", [128, 256], BF16)
        ROT_b = AR.alloc("ROT_b", [128, 128], BF16)
        BONES_b = AR.alloc("BONES_b", [128, 128], BF16)
        GQK = AR.alloc("GQK", [128, 6], F32)
        ESINK = AR.alloc("ESINK", [128, 16], F32)
        EPSV = AR.alloc("EPSV", [128, 1], F32)
        b_c3 = Buf()
        S.dve(lambda e: e.memset(EPSV[:, :], EPS), writes=[b_c3])
        m_tmp = AR.mark()
        tmpc = AR.alloc("tmpc", [128, 256], F32)
        tmpm = AR.alloc("tmpm", [128, 3, 128], F32)
        for i3 in range(3):
            S.dma("sp", tmpm[:, i3, :], mask_d[i3], writes=[b_c3])
        S.dve(lambda e: e.tensor_copy(out=MASK_R[:, :], in_=V(tmpm, 0, [[1, 256]])), reads=[b_c3], writes=[b_c3])
        S.dve(lambda e: e.tensor_copy(out=MASK_F[:, 0:128], in_=tmpm[:, 2, :]), reads=[b_c3], writes=[b_c3])
        S.dve(lambda e: e.tensor_copy(out=MASK_F[:, 128:256], in_=tmpm[:, 1, :]), reads=[b_c3], writes=[b_c3])
        S.dma("sp", GQK[:, :], gqk_d, writes=[b_c3])
        S.dma("sp", ESINK[:, :], sinks_d, writes=[b_c3])
        S.act(lambda e: e.activation(out=ESINK[:, :], in_=ESINK[:, :], func=AF.Exp), reads=[b_c3], writes=[b_c3])
        S.dma("sp", tmpc[:, 0:128], rot_d, writes=[b_c3])
        S.dma("sp", tmpc[:, 128:256], bones_d, writes=[b_c3])
        S.dve(lambda e: e.tensor_copy(out=ROT_b[:, :], in_=tmpc[:, 0:128]), reads=[b_c3], writes=[b_c3])
        S.dve(lambda e: e.tensor_copy(out=BONES_b[:, :], in_=tmpc[:, 128:256]), reads=[b_c3], writes=[b_c3])
        AR.release(m_tmp)
        S.barrier()

    class QK:
        def __init__(self):
            self.cs_r = Rot(AR, "cs", [128, 2, TW], F32, 2)
            self.sqb_r = Rot(AR, "sqb", [128, TW], BF16, 2)
            self.kb_r = Rot(AR, "kbb", [128, TW], BF16, 2)
            self.rq_r = Rot(AR, "rq", [128, TW], F32, 2)
            self.ta_r = Rot(AR, "ta", [128, TW], F32, 2)
            self.tb_r = Rot(AR, "tbb", [128, TW], F32, 2)

    if True:
        def load_cs(H, tile):
            c0, n, bl = tile
            cs, csb = H.cs_r.get()
            S.dma("sp", cs[:, 0, 0:n], cos_d[:, c0:c0 + n], writes=[csb])
            S.dma("sp", cs[:, 1, 0:n], sin_d[:, c0:c0 + n], writes=[csb])
            return cs, csb

        def qk_post(H, ps, pb, n, gcol, cs, csb, out_ap, out_bs):
            sq, sqb = H.sqb_r.get()
            kb, kbb = H.kb_r.get()
            S.act(lambda e: e.activation(out=sq[:, 0:n], in_=ps[:, 0:n], func=AF.Square), reads=[pb], writes=[sqb])
            S.act(lambda e: e.copy(out=kb[:, 0:n], in_=ps[:, 0:n]), reads=[pb], writes=[kbb])
            ps2, pb2 = psA.get()
            S.pe(lambda e: e.matmul(out=ps2[:, 0:n], lhsT=BONES_b[:, :], rhs=sq[:, 0:n], start=True, stop=True),
                 reads=[sqb, b_c3], writes=[pb2])
            S.pe(lambda e: e.matmul(out=ps2[:, 256:256 + n], lhsT=ROT_b[:, :], rhs=kb[:, 0:n], start=True, stop=True),
                 reads=[kbb, b_c3], writes=[pb2])
            rq, rqb = H.rq_r.get()
            S.act(lambda e: e.activation(out=rq[:, 0:n], in_=ps2[:, 0:n], func=AF.Ln, bias=EPSV[:, 0:1], scale=1.0 / 64),
                  reads=[pb2, b_c3], writes=[rqb])
            S.act(lambda e: e.activation(out=rq[:, 0:n], in_=rq[:, 0:n], func=AF.Exp, scale=-0.5),
                  reads=[rqb], writes=[rqb])
            ta, tab = H.ta_r.get()
            tb, tbb = H.tb_r.get()
            S.dve(lambda e: e.scalar_tensor_tensor(out=ta[:, 0:n], in0=ps[:, 0:n], scalar=GQK[:, gcol:gcol + 1],
                                                   in1=cs[:, 0, 0:n], op0=ALU.mult, op1=ALU.mult),
                  reads=[pb, csb, b_c3], writes=[tab])
            S.dve(lambda e: e.scalar_tensor_tensor(out=tb[:, 0:n], in0=ps2[:, 256:256 + n],
                                                   scalar=GQK[:, gcol + 1:gcol + 2], in1=cs[:, 1, 0:n],
                                                   op0=ALU.mult, op1=ALU.mult),
                  reads=[pb2, csb, b_c3], writes=[tbb])
            S.pool(lambda e: e.tensor_tensor(out=ta[:, 0:n], in0=ta[:, 0:n], in1=tb[:, 0:n], op=ALU.add),
                   reads=[tab, tbb], writes=[tab])
            S.dve(lambda e: e.tensor_tensor(out=out_ap, in0=ta[:, 0:n], in1=rq[:, 0:n], op=ALU.mult),
                  reads=[tab, rqb], writes=list(out_bs))

    def kv_phase(hook=None):
        m_kv = AR.mark()
        H = QK()
        KT_all = AR.alloc("KT_all", [128, 4, 128 + NT], BF16)
        V_all = AR.alloc("V_all", [128, 18, 256], BF16)
        KTs = AR.alloc("KTs", [128, TS, 4, 128], BF16)
        Vs = AR.alloc("Vs", [128, TS, 256], BF16)
        b_kt = [Buf() for _ in range(18)]
        b_vv = [Buf() for _ in range(18)]
        b_kts, b_vs2 = Buf(), Buf()
        b_kso, b_vso = Buf(), Buf()
        S.dma("sp", ks_out[:, 0:127, :], cache_k_d[:, 1:128, :], writes=[b_kso])
        S.dma("sp", vs_out[:, 0:127, :], cache_v_d[:, 1:128, :], writes=[b_vso])
        S.pool(lambda e: e.memset(V_all[:, 17, :], 0.0), writes=[b_vv[17]])
        WKd = AR.alloc("WKd", [128, 8, 4, 2, 64], BF16)
        WVt = AR.alloc("WVt", [128, 8, 256], BF16)
        b_wk, b_wv = Buf(), Buf()
        WKs = AR.alloc("WKs", [128, 8, 256], BF16)
        b_wks = Buf()
        S.dma("pool", WKs[:, :, :], w_k_d.rearrange("(q p) f -> p q f", p=128), writes=[b_wks])
        for q in range(8):
            S.pool(lambda e, q=q: e.tensor_copy(out=V(WKd, q * 512, [[128, 4], [64, 2], [1, 64]]),
                                                in_=V(WKs, q * 256, [[64, 4], [0, 2], [1, 64]])),
                   reads=[b_wks], writes=[b_wk])
        S.dma("pool", WVt[:, :, :], w_v_d.rearrange("(q p) f -> p q f", p=128), writes=[b_wv])
        if hook is not None:
            hook()
        for ti, tile in enumerate(TILES):
            c0, n, bl = tile
            xt, xb = load_xt(tile)
            ut, utb = rmsnorm(xt, xb, n, 64, lnexp=True)
            cs, csb = load_cs(H, tile)
            kbufs = [b_kt[1 + i] for i in bl]
            for kvh in range(4):
                ps, pb = psA.get()
                for q in range(8):
                    S.pe(lambda e, ps=ps, q=q, kvh=kvh, ut=ut, n=n: e.matmul(
                        out=ps[:, 0:n], lhsT=V(WKd, (q * 4 + kvh) * 128, [[1, 128]]), rhs=ut[:, q, 0:n],
                        start=(q == 0), stop=(q == 7)), reads=[utb, b_wk], writes=[pb])
                qk_post(H, ps, pb, n, 4, cs, csb, KT_all[:, kvh, 128 + c0:128 + c0 + n], kbufs)
            for sb in range((n + 127) // 128):
                nn = min(128, n - sb * 128)
                blk = bl[sb]
                ps, pb = psA.get()
                for q in range(8):
                    S.pe(lambda e, ps=ps, q=q, ut=ut, sb=sb, nn=nn: e.matmul(
                        out=ps[0:nn, 0:256], lhsT=ut[:, q, sb * 128:sb * 128 + nn], rhs=WVt[:, q, :],
                        start=(q == 0), stop=(q == 7)), reads=[utb, b_wv], writes=[pb])
                evac_copy(V_all[0:nn, 1 + blk, :], ps[0:nn, 0:256], [pb], [b_vv[1 + blk]])

        kl, b_kl = TB("kl", [128, 4, 64], F32)
        vl, b_vl = TB("vl", [128, 256], F32)
        ps, pb = psA.get()
        psb = ps[:, :].bitcast(BF16)
        for kvh in range(4):
            S.pe(lambda e, psb=psb, kvh=kvh: e.transpose(
                out=psb[:, kvh * 128:(kvh + 1) * 128], in_=KT_all[:, kvh, 128 + TP - 128:128 + TP], identity=ident_b[:, :]),
                reads=[b_kt[16], B_identb], writes=[pb])
        S.dve(lambda e, psb=psb: e.tensor_copy(out=kl[:, :, :], in_=bass.AP(tensor=psb.tensor, offset=0,
                                                                           ap=[[1024, 128], [128, 4], [1, 64]])),
              reads=[pb], writes=[b_kl])
        S.dma("pool", k_last, V(kl, 0, [[1, 256]]), reads=[b_kl])
        S.act(lambda e: e.copy(out=vl[:, :], in_=V_all[:, 16, :]), reads=[b_vv[16]], writes=[b_vl])
        S.dma("pool", v_last, vl[:, :], reads=[b_vl])
        kn, b_kn = TB("kn", [TS, 4, 64], F32)
        vn, b_vn = TB("vn", [TS, 256], F32)
        ps, pb = psA.get()
        psb = ps[:, :].bitcast(BF16)
        for kvh in range(4):
            S.pe(lambda e, psb=psb, kvh=kvh: e.transpose(
                out=psb[0:TS, kvh * 128:(kvh + 1) * 128], in_=KT_all[:, kvh, 128 + TP:128 + TP + TS], identity=ident_b[:, :]),
                reads=[b_kt[17], B_identb], writes=[pb])
        S.dve(lambda e, psb=psb: e.tensor_copy(out=kn[:, :, :], in_=bass.AP(tensor=psb.tensor, offset=0,
                                                                           ap=[[1024, TS], [128, 4], [1, 64]])),
              reads=[pb], writes=[b_kn])
        S.act(lambda e: e.copy(out=vn[:, :], in_=V_all[0:TS, 17, :]), reads=[b_vv[17]], writes=[b_vn])
        S.dma("pool", ks_out[:, 127, :], V(kn, 0, [[1, 256]], npart=TS), reads=[b_kn], writes=[b_kso])
        S.dma("pool", vs_out[:, 127, :], vn[:, :], reads=[b_vn], writes=[b_vso])
        b_xin, b_xout = Buf(), Buf()
        xin_b = kvx_in.ap().bitcast(BF16)
        for kvh in range(4):
            S.dma("pool", xin_b[:, kvh * 128:(kvh + 1) * 128], KT_all[:, kvh, 128 + TP - 128:128 + TP],
                  reads=[b_kt[16]], writes=[b_xin])
        S.dma("pool", xin_b[:, 512:768], V_all[:, 16, :], reads=[b_vv[16]], writes=[b_xin])
        S.cc(lambda e: e.collective_compute("AllGather", ALU.bypass, replica_groups=[[0, 1], [2, 3], [4, 5], [6, 7]],
                                            ins=[kvx_in.ap().opt()], outs=[kvx_out.ap().opt()]),
             reads=[b_xin], writes=[b_xout])
        kc_r = Rot(AR, "kc", [128, 256], F32, 2)
        kc2_r = Rot(AR, "kc2", [128, 4, 2, 64], BF16, 2)
        for b in range(TS):
            kc, kcb = kc_r.get()
            S.dma("sp", kc[:, :], ks_out[b], reads=[b_kso], writes=[kcb])
            kc2, kc2b = kc2_r.get()
            S.dve(lambda e, kc=kc, kc2=kc2: e.tensor_copy(out=kc2[:, :, :, :], in_=V(kc, 0, [[64, 4], [0, 2], [1, 64]])),
                  reads=[kcb], writes=[kc2b])
            ps, pb = psA.get()
            psb = ps[:, :].bitcast(BF16)
            for kvh in range(4):
                S.pe(lambda e, psb=psb, kvh=kvh, kc2=kc2: e.transpose(
                    out=psb[:, kvh * 128:(kvh + 1) * 128], in_=V(kc2, kvh * 128, [[1, 128]]), identity=ident_b[:, :]),
                    reads=[kc2b, B_identb], writes=[pb])
            evac_copy(V(KTs, b * 512, [[1, 512]]), psb[:, 0:512], [pb], [b_kts])
            S.dma("pool", Vs[:, b, :], vs_out[b], reads=[b_vso], writes=[b_vs2])
        xout_b = kvx_out.ap().bitcast(BF16)
        for kvh in range(4):
            S.dma("sp", KT_all[:, kvh, 0:128], xout_b[0:128, kvh * 128:(kvh + 1) * 128], reads=[b_xout], writes=[b_kt[0]])
        S.dma("sp", V_all[:, 0, :], xout_b[0:128, 512:768], reads=[b_xout], writes=[b_vv[0]])
        S.dma("pool", KT_d, KT_all[:, :, :], reads=b_kt, writes=[b_kvd])
        S.dma("pool", V_d, V_all[:, :, :], reads=b_vv, writes=[b_kvd])
        S.dma("pool", KTs_d, KTs[:, :, :, :], reads=[b_kts], writes=[b_kvd])
        S.dma("pool", Vs_d, Vs[:, :, :], reads=[b_vs2], writes=[b_kvd])
        phase_end(m_kv)

    def attn_w_alloc(which):
        w = {}
        for nm in which:
            w[nm] = (AR.alloc("W%st" % nm, [128, 8, D], BF16), [Buf(), Buf()])
        return w

    def attn_w_load(j, w):
        for nm, srcd in (("Q", w_q_d), ("O", w_o_d)):
            if nm in w and not w.get(nm + "_loaded"):
                t, bs = w[nm]
                for hf in range(2):
                    S.dma("pool", t[:, :, hf * 512:(hf + 1) * 512],
                          srcd[j, :, hf * 512:(hf + 1) * 512].rearrange("(q p) f -> p q f", p=128), writes=[bs[hf]])
                w[nm + "_loaded"] = True

    def attn_layer(j, w=None, hook=None):
        if True:
            layer = 2 + j
            m_at = AR.mark()
            w = dict(w) if w is not None else {}
            H = QK()
            KT_all = AR.alloc("KT_all", [128, 4, 128 + NT], BF16)
            V_all = AR.alloc("V_all", [128, 18, 256], BF16)
            KTs = AR.alloc("KTs", [128, TS, 4, 128], BF16)
            Vs = AR.alloc("Vs", [128, TS, 256], BF16)
            b_kvs = Buf()
            b_kt = [b_kvs] * 18
            b_vv = [b_kvs] * 18
            b_kts = b_vs2 = b_kvs
            S.dma("sp", KT_all[:, :, :], KT_d, reads=[b_kvd], writes=[b_kvs])
            S.dma("sp", V_all[:, :, :], V_d, reads=[b_kvd], writes=[b_kvs])
            S.dma("sp", KTs[:, :, :, :], KTs_d, reads=[b_kvd], writes=[b_kvs])
            S.dma("sp", Vs[:, :, :], Vs_d, reads=[b_kvd], writes=[b_kvs])
            missing = [nm for nm in ("Q", "O") if nm not in w]
            w.update(attn_w_alloc(missing))
            attn_w_load(j, w)
            WQt, b_wq = w["Q"]
            WOt, b_wo = w["O"]
            if hook is not None:
                hook()
            qt_r = Rot(AR, "qt", [128, 8, TW], BF16, 2)
            ot_r = Rot(AR, "ot", [128, 8, TW], BF16, 2)
            pt_r = Rot(AR, "pt", [128, 256], BF16, 12)
            ln_r = Rot(AR, "lnr", [128, 128], F32, 4)
            for ti, tile in enumerate(TILES):
                c0, n, bl = tile
                xt, xb = load_xt(tile)
                ut, utb = rmsnorm(xt, xb, n, 8 * layer, lnexp=True)
                cs, csb = load_cs(H, tile)
                qt, qtb = qt_r.get()
                pss = {}

                def projq(c):
                    ps, pb = psA.get()
                    for q in range(8):
                        S.pe(lambda e, ps=ps, q=q, c=c, ut=ut, n=n: e.matmul(
                            out=ps[:, 0:n], lhsT=WQt[:, q, c * 128:(c + 1) * 128], rhs=ut[:, q, 0:n],
                            start=(q == 0), stop=(q == 7)), reads=[utb, b_wq[c // 4]], writes=[pb])
                    pss[c] = (ps, pb)

                projq(0)
                projq(1)
                for c in range(8):
                    ps, pb = pss.pop(c)
                    qk_post(H, ps, pb, n, 2 * j, cs, csb, qt[:, c, 0:n], [qtb])
                    if c + 2 < 8:
                        projq(c + 2)
                ot, otb = ot_r.get()
                if ti < 8:
                    def stage_a(sb, c):
                        qb = 2 * ti + sb
                        MASK = MASK_F if qb == 0 else MASK_R
                        kvh = c // 2
                        pts = []
                        for hh in range(2):
                            psS, pbS = psA.get()
                            for kt in range(2):
                                S.pe(lambda e, psS=psS, hh=hh, kt=kt, kvh=kvh, qb=qb, c=c, qt=qt, sb=sb: e.matmul(
                                    out=psS[:, kt * 128:(kt + 1) * 128],
                                    lhsT=KT_all[hh * 64:(hh + 1) * 64, kvh, (qb + kt) * 128:(qb + kt + 1) * 128],
                                    rhs=qt[hh * 64:(hh + 1) * 64, c, sb * 128:(sb + 1) * 128], start=True, stop=True),
                                    reads=[b_kt[qb + kt], qtb], writes=[pbS])
                            pt, ptb = pt_r.get()
                            S.act(lambda e, psS=psS, pt=pt: e.activation(out=pt[:, :], in_=psS[:, 0:256], func=AF.Exp,
                                                                         scale=0.125), reads=[pbS], writes=[ptb])
                            (S.pool if hh == 0 else S.dve)(
                                lambda e, pt=pt, MASK=MASK: e.tensor_tensor(out=pt[:, :], in0=pt[:, :], in1=MASK[:, :],
                                                                            op=ALU.mult), reads=[ptb, b_c3], writes=[ptb])
                            pts.append((pt, ptb))
                        return pts

                    def stage_b(sb, c, pts):
                        qb = 2 * ti + sb
                        kvh = c // 2
                        psO, pbO = psA.get()
                        for part in range(2):
                            for hh in range(2):
                                pt, ptb = pts[hh]
                                for kt in range(2):
                                    if part == 0:
                                        lhs = V_all[:, qb + kt, kvh * 64:(kvh + 1) * 64]
                                        rd = [b_vv[qb + kt], ptb]
                                    else:
                                        lhs = ones_b[:, 0:64]
                                        rd = [B_ones, ptb]
                                    S.pe(lambda e, psO=psO, hh=hh, kt=kt, lhs=lhs, pt=pt, part=part: e.matmul(
                                        out=psO[hh * 64:(hh + 1) * 64, part * 128:(part + 1) * 128], lhsT=lhs,
                                        rhs=pt[:, kt * 128:(kt + 1) * 128], start=(kt == 0), stop=(kt == 1)),
                                        reads=rd, writes=[pbO])
                        ln, lnb = ln_r.get()
                        S.act(lambda e, psO=psO, ln=ln, c=c, j=j: e.activation(
                            out=ln[:, :], in_=psO[:, 128:256], func=AF.Ln, bias=ESINK[:, j * 8 + c:j * 8 + c + 1]),
                            reads=[pbO, b_c3], writes=[lnb])
                        S.act(lambda e, ln=ln: e.activation(out=ln[:, :], in_=ln[:, :], func=AF.Exp, scale=-1.0),
                              reads=[lnb], writes=[lnb])
                        S.dve(lambda e, psO=psO, ln=ln, ot=ot, c=c, sb=sb: e.tensor_tensor(
                            out=ot[:, c, sb * 128:(sb + 1) * 128], in0=psO[:, 0:128], in1=ln[:, :], op=ALU.mult),
                            reads=[pbO, lnb], writes=[otb])

                    its = [(sb, c) for sb in range(2) for c in range(8)]
                    pend = []
                    SKEW = 2
                    for it in range(len(its) + SKEW):
                        if it < len(its):
                            pend.append(stage_a(*its[it]))
                        if it >= SKEW:
                            stage_b(*its[it - SKEW], pend[it - SKEW])
                else:
                    pts = []
                    for hh in range(2):
                        psS, pbS = psA.get()
                        for b in range(TS):
                            for kvh in range(4):
                                S.pe(lambda e, psS=psS, hh=hh, b=b, kvh=kvh, qt=qt: e.matmul(
                                    out=psS[:, (b * 4 + kvh) * 2:(b * 4 + kvh) * 2 + 2],
                                    lhsT=V(KTs, (b * 4 + kvh) * 128, [[1, 128]], p0=hh * 64, npart=64),
                                    rhs=V(qt, (2 * kvh) * TW + b, [[TW, 2]], p0=hh * 64, npart=64), start=True, stop=True),
                                    reads=[b_kts, qtb], writes=[pbS])
                        pt, ptb = pt_r.get()
                        S.act(lambda e, psS=psS, pt=pt: e.activation(out=pt[:, 0:128], in_=psS[:, 0:128], func=AF.Exp,
                                                                     scale=0.125), reads=[pbS], writes=[ptb])
                        pts.append((pt, ptb))
                    psO, pbO = psA.get()
                    for part in range(2):
                        for hh in range(2):
                            pt, ptb = pts[hh]
                            for b in range(TS):
                                for kvh in range(4):
                                    col = (b * 4 + kvh) * 2
                                    if part == 0:
                                        lhs = Vs[:, b, kvh * 64:(kvh + 1) * 64]
                                        rd = [b_vs2, ptb]
                                    else:
                                        lhs = ones_b[:, 0:64]
                                        rd = [B_ones, ptb]
                                    S.pe(lambda e, psO=psO, hh=hh, lhs=lhs, pt=pt, part=part, col=col: e.matmul(
                                        out=psO[hh * 64:(hh + 1) * 64, part * 128 + col:part * 128 + col + 2], lhsT=lhs,
                                        rhs=pt[:, col:col + 2], start=True, stop=True), reads=rd, writes=[pbO])
                    ln, lnb = ln_r.get()
                    S.dve(lambda e, psO=psO, ln=ln, j=j: e.tensor_tensor(
                        out=V(ln, 0, [[8, TS], [1, 8]]), in0=V(psO, 128, [[8, TS], [1, 8]]),
                        in1=V(ESINK, j * 8, [[0, TS], [1, 8]]), op=ALU.add),
                        reads=[pbO, b_c3], writes=[lnb])
                    S.act(lambda e, ln=ln: e.activation(out=ln[:, :], in_=ln[:, :], func=AF.Ln), reads=[lnb], writes=[lnb])
                    S.act(lambda e, ln=ln: e.activation(out=ln[:, :], in_=ln[:, :], func=AF.Exp, scale=-1.0),
                          reads=[lnb], writes=[lnb])
                    S.dve(lambda e, psO=psO, ln=ln, ot=ot: e.tensor_tensor(
                        out=V(ot, 0, [[1, TS], [TW, 8]]), in0=V(psO, 0, [[8, TS], [1, 8]]),
                        in1=V(ln, 0, [[8, TS], [1, 8]]), op=ALU.mult), reads=[pbO, lnb], writes=[otb])
                for d2 in range(4):
                    ps, pb = psA.get()
                    for k in range(2):
                        dm = d2 * 2 + k
                        for c in range(8):
                            S.pe(lambda e, ps=ps, k=k, dm=dm, c=c, ot=ot, n=n: e.matmul(
                                out=ps[:, k * 256:k * 256 + n], lhsT=WOt[:, c, dm * 128:(dm + 1) * 128],
                                rhs=ot[:, c, 0:n], start=(c == 0), stop=(c == 7)),
                                reads=[otb, b_wo[dm // 4]], writes=[pb])
                    psv = V(ps, 0, [[256, 2], [1, n]])
                    S.dve(lambda e, xt=xt, d2=d2, psv=psv, n=n: e.tensor_tensor(
                        out=xt[:, 2 * d2:2 * d2 + 2, 0:n], in0=psv, in1=xt[:, 2 * d2:2 * d2 + 2, 0:n], op=ALU.add),
                        reads=[pb, xb], writes=[xb])
                store_xt(tile, xt, xb)
            phase_end(m_at)

    def mlp_alloc(win0=None):
        if win0 is None:
            win0 = (AR.alloc("win0", [128, 8, 2048], BF16), [Buf() for _ in range(4)])
        win_t = [win0[0], AR.alloc("win1", [128, 8, 2048], BF16)]
        wout_t = [AR.alloc("wout%d" % i, [128, 16, 1024], BF16) for i in range(2)]
        win_b = [win0[1], [Buf() for _ in range(4)]]
        wout_b = [[Buf() for _ in range(4)] for _ in range(2)]
        return win_t, wout_t, win_b, wout_b

    def win0_load(l, win0):
        t, bs = win0
        for j in range(4):
            S.dma("pool", t[:, :, j * 512:(j + 1) * 512],
                  w_mlp_in[l, :, j * 512:(j + 1) * 512].rearrange("(q p) f -> p q f", p=128), writes=[bs[j]])

    def mlp_weights(l, pre=None, lazy=None, skip_win0=False):
        win_t, wout_t, win_b, wout_b = pre if pre is not None else mlp_alloc()

        class _Q:
            def dma(self, *a, **k):
                if lazy is None:
                    S.dma(*a, **k)
                else:
                    lazy.append(lambda a=a, k=k: S.dma(*a, **k))
        Sx = _Q()
        for hf in range(2):
            for j in range(4):
                if hf == 0 and skip_win0:
                    continue
                Sx.dma("pool", win_t[hf][:, :, j * 512:(j + 1) * 512],
                      w_mlp_in[l, :, hf * 2048 + j * 512: hf * 2048 + (j + 1) * 512].rearrange("(q p) f -> p q f", p=128),
                      writes=[win_b[hf][j]])
            for j in range(4):
                Sx.dma("pool", wout_t[hf][:, j * 4:(j + 1) * 4, :],
                      w_mlp_out[l, hf * 2048 + j * 512: hf * 2048 + (j + 1) * 512, :].rearrange("(f p) d -> p f d", p=128),
                      writes=[wout_b[hf][j]])
        return win_t, wout_t, win_b, wout_b

    def mlp(l, pre=None, rest=(), win0=None, hook=None):
        m_mlp = AR.mark()
        if pre is None and win0 is not None:
            pre = mlp_weights(l, mlp_alloc(win0), None, True)
        win_t, wout_t, win_b, wout_b = pre if pre is not None else mlp_weights(l)
        if hook is not None:
            hook()
        for fn in rest:
            fn()
        hT_r = Rot(AR, "hT", [128, 16, TW], BF16, 2)
        rl_r = Rot(AR, "rl", [128, 512], F32, 2)
        if l == 3 and FUSE_IO:
            yo_r = Rot(AR, "yo", [128, D], F32, 2)
        for hf in range(2):
            for tile in TILES:
                c0, n, bl = tile
                xt, xb = load_xt(tile)
                if hf == 0:
                    ut, utb = rmsnorm(xt, xb, n, 32 + 8 * l)
                    S.dma("pool", UT_d[:, :, c0:c0 + n], ut[:, :, 0:n], reads=[utb], writes=[UT_b[i] for i in bl])
                else:
                    ut, utb = ut_r.get()
                    S.dma("sp", ut[:, :, 0:n], UT_d[:, :, c0:c0 + n], reads=[UT_b[i] for i in bl], writes=[utb])
                hT, hb = hT_r.get()
                for f2 in range(8):
                    ps, pb = psA.get()
                    for k in range(2):
                        fc = f2 * 2 + k
                        for q in range(8):
                            S.pe(lambda e, ps=ps, k=k, fc=fc, q=q, hf=hf, ut=ut, n=n: e.matmul(
                                out=ps[:, k * 256:k * 256 + n], lhsT=win_t[hf][:, q, fc * 128:(fc + 1) * 128],
                                rhs=ut[:, q, 0:n], start=(q == 0), stop=(q == 7)),
                                reads=[utb, win_b[hf][fc // 4]], writes=[pb])
                    rl, rlb = rl_r.get()
                    psv = V(ps, 0, [[256, 2], [1, n]])
                    rlv = V(rl, 0, [[256, 2], [1, n]])
                    S.act(lambda e, rlv=rlv, psv=psv: e.activation(out=rlv, in_=psv, func=AF.Relu), reads=[pb], writes=[rlb])
                    S.dve(lambda e, hT=hT, f2=f2, rlv=rlv, n=n: e.tensor_tensor(
                        out=hT[:, 2 * f2:2 * f2 + 2, 0:n], in0=rlv, in1=rlv, op=ALU.mult), reads=[rlb], writes=[hb])
                for d2 in range(4):
                    ps, pb = psA.get()
                    for k in range(2):
                        dm = d2 * 2 + k
                        for fc in range(16):
                            S.pe(lambda e, ps=ps, k=k, dm=dm, fc=fc, hf=hf, hT=hT, n=n: e.matmul(
                                out=ps[:, k * 256:k * 256 + n], lhsT=wout_t[hf][:, fc, dm * 128:(dm + 1) * 128],
                                rhs=hT[:, fc, 0:n], start=(fc == 0), stop=(fc == 15)),
                                reads=[hb, wout_b[hf][fc // 4]], writes=[pb])
                    psv = V(ps, 0, [[256, 2], [1, n]])
                    S.dve(lambda e, xt=xt, d2=d2, psv=psv, n=n: e.tensor_tensor(
                        out=xt[:, 2 * d2:2 * d2 + 2, 0:n], in0=psv, in1=xt[:, 2 * d2:2 * d2 + 2, 0:n], op=ALU.add),
                        reads=[pb, xb], writes=[xb])
                if l == 3 and hf == 1 and FUSE_IO:
                    for sb, blk in enumerate(bl):
                        nn = blk_rows(blk)
                        yt, yb = yo_r.get()
                        for half in range(2):
                            ps, pb = psA.get()
                            for qq in range(4):
                                q = half * 4 + qq
                                S.pe(lambda e, ps=ps, xt=xt, q=q, qq=qq, nn=nn, sb=sb: e.transpose(
                                    out=ps[0:nn, qq * 128:(qq + 1) * 128], in_=xt[:, q, sb * 128:sb * 128 + nn],
                                    identity=ident_f[:, :]), reads=[xb, B_identf], writes=[pb])
                            evac_copy(yt[0:nn, half * 512:(half + 1) * 512], ps[0:nn, :], [pb], [yb], "act")
                        dst = y_p[blk * 128:(blk + 1) * 128, :] if blk < 16 else y_s[:, :]
                        S.dma("pool", dst, yt[0:nn, :], reads=[yb])
                else:
                    store_xt(tile, xt, xb)
        phase_end(m_mlp)

    try:
        for l in range(4):
            if l < 2 and on("s5_%d" % l):
                s5_layer(l)
            pre = None
            rest = ()
            m_pre = AR.mark()
            if l < 2 and on("glu%d" % l):
                hook = None
                if on("mlp%d" % l):
                    pre = mlp_alloc()
                    def hook(l=l, pre=pre):
                        lz = []
                        mlp_weights(l, pre, lz)
                        return lz
                rest = glu(l, hook)
            if l >= 2 and ALL:
                if l == 2:
                    win0 = (AR.alloc("win0p", [128, 8, 2048], BF16), [Buf() for _ in range(4)])
                    m_l2 = AR.mark()
                    w0 = attn_w_alloc(["Q", "O"])
                    kv_phase(hook=lambda: attn_w_load(0, w0))
                    attn_layer(0, w0, hook=lambda: win0_load(2, win0))
                    AR.release(m_l2)
                    S.barrier()
                    w1 = attn_w_alloc(["Q"])
                    mlp(2, win0=win0, hook=lambda: attn_w_load(1, w1))
                else:
                    attn_layer(1, w1, hook=lambda: win0_load(3, win0))
                    AR.release(m_l2)
                    S.barrier()
                    mlp(3, win0=win0)
                continue
            if l == 2 and on("kv"):
                kv_phase()
            if l >= 2 and on("attn%d" % (l - 2)):
                attn_layer(l - 2)
            if on("mlp%d" % l):
                mlp(l, pre, rest)
            if pre is not None:
                phase_end(m_pre)
    except StopBuild:
        S.emit()
        return nc

    xfin = Rot(AR, "xfin", [128, 8, 128], F32, 2)
    yout = Rot(AR, "yout", [128, D], F32, 2)
    for b in range(0 if FUSE_IO else 17):
        n = blk_rows(b)
        t, tb = xfin.get()
        S.dma("sp", t[:, :, 0:n], XT_d[:, :, b * 128:b * 128 + n], reads=[XT_b[b]], writes=[tb])
        yt, yb = yout.get()
        for half in range(2):
            ps, pb = psA.get()
            for qq in range(4):
                q = half * 4 + qq
                S.pe(lambda e, ps=ps, t=t, q=q, qq=qq, n=n: e.transpose(
                    out=ps[0:n, qq * 128:(qq + 1) * 128], in_=t[:, q, 0:n],
                    identity=ident_f[:, :]), reads=[tb, B_identf], writes=[pb])
            evac_copy(yt[0:n, half * 512:(half + 1) * 512], ps[0:n, :], [pb], [yb])
        dst = y_p[b * 128:(b + 1) * 128, :] if b < 16 else y_s[:, :]
        S.dma("pool", dst, yt[0:n, :], reads=[yb])

    S.emit()
    return nc


def make_in_maps(inputs):
    f = lambda k: np.ascontiguousarray(inputs[k], dtype=np.float32)
    x_prompt = f("x_prompt")
    x_sample = f("x_sample")
    ident = np.eye(128, dtype=np.float32)
    shared = {k: f(k) for k in ("norm_mix", "norm_mlp", "norm_kv", "b_glu", "w_mlp_in", "w_mlp_out",
                                "ssm_a_re", "ssm_a_im", "ssm_log_dt", "ssm_b_re", "ssm_b_im",
                                "ssm_c_re", "ssm_c_im", "ssm_d", "w_glu", "w_k", "w_v", "w_q", "w_o")}
    ev = np.array([7, 6, 5, 4, 3, 2, 1, 0] + list(range(1, 9)) + [-k for k in range(1, 9)] + [64], np.float32)
    shared["ev"] = np.ascontiguousarray(np.broadcast_to(ev[None, :], (128, 25)))
    jj = np.arange(128) // 16
    shared["cmask"] = (jj[:, None] <= jj[None, :]).astype(np.float32)
    hd = np.arange(128) % 64
    hd_sw = (hd + 32) % 64
    qn = f("q_norm")
    kn = f("k_norm")
    shared["gqk"] = np.ascontiguousarray(np.stack([qn[0][hd], qn[0][hd_sw], qn[1][hd], qn[1][hd_sw],
                                                   kn[hd], kn[hd_sw]], axis=1))
    sk = f("attn_sinks")
    half = (np.arange(128) >= 64).astype(np.int64)
    sinks = np.zeros((128, 16), np.float32)
    for j in range(2):
        for c in range(8):
            sinks[:, j * 8 + c] = sk[j][2 * c + half]
    shared["sinks"] = sinks
    rot = np.zeros((128, 128), np.float32)
    for m in range(128):
        if m % 64 < 32:
            rot[m + 32, m] = -1.0
        else:
            rot[m - 32, m] = 1.0
    shared["rot"] = rot
    blk = np.arange(128) // 64
    shared["bones"] = (blk[:, None] == blk[None, :]).astype(np.float32)
    kj = np.arange(128)[:, None]
    qi = np.arange(128)[None, :]
    m_prev = (kj > qi).astype(np.float32)
    m_own = (kj <= qi).astype(np.float32)
    m_none = np.zeros((128, 128), np.float32)
    inv = (np.float32(10000.0) ** (-np.arange(32, dtype=np.float32) / np.float32(32))).astype(np.float32)
    fidx = (np.arange(128) % 64) % 32
    st_re = f("state_ssm_re").reshape(2, 128, 4096)
    st_im = f("state_ssm_im").reshape(2, 128, 4096)
    ck = f("cache_k").reshape(128, 128, 256)
    cv = f("cache_v").reshape(128, 128, 256)
    in_maps = []
    for c in range(NCORES):
        b, h = c // 2, c % 2
        pos = np.concatenate([np.arange(h * TP, (h + 1) * TP), np.full(TS, 8192)]).astype(np.float32)
        ang = (pos[None, :] * inv[fidx][:, None]).astype(np.float32)
        m = {
            "xp": np.ascontiguousarray(x_prompt[b, h * TP:(h + 1) * TP, :]),
            "xs": np.ascontiguousarray(x_sample[c * TS:(c + 1) * TS, 0, :]),
            "ident": ident,
            "flag": np.full((128, 1), float(h), np.float32),
            "st_re": np.ascontiguousarray(st_re[:, c * TS:(c + 1) * TS]),
            "st_im": np.ascontiguousarray(st_im[:, c * TS:(c + 1) * TS]),
            "masks": np.ascontiguousarray(np.stack([m_prev, m_own, m_prev if h == 1 else m_none])),
            "cos_t": np.cos(ang.astype(np.float64)).astype(np.float32),
            "sin_t": np.sin(ang.astype(np.float64)).astype(np.float32),
            "cache_k": np.ascontiguousarray(ck[c * TS:(c + 1) * TS]),
            "cache_v": np.ascontiguousarray(cv[c * TS:(c + 1) * TS]),
        }
        m.update(shared)
        in_maps.append(m)
    return in_maps


def kernel(_stages=("all",), **inputs):
    nc = build(_stages)
    in_maps = make_in_maps(inputs)
    res = run_bass_kernel_spmd(nc, in_maps, core_ids=list(range(NCORES)))
    r = res.results
    if any(s.startswith("dbg_") for s in _stages):
        return r
    y_prompt = np.zeros((4, 4096, D), np.float32)
    y_sample = np.zeros((128, 1, D), np.float32)
    sp_re = np.zeros((2, 4, 64, 64), np.float32)
    sp_im = np.zeros((2, 4, 64, 64), np.float32)
    ss_re = np.zeros((2, 128, 64, 64), np.float32)
    ss_im = np.zeros((2, 128, 64, 64), np.float32)
    kp = np.zeros((4, 128, 4, 64), np.float32)
    vp = np.zeros((4, 128, 4, 64), np.float32)
    ks = np.zeros((128, 128, 4, 64), np.float32)
    vs = np.zeros((128, 128, 4, 64), np.float32)
    for c in range(NCORES):
        b, h = c // 2, c % 2
        rc = r[c]
        y_prompt[b, h * TP:(h + 1) * TP] = rc["y_p"]
        y_sample[c * TS:(c + 1) * TS, 0] = rc["y_s"]
        if "sp_state" in rc:
            if h == 1:
                sp = rc["sp_state"]
                sp_re[:, b] = sp[:, 0:32].reshape(2, 64, 64)
                sp_im[:, b] = sp[:, 32:64].reshape(2, 64, 64)
            ss = rc["ss_state"]
            ss_re[:, c * TS:(c + 1) * TS] = ss[:, 0].reshape(2, TS, 64, 64)
            ss_im[:, c * TS:(c + 1) * TS] = ss[:, 1].reshape(2, TS, 64, 64)
        if "k_last" in rc:
            if h == 1:
                kp[b] = rc["k_last"].reshape(128, 4, 64)
                vp[b] = rc["v_last"].reshape(128, 4, 64)
            ks[c * TS:(c + 1) * TS] = rc["ks_out"].reshape(TS, 128, 4, 64)
            vs[c * TS:(c + 1) * TS] = rc["vs_out"].reshape(TS, 128, 4, 64)
    return y_prompt, y_sample, sp_re, sp_im, kp, vp, ss_re, ss_im, ks, vs
```

```python
import math
import numpy as np
import concourse.bass as bass
import concourse.mybir as mybir
from concourse.bass_utils import run_bass_kernel_spmd

F32 = mybir.dt.float32
BF16 = mybir.dt.bfloat16
I32 = mybir.dt.int32
AF = mybir.ActivationFunctionType
ALU = mybir.AluOpType

ENGS = ("pe", "act", "dve", "pool", "sp")

NCORES = 8
TP = 2048
TS = 16
NT = TP + TS
D = 1024
DFF = 4096
EPS = 1e-6


class Buf:
    __slots__ = ("w", "r", "name", "excl")

    def __init__(self, name="", excl=False):
        self.w = None
        self.r = {}
        self.name = name
        self.excl = excl


class Op:
    __slots__ = ("eng", "fn", "deps", "dma", "sig", "idx", "dsem", "dval")

    def __init__(self, eng, fn, dma):
        self.eng = eng
        self.fn = fn
        self.deps = []
        self.dma = dma
        self.sig = False
        self.idx = None
        self.dsem = None
        self.dval = None


class Sched:
    NDS = 24

    def __init__(self, nc):
        self.nc = nc
        self.ops = {e: [] for e in ENGS}
        self.ndma = {e: 0 for e in ENGS}
        self.bar = Buf("barrier")
        self.bar_t = None

    def barrier(self):
        if self.bar_t is None:
            self.bar_t = self.nc.alloc_sbuf_tensor_at("bar_t", [128, 8], F32, offset=SB_BASE)
        t = self.bar_t
        self.add("pool", lambda e: e.memset(t[:, :], 0.0), writes=[self.bar])

    def add(self, eng, fn, reads=(), writes=(), dma=False):
        op = Op(eng, fn, dma)
        deps = {}
        reads = list(reads) + [self.bar]

        def need(d, kind):
            if d is None:
                return
            if (not d.dma) and (not dma) and d.eng == eng:
                if kind != "raw" or eng == "pe":
                    return
            deps[id(d)] = d

        for b in reads:
            need(b.w, "raw")
            if b.excl:
                for r in b.r.values():
                    need(r, "war")
        for b in writes:
            need(b.w, "waw")
            for r in b.r.values():
                need(r, "war")
        for d in deps.values():
            d.sig = True
            op.deps.append(d)
        for b in writes:
            b.w = op
            b.r = {}
        for b in reads:
            if dma:
                b.r[("dma", id(op))] = op
            else:
                b.r[eng] = op
        if dma:
            op.sig = True
            k = self.ndma[eng]
            self.ndma[eng] += 1
            op.dsem = (eng, k % self.NDS)
            op.dval = 16 * (k // self.NDS + 1)
        self.ops[eng].append(op)
        return op

    def pe(self, fn, reads=(), writes=()):
        return self.add("pe", fn, reads, writes)

    def act(self, fn, reads=(), writes=()):
        return self.add("act", fn, reads, writes)

    def dve(self, fn, reads=(), writes=()):
        return self.add("dve", fn, reads, writes)

    def pool(self, fn, reads=(), writes=()):
        return self.add("pool", fn, reads, writes)

    def cc(self, fn, reads=(), writes=()):
        op = self.add("pool", fn, reads, writes, dma=True)
        self.ndma["pool"] -= 1
        self.ncc = getattr(self, "ncc", 0) + 1
        op.dsem = ("cc", self.ncc)
        op.dval = 1
        return op

    def dma(self, q, out, in_, reads=(), writes=(), **kw):
        return self.add(q, lambda e: e.dma_start(out=out, in_=in_, **kw), reads, writes, dma=True)

    def emit(self):
        nc = self.nc
        for e in ENGS:
            k = 0
            for op in self.ops[e]:
                if op.sig and not op.dma:
                    k += 1
                    op.idx = k
        esem = {e: nc.alloc_semaphore("es_" + e) for e in ENGS}
        dsem = {}
        for e in ENGS:
            for j in range(min(self.NDS, self.ndma[e])):
                dsem[(e, j)] = nc.alloc_semaphore("ds_%s_%d" % (e, j))
        for j in range(1, getattr(self, "ncc", 0) + 1):
            dsem[("cc", j)] = nc.alloc_semaphore("cc_%d" % j)
        handles = {"pe": "tensor", "act": "scalar", "dve": "vector", "pool": "gpsimd", "sp": "sync"}
        with nc.Block() as block:
            for e in ENGS:
                ops = self.ops[e]
                if not ops:
                    continue

                def body(h, e=e, ops=ops):
                    waited = {}

                    def wait(key, sem, val):
                        if waited.get(key, 0) >= val:
                            return
                        waited[key] = val
                        h.wait_ge(sem, val)

                    for op in ops:
                        for d in op.deps:
                            if d.dma:
                                wait(d.dsem, dsem[d.dsem], d.dval)
                            else:
                                wait(d.eng, esem[d.eng], d.idx)
                        if op.dma and op.dsem[0] == "cc":
                            op.fn(h).then_inc(dsem[op.dsem], 1)
                        elif op.dma:
                            if op.dval > 16:
                                wait(op.dsem, dsem[op.dsem], op.dval - 16)
                            op.fn(h).then_inc(dsem[op.dsem], 16)
                        else:
                            ins = op.fn(h)
                            if op.sig:
                                ins.then_inc(esem[e], 1)
                    last = {}
                    for op in ops:
                        if op.dma:
                            last[op.dsem] = op.dval
                    for k, v in last.items():
                        wait(k, dsem[k], v)

                getattr(block, handles[e])(body)


def V(t, off, dims, p0=0, npart=128):
    shp = list(t.shape)
    ps = 1
    for s in shp[1:]:
        ps *= int(s)
    return bass.AP(tensor=t, offset=p0 * ps + off, ap=[[ps, npart]] + [list(d) for d in dims])


SB_BASE = 16512
SB_TOP = 229344
_DT_SIZE = {F32: 4, BF16: 2, I32: 4}


class Arena:
    def __init__(self, nc):
        self.nc = nc
        self.p = SB_BASE + 32
        self.k = 0

    def alloc(self, name, shape, dtype):
        n = _DT_SIZE[dtype]
        for s in shape[1:]:
            n *= int(s)
        off = self.p
        self.p = (off + n + 31) // 32 * 32
        assert self.p <= SB_TOP, "SBUF arena overflow %s: %d > %d" % (name, self.p, SB_TOP)
        self.k += 1
        return self.nc.alloc_sbuf_tensor_at("%s_%d" % (name, self.k), list(shape), dtype, offset=off)

    def mark(self):
        return self.p

    def release(self, m):
        self.p = m


class Rot:
    def __init__(self, nc, name, shape, dtype, n, psum=False):
        self.items = []
        for i in range(n):
            if psum:
                t = nc.alloc_psum_tensor("%s%d" % (name, i), shape, dtype)
            else:
                t = nc.alloc("%s%d" % (name, i), shape, dtype)
            self.items.append((t, Buf("%s%d" % (name, i), excl=psum)))
        self.i = 0

    def get(self):
        it = self.items[self.i % len(self.items)]
        self.i += 1
        return it


TW = 256
TILES = [(i * TW, TW, (2 * i, 2 * i + 1)) for i in range(TP // TW)] + [(TP, TS, (16,))]


def build(stages=("all",)):
    nc = bass.Bass("TRN2", target_bir_lowering=False)
    S = Sched(nc)
    AR = Arena(nc)
    ALL = "all" in stages

    def phase_end(m):
        AR.release(m)
        S.barrier()

    class StopBuild(Exception):
        pass

    def dbg(name, sb_ap, shape, reads, dt=F32):
        if ("dbg_" + name) in stages or "dbg_all" in stages:
            o = dout("dbg_" + name, shape, dt)
            S.dma("sp", o, sb_ap, reads=reads)

    def stop(name):
        if ("stop_" + name) in stages:
            raise StopBuild()

    def on(s):
        return ALL or s in stages

    def din(name, shape, dt=F32):
        return nc.dram_tensor(name, list(shape), dt, kind="ExternalInput").ap()

    def dout(name, shape, dt=F32):
        return nc.dram_tensor(name, list(shape), dt, kind="ExternalOutput").ap()

    xp = din("xp", [TP, D])
    xs = din("xs", [TS, D])
    ident_d = din("ident", [128, 128])
    norm_mix = din("norm_mix", [4, D])
    norm_mlp = din("norm_mlp", [4, D])
    norm_kv = din("norm_kv", [D])
    b_glu = din("b_glu", [2, 2 * D])
    w_mlp_in = din("w_mlp_in", [4, D, DFF])
    w_mlp_out = din("w_mlp_out", [4, DFF, D])
    y_p = dout("y_p", [TP, D])
    y_s = dout("y_s", [TS, D])

    XT_d = nc.dram_tensor("XT_d", [128, 8, NT], F32).ap()
    UT_d = nc.dram_tensor("UT_d", [128, 8, NT], BF16).ap()
    XT_b = [Buf("XT_%d" % i) for i in range(17)]
    UT_b = [Buf("UT_%d" % i) for i in range(17)]

    ident_f = AR.alloc("ident_f", [128, 128], F32)
    ident_b = AR.alloc("ident_b", [128, 128], BF16)
    ones_b = AR.alloc("ones_b", [128, 128], BF16)
    B_identf, B_identb, B_ones = Buf(), Buf(), Buf()
    S.dma("sp", ident_f[:, :], ident_d, writes=[B_identf])
    S.dve(lambda e: e.tensor_copy(out=ident_b[:, :], in_=ident_f[:, :]), reads=[B_identf], writes=[B_identb])
    S.dve(lambda e: e.memset(ones_b[:, :], 1.0), writes=[B_ones])

    psA = Rot(nc, "psA", [128, 512], F32, 8, psum=True)

    vec_in = AR.alloc("vec_in", [104, 128], F32)
    VEC = AR.alloc("VEC", [128, 104], F32)
    B_vecin, B_vec = Buf(), Buf()
    S.dma("sp", vec_in[0:32, :], norm_mix.rearrange("l (q p) -> (l q) p", p=128), writes=[B_vecin])
    S.dma("sp", vec_in[32:64, :], norm_mlp.rearrange("l (q p) -> (l q) p", p=128), writes=[B_vecin])
    S.dma("sp", vec_in[64:72, :], norm_kv.rearrange("(q p) -> q p", p=128), writes=[B_vecin])
    S.dma("sp", vec_in[72:104, :], b_glu.rearrange("l (q p) -> (l q) p", p=128), writes=[B_vecin])
    ps, pb = psA.get()
    S.pe(lambda e, ps=ps: e.transpose(out=ps[:, 0:104], in_=vec_in[:, :], identity=ident_f[0:104, 0:104]),
         reads=[B_vecin, B_identf], writes=[pb])
    S.dve(lambda e, ps=ps: e.tensor_copy(out=VEC[:, :], in_=ps[:, 0:104]), reads=[pb], writes=[B_vec])

    def blk_rows(b):
        return 128 if b < 16 else TS

    evq = [0]

    def evac_copy(out, in_, reads, writes, eng=None):
        evq[0] += 1
        if eng == "act" or (eng is None and evq[0] % 2):
            S.act(lambda e: e.copy(out=out, in_=in_), reads, writes)
        else:
            S.dve(lambda e: e.tensor_copy(out=out, in_=in_), reads, writes)

    FUSE_IO = ALL
    m0 = AR.mark()
    xin = Rot(AR, "xin", [128, D], F32, 2)
    xst = Rot(AR, "xst", [128, 8, 128], F32, 2)
    for b in range(0 if FUSE_IO else 17):
        n = blk_rows(b)
        t, tb = xin.get()
        src = xp[b * 128:(b + 1) * 128, :] if b < 16 else xs[:, :]
        S.dma("sp", t[0:n, :], src, writes=[tb])
        st, stb = xst.get()
        for half in range(2):
            ps, pb = psA.get()
            for qq in range(4):
                q = half * 4 + qq
                S.pe(lambda e, ps=ps, t=t, q=q, qq=qq, n=n: e.transpose(
                    out=ps[:, qq * 128:qq * 128 + n], in_=t[0:n, q * 128:(q + 1) * 128],
                    identity=ident_f[0:n, 0:n]), reads=[tb, B_identf], writes=[pb])
            evac_copy(st[:, half * 4:half * 4 + 4, 0:n],
                      V(ps, 0, [[128, 4], [1, n]]), [pb], [stb])
        S.dma("pool", XT_d[:, :, b * 128:b * 128 + n], st[:, :, 0:n], reads=[stb], writes=[XT_b[b]])

    phase_end(m0)

    xt_r = Rot(AR, "xt", [128, 8, TW], F32, 2)
    ut_r = Rot(AR, "ut", [128, 8, TW], BF16, 2)
    sq_r = Rot(AR, "sq", [128, 8, TW], BF16, 1)
    rs_r = Rot(AR, "rs", [128, TW], F32, 2)
    rstd_r = Rot(AR, "rstd", [128, TW], F32, 2)

    def load_xt(tile):
        c0, n, bl = tile
        t, tb = xt_r.get()
        S.dma("sp", t[:, :, 0:n], XT_d[:, :, c0:c0 + n], reads=[XT_b[i] for i in bl], writes=[tb])
        return t, tb

    def load_xt_input(tile, xin1, b_xin1):
        c0, n, bl = tile
        t, tb = xt_r.get()
        for sb, blk in enumerate(bl):
            nn = blk_rows(blk)
            srcr = xp[blk * 128:(blk + 1) * 128, :] if blk < 16 else xs[:, :]
            S.dma("sp", V(xin1, 0, [[1, D]], npart=nn), srcr, writes=[b_xin1])
            for half in range(2):
                ps, pb = psA.get()
                for qq in range(4):
                    q = half * 4 + qq
                    S.pe(lambda e, ps=ps, q=q, qq=qq, nn=nn: e.transpose(
                        out=ps[:, qq * 128:qq * 128 + nn], in_=V(xin1, q * 128, [[1, 128]], npart=nn),
                        identity=ident_f[0:nn, 0:nn]), reads=[b_xin1, B_identf], writes=[pb])
                evac_copy(t[:, half * 4:half * 4 + 4, sb * 128:sb * 128 + nn],
                          V(ps, 0, [[128, 4], [1, nn]]), [pb], [tb])
        store_xt(tile, t, tb)
        return t, tb

    def store_xt(tile, t, tb):
        c0, n, bl = tile
        S.dma("pool", XT_d[:, :, c0:c0 + n], t[:, :, 0:n], reads=[tb], writes=[XT_b[i] for i in bl])

    def rmsnorm(xt, xb, n, grow, out=None, lnexp=False):
        sq, sqb = sq_r.get()
        S.act(lambda e: e.activation(out=sq[:, :, 0:n], in_=xt[:, :, 0:n], func=AF.Square), reads=[xb], writes=[sqb])
        ps, pb = psA.get()
        for q in range(8):
            S.pe(lambda e, q=q: e.matmul(out=ps[:, 0:n], lhsT=ones_b[:, :], rhs=sq[:, q, 0:n],
                                         start=(q == 0), stop=(q == 7)), reads=[sqb, B_ones], writes=[pb])
        rs, rsb = rs_r.get()
        rstd, rstdb = rstd_r.get()
        if lnexp:
            S.act(lambda e: e.activation(out=rs[:, 0:n], in_=ps[:, 0:n], func=AF.Ln, bias=EPSV[:, 0:1], scale=1.0 / D),
                  reads=[pb, b_c3], writes=[rsb])
            S.act(lambda e: e.activation(out=rstd[:, 0:n], in_=rs[:, 0:n], func=AF.Exp, scale=-0.5),
                  reads=[rsb], writes=[rstdb])
        else:
            S.act(lambda e: e.activation(out=rs[:, 0:n], in_=ps[:, 0:n], func=AF.Sqrt, bias=EPS, scale=1.0 / D),
                  reads=[pb], writes=[rsb])
            S.dve(lambda e: e.reciprocal(out=rstd[:, 0:n], in_=rs[:, 0:n]), reads=[rsb], writes=[rstdb])
        if out is None:
            ut, utb = ut_r.get()
            c0 = 0
        else:
            ut, utb, c0 = out
        for q in range(8):
            S.dve(lambda e, q=q: e.scalar_tensor_tensor(
                out=ut[:, q, c0:c0 + n], in0=xt[:, q, 0:n], scalar=VEC[:, grow + q:grow + q + 1], in1=rstd[:, 0:n],
                op0=ALU.mult, op1=ALU.mult), reads=[xb, rstdb, B_vec], writes=[utb])
        return ut, utb

    a_re_d = din("ssm_a_re", [2, 64, 64])
    a_im_d = din("ssm_a_im", [2, 64, 64])
    ldt_d = din("ssm_log_dt", [2, 64, 64])
    b_re_d = din("ssm_b_re", [2, 64, 64, 16])
    b_im_d = din("ssm_b_im", [2, 64, 64, 16])
    c_re_d = din("ssm_c_re", [2, 64, 16, 64])
    c_im_d = din("ssm_c_im", [2, 64, 16, 64])
    d_d = din("ssm_d", [2, D])
    w_glu_d = din("w_glu", [2, D, 2 * D])
    st_re_d = din("st_re", [2, TS, 4096])
    st_im_d = din("st_im", [2, TS, 4096])
    ev_d = din("ev", [128, 25])
    cmask_d = din("cmask", [128, 128])
    flag_d = din("flag", [128, 1])
    sp_state = dout("sp_state", [2, 64, 128])
    ss_state = dout("ss_state", [2, 2, TS, 4096])
    UG_d = nc.dram_tensor("UG_d", [2, 128, 64, 128], BF16).ap()
    VS_d = nc.dram_tensor("VS_d", [2, 128, 64, 128], BF16).ap()
    cc_in = [nc.dram_tensor("cc_in%d" % i, [128, 64], F32) for i in range(2)]
    cc_out = [nc.dram_tensor("cc_out%d" % i, [256, 64], F32) for i in range(2)]
    UG_b = [Buf(), Buf()]
    VS_b = [Buf(), Buf()]

    EV = AR.alloc("EV", [128, 25], F32)
    CMASK = AR.alloc("CMASK", [128, 128], F32)
    FLAG = AR.alloc("FLAG", [128, 1], F32)
    B_c2 = Buf()
    S.dma("sp", EV[:, :], ev_d, writes=[B_c2])
    S.dma("sp", CMASK[:, :], cmask_d, writes=[B_c2])
    S.dma("sp", FLAG[:, :], flag_d, writes=[B_c2])

    TWO_PI = 2.0 * math.pi
    PI_LO = 3.1415925

    def TB(name, shape, dt):
        return AR.alloc(name, shape, dt), Buf(name)

    def s5_layer(l):
        m_layer = AR.mark()
        WVT = AR.alloc("WVT", [128, 32, 2, 128], BF16)
        WO = AR.alloc("WO", [128, 32, 2, 128], BF16)
        WK = AR.alloc("WK", [128, 64, 128], BF16)
        A8x = AR.alloc("A8x", [128, 64], F32)
        Bm8 = AR.alloc("Bm8", [128, 64], F32)
        A1x = AR.alloc("A1x", [128, 64], F32)
        Bm1 = AR.alloc("Bm1", [128, 64], F32)
        ARx = AR.alloc("ARx", [128, 64], F32)
        BmR = AR.alloc("BmR", [128, 64], F32)
        b_fin = Buf("fin")
        m_prep = AR.mark()
        bp = Buf("prep_small")

        def dv(fn, extra_r=(), extra_w=()):
            S.dve(fn, reads=[bp] + list(extra_r), writes=[bp] + list(extra_w))

        def ac(fn, extra_r=(), extra_w=()):
            S.act(fn, reads=[bp] + list(extra_r), writes=[bp] + list(extra_w))

        par_in = AR.alloc("par_in", [96, 128], F32)
        S.dma("sp", par_in[0:32, :], a_re_d[l].rearrange("(i h) p -> i (h p)", h=2), writes=[bp])
        S.dma("sp", par_in[32:64, :], a_im_d[l].rearrange("(i h) p -> i (h p)", h=2), writes=[bp])
        S.dma("sp", par_in[64:96, :], ldt_d[l].rearrange("(i h) p -> i (h p)", h=2), writes=[bp])
        BRE = AR.alloc("BRE", [128, 32, 16], F32)
        BIM = AR.alloc("BIM", [128, 32, 16], F32)
        BBR = AR.alloc("BBR", [128, 32, 16], F32)
        BBI = AR.alloc("BBI", [128, 32, 16], F32)
        BT = AR.alloc("BT", [128, 32, 16], F32)
        b_bld = Buf("bload")
        S.dma("sp", BRE[:, :, :], b_re_d[l].rearrange("(i h) p c -> (h p) i c", h=2), writes=[b_bld])
        S.dma("sp", BIM[:, :, :], b_im_d[l].rearrange("(i h) p c -> (h p) i c", h=2), writes=[b_bld])
        b_cev = Buf('cev')
        b_cld = Buf("cload")
        CRE = AR.alloc("CRE", [128, 32, 16], F32)
        CIM = AR.alloc("CIM", [128, 32, 16], F32)
        CINs = []
        for (srcd, nm) in ((c_re_d, "cr"), (c_im_d, "ci")):
            CIN = AR.alloc("CIN" + nm, [128, 4, 2, 64], F32)
            sv = srcd[l].rearrange("(i4 i8 h) c p -> i8 c i4 h p", i8=8, h=2)
            for i8 in range(8):
                for hh in range(2):
                    S.dma("sp", CIN[i8 * 16:(i8 + 1) * 16, :, hh, :], sv[i8][:, :, hh, :], writes=[b_cld])
            CINs.append(CIN)
        ps, pb = psA.get()
        S.pe(lambda e, ps=ps: e.transpose(out=ps[:, 0:96], in_=par_in[:, :], identity=ident_f[0:96, 0:96]),
             reads=[bp, B_identf], writes=[pb])
        PAR = AR.alloc("PAR", [128, 96], F32)
        dv(lambda e, ps=ps: e.tensor_copy(out=PAR[:, :], in_=ps[:, 0:96]), [pb])
        for (CIN, dst) in zip(CINs, (CRE, CIM)):
            ps, pb = psA.get()
            for i4 in range(4):
                S.pe(lambda e, ps=ps, i4=i4, CIN=CIN: e.transpose(
                    out=ps[:, i4 * 128:(i4 + 1) * 128], in_=V(CIN, i4 * 128, [[1, 128]]), identity=ident_f[:, :]),
                    reads=[b_cld, B_identf], writes=[pb])
            S.act(lambda e, ps=ps, dst=dst: e.copy(out=V(dst, 0, [[1, 512]]), in_=ps[:, :]), reads=[pb], writes=[b_cev])
        ARE = PAR[:, 0:32]
        AIM = PAR[:, 32:64]
        DT = AR.alloc("DT", [128, 32], F32)
        XR = AR.alloc("XR", [128, 32], F32)
        XI = AR.alloc("XI", [128, 32], F32)
        def exp_taylor(dst, src, deg):
            dv(lambda e: e.tensor_scalar(out=dst, in0=src, scalar1=1.0 / deg, scalar2=1.0, op0=ALU.mult, op1=ALU.add))
            for k in range(deg - 1, 0, -1):
                dv(lambda e, k=k: e.scalar_tensor_tensor(out=dst, in0=dst, scalar=1.0 / k, in1=src,
                                                         op0=ALU.mult, op1=ALU.mult))
                dv(lambda e: e.tensor_scalar(out=dst, in0=dst, scalar1=1.0, scalar2=None, op0=ALU.add))

        NI = AR.alloc("NI", [128, 32], I32)
        NF = AR.alloc("NF", [128, 32], F32)
        RX = AR.alloc("RX", [128, 32], F32)
        I2 = AR.alloc("I2", [128, 32], I32)
        LDT = PAR[:, 64:96]
        dv(lambda e: e.tensor_scalar(out=NI[:, :], in0=LDT, scalar1=1.0 / math.log(2.0), scalar2=None, op0=ALU.mult))
        dv(lambda e: e.tensor_copy(out=NF[:, :], in_=NI[:, :]))
        dv(lambda e: e.scalar_tensor_tensor(out=RX[:, :], in0=NF[:, :], scalar=-0.693359375, in1=LDT,
                                            op0=ALU.mult, op1=ALU.add))
        dv(lambda e: e.scalar_tensor_tensor(out=RX[:, :], in0=NF[:, :], scalar=2.12194440e-4, in1=RX[:, :],
                                            op0=ALU.mult, op1=ALU.add))
        exp_taylor(DT[:, :], RX[:, :], 12)
        dv(lambda e: e.tensor_scalar(out=NI[:, :], in0=NF[:, :], scalar1=127.0, scalar2=None, op0=ALU.add))
        dv(lambda e: e.tensor_scalar(out=I2[:, :], in0=NI[:, :], scalar1=23, scalar2=None, op0=ALU.logical_shift_left))
        dv(lambda e: e.tensor_tensor(out=DT[:, :], in0=DT[:, :], in1=I2[:, :].bitcast(F32), op=ALU.mult))
        dv(lambda e: e.tensor_tensor(out=XR[:, :], in0=ARE, in1=DT[:, :], op=ALU.mult))
        dv(lambda e: e.tensor_tensor(out=XI[:, :], in0=AIM, in1=DT[:, :], op=ALU.mult))
        NE = 25
        ANG = AR.alloc("ANG", [128, NE, 32], F32)
        MAG = AR.alloc("MAG", [128, NE, 32], F32)
        QF = AR.alloc("QF", [128, NE, 32], F32)
        QI = AR.alloc("QI", [128, NE, 32], I32)
        RR = AR.alloc("RR", [128, NE, 32], F32)
        W1 = AR.alloc("W1", [128, NE, 32], F32)
        W2 = AR.alloc("W2", [128, NE, 32], F32)
        LR = AR.alloc("LR", [128, NE, 32], F32)
        LI = AR.alloc("LI", [128, NE, 32], F32)
        ev_b = V(EV, 0, [[1, NE], [0, 32]])
        dv(lambda e: e.tensor_tensor(out=ANG[:, :, :], in0=V(XI, 0, [[0, NE], [1, 32]]), in1=ev_b, op=ALU.mult), [B_c2])
        dv(lambda e: e.tensor_tensor(out=MAG[:, :, :], in0=V(XR, 0, [[0, NE], [1, 32]]), in1=ev_b, op=ALU.mult), [B_c2])
        dv(lambda e: e.tensor_scalar(out=MAG[:, :, :], in0=MAG[:, :, :], scalar1=0.125, scalar2=None, op0=ALU.mult))
        exp_taylor(QF[:, :, :], MAG[:, :, :], 10)
        dv(lambda e: e.tensor_tensor(out=MAG[:, :, :], in0=QF[:, :, :], in1=QF[:, :, :], op=ALU.mult))
        dv(lambda e: e.tensor_tensor(out=QF[:, :, :], in0=MAG[:, :, :], in1=MAG[:, :, :], op=ALU.mult))
        dv(lambda e: e.tensor_tensor(out=MAG[:, :, :], in0=QF[:, :, :], in1=QF[:, :, :], op=ALU.mult))
        dv(lambda e: e.tensor_scalar(out=QI[:, :, :], in0=ANG[:, :, :], scalar1=1.0 / TWO_PI, scalar2=None, op0=ALU.mult))
        dv(lambda e: e.tensor_copy(out=QF[:, :, :], in_=QI[:, :, :]))
        dv(lambda e: e.scalar_tensor_tensor(out=RR[:, :, :], in0=QF[:, :, :], scalar=-TWO_PI, in1=ANG[:, :, :],
                                            op0=ALU.mult, op1=ALU.add))

        def wrap_sin(dst, shift):
            dv(lambda e: e.tensor_scalar(out=W1[:, :, :], in0=RR[:, :, :], scalar1=shift, scalar2=None, op0=ALU.add))
            dv(lambda e: e.tensor_scalar(out=W2[:, :, :], in0=W1[:, :, :], scalar1=math.pi, scalar2=TWO_PI,
                                         op0=ALU.is_gt, op1=ALU.mult))
            dv(lambda e: e.tensor_tensor(out=W1[:, :, :], in0=W1[:, :, :], in1=W2[:, :, :], op=ALU.subtract))
            dv(lambda e: e.tensor_scalar(out=W2[:, :, :], in0=W1[:, :, :], scalar1=-math.pi, scalar2=TWO_PI,
                                         op0=ALU.is_lt, op1=ALU.mult))
            dv(lambda e: e.tensor_tensor(out=W1[:, :, :], in0=W1[:, :, :], in1=W2[:, :, :], op=ALU.add))
            dv(lambda e: e.tensor_scalar(out=W1[:, :, :], in0=W1[:, :, :], scalar1=PI_LO, scalar2=-PI_LO,
                                         op0=ALU.min, op1=ALU.max))
            ac(lambda e: e.activation(out=W2[:, :, :], in_=W1[:, :, :], func=AF.Sin))
            dv(lambda e: e.tensor_tensor(out=dst[:, :, :], in0=W2[:, :, :], in1=MAG[:, :, :], op=ALU.mult))

        wrap_sin(LI, 0.0)
        wrap_sin(LR, math.pi / 2)

        sm = [AR.alloc("sm%d" % i, [128, 32], F32) for i in range(8)]
        LBR = LR[:, 8, :]
        LBI = LI[:, 8, :]
        NR, DEN, U1, U2, FR, FI, U3, U4 = sm
        dv(lambda e: e.tensor_scalar(out=NR[:, :], in0=LBR, scalar1=-1.0, scalar2=None, op0=ALU.add))
        dv(lambda e: e.tensor_tensor(out=U1[:, :], in0=ARE, in1=ARE, op=ALU.mult))
        dv(lambda e: e.tensor_tensor(out=U2[:, :], in0=AIM, in1=AIM, op=ALU.mult))
        dv(lambda e: e.tensor_tensor(out=DEN[:, :], in0=U1[:, :], in1=U2[:, :], op=ALU.add))
        dv(lambda e: e.reciprocal(out=DEN[:, :], in_=DEN[:, :]))
        dv(lambda e: e.tensor_tensor(out=U1[:, :], in0=NR[:, :], in1=ARE, op=ALU.mult))
        dv(lambda e: e.tensor_tensor(out=U2[:, :], in0=LBI, in1=AIM, op=ALU.mult))
        dv(lambda e: e.tensor_tensor(out=U1[:, :], in0=U1[:, :], in1=U2[:, :], op=ALU.add))
        dv(lambda e: e.tensor_tensor(out=FR[:, :], in0=U1[:, :], in1=DEN[:, :], op=ALU.mult))
        dv(lambda e: e.tensor_tensor(out=U3[:, :], in0=LBI, in1=ARE, op=ALU.mult))
        dv(lambda e: e.tensor_tensor(out=U4[:, :], in0=NR[:, :], in1=AIM, op=ALU.mult))
        dv(lambda e: e.tensor_tensor(out=U3[:, :], in0=U3[:, :], in1=U4[:, :], op=ALU.subtract))
        dv(lambda e: e.tensor_tensor(out=FI[:, :], in0=U3[:, :], in1=DEN[:, :], op=ALU.mult))

        for (Ax, Bm, m) in ((A8x, Bm8, 15), (A1x, Bm1, 8), (ARx, BmR, 24)):
            dv(lambda e, Ax=Ax, m=m: e.tensor_copy(out=V(Ax, 0, [[32, 2], [1, 32]]), in_=V(LR, m * 32, [[0, 2], [1, 32]])),
               extra_w=[b_fin])
            dv(lambda e, Bm=Bm, m=m: e.tensor_scalar(out=Bm[:, 0:32], in0=LI[:, m, :], scalar1=-1.0, scalar2=None,
                                                     op0=ALU.mult), extra_w=[b_fin])
            dv(lambda e, Bm=Bm, m=m: e.tensor_copy(out=Bm[:, 32:64], in_=LI[:, m, :]), extra_w=[b_fin])

        fr_b = V(FR, 0, [[1, 32], [0, 16]])
        fi_b = V(FI, 0, [[1, 32], [0, 16]])
        dv(lambda e: e.tensor_tensor(out=BBR[:, :, :], in0=BRE[:, :, :], in1=fr_b, op=ALU.mult), [b_bld])
        dv(lambda e: e.tensor_tensor(out=BT[:, :, :], in0=BIM[:, :, :], in1=fi_b, op=ALU.mult))
        dv(lambda e: e.tensor_tensor(out=BBR[:, :, :], in0=BBR[:, :, :], in1=BT[:, :, :], op=ALU.subtract))
        dv(lambda e: e.tensor_tensor(out=BBI[:, :, :], in0=BIM[:, :, :], in1=fr_b, op=ALU.mult))
        dv(lambda e: e.tensor_tensor(out=BT[:, :, :], in0=BRE[:, :, :], in1=fi_b, op=ALU.mult))
        dv(lambda e: e.tensor_tensor(out=BBI[:, :, :], in0=BBI[:, :, :], in1=BT[:, :, :], op=ALU.add))

        tmpA = AR.alloc("tmpA", [128, 32, 8, 16], F32)
        tmpB = AR.alloc("tmpB", [128, 32, 8, 16], F32)
        XK = AR.alloc("XK", [128, 32, 2, 128], BF16)
        m_xv = AR.mark()
        XV = AR.alloc("XV", [128, 32, 2, 128], BF16)

        def lam_v(T, m0):
            return V(T, m0 * 32, [[1, 32], [32, 8], [0, 16]])

        def coef_v(T):
            return V(T, 0, [[16, 32], [0, 8], [1, 16]])

        def xout(T, ri):
            return V(T, ri * 128, [[256, 32], [16, 8], [1, 16]])

        def build_x(dst, m0, Pre, Pim, conj_neg):
            dv(lambda e: e.tensor_tensor(out=tmpA[:, :, :, :], in0=lam_v(LR, m0), in1=coef_v(Pre), op=ALU.mult), [b_cev])
            dv(lambda e: e.tensor_tensor(out=tmpB[:, :, :, :], in0=lam_v(LI, m0), in1=coef_v(Pim), op=ALU.mult))
            dv(lambda e: e.tensor_tensor(out=xout(dst, 0), in0=tmpA[:, :, :, :], in1=tmpB[:, :, :, :], op=ALU.subtract),
               extra_w=[b_fin])
            dv(lambda e: e.tensor_tensor(out=tmpA[:, :, :, :], in0=lam_v(LI, m0), in1=coef_v(Pre), op=ALU.mult))
            dv(lambda e: e.tensor_tensor(out=tmpB[:, :, :, :], in0=lam_v(LR, m0), in1=coef_v(Pim), op=ALU.mult))
            if conj_neg:
                dv(lambda e: e.scalar_tensor_tensor(out=xout(dst, 1), in0=tmpA[:, :, :, :], scalar=-1.0,
                                                    in1=tmpB[:, :, :, :], op0=ALU.mult, op1=ALU.subtract), extra_w=[b_fin])
            else:
                dv(lambda e: e.tensor_tensor(out=xout(dst, 1), in0=tmpA[:, :, :, :], in1=tmpB[:, :, :, :], op=ALU.add),
                   extra_w=[b_fin])

        build_x(XV, 0, BBR, BBI, False)
        build_x(XK, 16, BBR, BBI, False)
        build_x(WO, 8, CRE, CIM, True)

        for bk in range(8):
            ps, pb = psA.get()
            psb = ps[:, :].bitcast(BF16)
            for n in range(8):
                blk = bk * 8 + n
                S.pe(lambda e, psb=psb, n=n, blk=blk: e.transpose(
                    out=psb[:, n * 128:(n + 1) * 128], in_=V(XV, blk * 128, [[1, 128]]), identity=ident_b[:, :]),
                    reads=[bp, B_identb], writes=[pb])
            evac_copy(V(WVT, bk * 1024, [[1, 1024]]), psb[:, :], [pb], [b_fin])

        AR.release(m_xv)
        S.barrier()
        D8a = AR.alloc("D8a", [64, 16], F32)
        D8 = AR.alloc("D8", [64, 8, 16], F32)
        DPART = AR.alloc("DPART", [128, 64], F32)
        DIAG = AR.alloc("DIAG", [128, 64, 128], BF16)
        S.dma("sp", D8a[:, :], d_d[l].rearrange("(g c) -> g c", c=16), writes=[bp])
        dv(lambda e: e.tensor_copy(out=D8[:, :, :], in_=V(D8a, 0, [[0, 8], [1, 16]], npart=64)))
        ps, pb = psA.get()
        S.pe(lambda e, ps=ps: e.transpose(out=ps[:, 0:64], in_=V(D8, 0, [[1, 128]], npart=64), identity=ident_f[0:64, 0:64]),
             reads=[bp, B_identf], writes=[pb])
        dv(lambda e, ps=ps: e.tensor_copy(out=DPART[:, :], in_=ps[:, 0:64]), [pb])
        dv(lambda e: e.tensor_tensor(out=DIAG[:, :, :], in0=V(ident_f, 0, [[0, 64], [1, 128]]),
                                     in1=V(DPART, 0, [[1, 64], [0, 128]]), op=ALU.mult), [B_identf])
        for h in range(2):
            for i0 in range(0, 32, 4):
                banks = [psA.get() for _ in range(4)]
                for ri in range(2):
                    for ii in range(4):
                        i = i0 + ii
                        ps, pb = banks[ii]
                        S.pe(lambda e, ps=ps, i=i, ri=ri, h=h: e.matmul(
                            out=ps[:, 0:128],
                            lhsT=V(XK, (i * 2 + ri) * 128, [[1, 128]], p0=h * 64, npart=64),
                            rhs=V(WO, (i * 2 + ri) * 128, [[1, 128]], p0=h * 64, npart=64),
                            start=(ri == 0), stop=False), reads=[bp, b_fin], writes=[pb])
                for ii in range(4):
                    g = 2 * (i0 + ii) + h
                    ps, pb = banks[ii]
                    S.pe(lambda e, ps=ps, g=g: e.matmul(
                        out=ps[:, 0:128], lhsT=V(DIAG, g * 128, [[1, 128]]), rhs=ident_b[:, :],
                        start=False, stop=True), reads=[bp, B_identb], writes=[pb])
                for ii in range(4):
                    g = 2 * (i0 + ii) + h
                    ps, pb = banks[ii]
                    dv(lambda e, ps=ps, g=g: e.tensor_tensor(
                        out=V(WK, g * 128, [[1, 128]]), in0=ps[:, 0:128], in1=CMASK[:, :], op=ALU.mult),
                        [pb, B_c2], [b_fin])

        dbg("LR", LR[:, :, :], [128, NE, 32], [bp])
        dbg("LI", LI[:, :, :], [128, NE, 32], [bp])
        dbg("FR", FR[:, :], [128, 32], [bp])
        dbg("DT", DT[:, :], [128, 32], [bp])
        dbg("WVT", WVT[:, :, :, :], [128, 32, 2, 128], [b_fin], BF16)
        dbg("WO", WO[:, :, :, :], [128, 32, 2, 128], [b_fin], BF16)
        dbg("WK", WK[:, :, :], [128, 64, 128], [b_fin], BF16)
        dbg("A8x", A8x[:, :], [128, 64], [b_fin])
        dbg("Bm8", Bm8[:, :], [128, 64], [b_fin])
        stop("prep")
        AR.release(m_prep)
        S.barrier()

        T1, b_t1 = TB("T1", [128, 64], F32)
        T2, b_t2 = TB("T2", [128, 64], F32)
        T3, b_t3 = TB("T3", [128, 64], F32)
        ZST, b_zst = TB("ZST", [128, 64], F32)
        SI, b_si = TB("SI", [128, 64], F32)
        Ugs, b_ugs = TB("Ugs", [128, 64, 16], BF16)
        m_work = AR.mark()
        UTS, b_uts = TB("UTS", [128, 8, 1024], BF16)
        Utok, b_utok = TB("Utok", [128, 8, 1024], BF16)
        ug_r = Rot(AR, "Ug", [128, 64, 128], BF16, 2)
        vss_r = Rot(AR, "VSS", [128, 64, 129], BF16, 2)
        RB = 8
        MB = 128 // RB
        SINI, b_sini = TB("SINI", [128, 64], F32)
        BSUM = [TB("BSUM%d" % i, [128, 64, MB], F32) for i in range(2)]
        TT1, b_tt1 = TB("TT1", [128, 64, MB], F32)
        TT2, b_tt2 = TB("TT2", [128, 64, MB], F32)
        zh_r = Rot(AR, "ZH", [128, MB, 64], F32, 1)
        S.dve(lambda e: e.memset(ZST[:, :], 0.0), writes=[b_zst])

        def cmul_add(P, b_p, VSt, b_v, col0):
            S.dve(lambda e: e.tensor_tensor(out=TT1[:, :, :], in0=P[:, :, :], in1=V(A8x, 0, [[1, 64], [0, MB]]), op=ALU.mult),
                  reads=[b_p, b_fin], writes=[b_tt1])
            S.dve(lambda e: e.tensor_tensor(out=V(TT2, 0, [[32 * MB, 2], [MB, 32], [1, MB]]),
                                            in0=V(P, 32 * MB, [[-32 * MB, 2], [MB, 32], [1, MB]]),
                                            in1=V(Bm8, 0, [[32, 2], [1, 32], [0, MB]]), op=ALU.mult),
                  reads=[b_p, b_fin], writes=[b_tt2])
            S.dve(lambda e: e.tensor_tensor(out=TT1[:, :, :], in0=TT1[:, :, :], in1=TT2[:, :, :], op=ALU.add),
                  reads=[b_tt1, b_tt2], writes=[b_tt1])
            S.dve(lambda e: e.tensor_tensor(out=P[:, :, :], in0=TT1[:, :, :], in1=V(VSt, col0, [[129, 64], [RB, MB]]),
                                            op=ALU.add), reads=[b_tt1, b_v], writes=[b_p])

        def scan_blocked(prev, VSt, b_v, hist, st):
            pt, pbuf, poff = prev
            ipt, ipbuf, ipoff = prev
            BS, b_bs = BSUM[st]
            if hist:
                S.dve(lambda e, pt=pt, poff=poff: e.tensor_copy(out=SINI[:, :], in_=V(pt, poff, [[1, 64]])),
                      reads=[pbuf], writes=[b_sini])
                ipt, ipbuf, ipoff = SINI, b_sini, 0
            if not hist:
                S.dve(lambda e: e.tensor_copy(out=BS[:, :, :], in_=V(VSt, 1, [[129, 64], [RB, MB]])), reads=[b_v[0]], writes=[b_bs])
                for s in range(1, RB):
                    cmul_add(BS, b_bs, VSt, b_v[s], 1 + s)
            ZH, zb = zh_r.get()
            for m in range(MB):
                S.dve(lambda e, pt=pt, poff=poff: e.tensor_tensor(
                    out=T1[:, :], in0=V(pt, poff, [[1, 64]]), in1=ARx[:, :], op=ALU.mult),
                    reads=[pbuf, b_fin], writes=[b_t1])
                S.dve(lambda e, pt=pt, poff=poff: e.tensor_tensor(
                    out=V(T2, 0, [[32, 2], [1, 32]]), in0=V(pt, poff + 32, [[-32, 2], [1, 32]]),
                    in1=V(BmR, 0, [[32, 2], [1, 32]]), op=ALU.mult), reads=[pbuf, b_fin], writes=[b_t2])
                S.dve(lambda e: e.tensor_tensor(out=T3[:, :], in0=T1[:, :], in1=T2[:, :], op=ALU.add),
                      reads=[b_t1, b_t2], writes=[b_t3])
                S.dve(lambda e, ZH=ZH, m=m: e.tensor_tensor(
                    out=ZH[:, m, :], in0=T3[:, :], in1=V(BS, m, [[MB, 64]]), op=ALU.add),
                    reads=[b_t3, b_bs], writes=[zb])
                pt, pbuf, poff = ZH, zb, m * 64
            if hist:
                S.pool(lambda e: e.tensor_copy(out=V(VSt, 0, [[129, 64]]), in_=V(ipt, ipoff, [[1, 64]])),
                       reads=[ipbuf], writes=[b_v[RB]])
                S.dve(lambda e: e.tensor_copy(out=V(BS, 0, [[MB, 64]]), in_=V(ipt, ipoff, [[1, 64]])),
                      reads=[ipbuf], writes=[b_bs])
                S.dve(lambda e, ZH=ZH: e.tensor_copy(out=V(BS, 1, [[MB, 64], [1, MB - 1]]),
                                                     in_=V(ZH, 0, [[1, 64], [64, MB - 1]])), reads=[zb], writes=[b_bs])
                for s in range(RB - 1):
                    cmul_add(BS, b_bs, VSt, b_v[s], 1 + s)
                    S.dve(lambda e, s=s: e.tensor_copy(out=V(VSt, 1 + s, [[129, 64], [RB, MB]]), in_=BS[:, :, :]),
                          reads=[b_bs], writes=[b_v[s]])
                S.pool(lambda e, ZH=ZH: e.tensor_copy(out=V(VSt, RB, [[129, 64], [RB, MB]]), in_=V(ZH, 0, [[1, 64], [64, MB]])),
                       reads=[zb], writes=[b_v[RB - 1]])
            return pt, pbuf, poff

        def scan(prev, VSt, b_v, nsteps, hist):
            pt, pbuf, poff = prev
            if hist:
                S.pool(lambda e, pt=pt, poff=poff: e.tensor_copy(out=V(VSt, 0, [[129, 64]]), in_=V(pt, poff, [[1, 64]])),
                       reads=[pbuf], writes=[b_v])
            ring = rb = None
            for k in range(nsteps):
                r = k % 32
                if r == 0:
                    ring, rb = zh_r.get()
                S.dve(lambda e, pt=pt, poff=poff: e.tensor_tensor(
                    out=T1[:, :], in0=V(pt, poff, [[1, 64]]), in1=A8x[:, :], op=ALU.mult),
                    reads=[pbuf, b_fin], writes=[b_t1])
                S.dve(lambda e, pt=pt, poff=poff: e.tensor_tensor(
                    out=V(T2, 0, [[32, 2], [1, 32]]), in0=V(pt, poff + 32, [[-32, 2], [1, 32]]),
                    in1=V(Bm8, 0, [[32, 2], [1, 32]]), op=ALU.mult), reads=[pbuf, b_fin], writes=[b_t2])
                S.dve(lambda e: e.tensor_tensor(out=T3[:, :], in0=T1[:, :], in1=T2[:, :], op=ALU.add),
                      reads=[b_t1, b_t2], writes=[b_t3])
                S.dve(lambda e, ring=ring, r=r, k=k: e.tensor_tensor(
                    out=ring[:, r, :], in0=T3[:, :], in1=V(VSt, k + 1, [[129, 64]]), op=ALU.add),
                    reads=[b_t3, b_v], writes=[rb])
                pt, pbuf, poff = ring, rb, r * 64
                if hist and r == 31:
                    k0 = k - 31
                    S.pool(lambda e, ring=ring, k0=k0: e.tensor_copy(
                        out=V(VSt, k0 + 1, [[129, 64], [1, 32]]), in_=V(ring, 0, [[1, 64], [64, 32]])),
                        reads=[rb], writes=[b_v])
            return pt, pbuf, poff

        def stage2(src_t, src_b, dst_t, dst_b, ncol, npart, eng=None):
            for gb in range(8):
                ps, pb = psA.get()
                psb = ps[:, :].bitcast(BF16)
                for gg in range(8):
                    g = gb * 8 + gg
                    S.pe(lambda e, psb=psb, gg=gg, g=g: e.transpose(
                        out=psb[:, gg * 128:gg * 128 + ncol],
                        in_=V(src_t, 128 * g, [[1, 128]], npart=npart), identity=ident_b[0:npart, 0:npart]),
                        reads=[src_b, B_identb], writes=[pb])
                evac_copy(V(dst_t, gb * 8 * ncol, [[ncol, 8], [1, ncol]]),
                          bass.AP(tensor=psb.tensor, offset=0, ap=[[1024, 128], [128, 8], [1, ncol]]), [pb], [dst_b], eng)

        vss_tiles = []
        for st in range(2):
            for tt in range(4):
                tile = TILES[st * 4 + tt]
                xt, xb = load_xt_input(tile, TT1, b_tt1) if (l == 0 and FUSE_IO) else load_xt(tile)
                rmsnorm(xt, xb, TW, 8 * l, out=(UTS, b_uts, tt * TW))
            for j in range(8):
                ps, pb = psA.get()
                psb = ps[:, :].bitcast(BF16)
                for q in range(8):
                    S.pe(lambda e, psb=psb, q=q, j=j: e.transpose(
                        out=psb[:, q * 128:(q + 1) * 128], in_=V(UTS, q * 1024 + j, [[8, 128]]), identity=ident_b[:, :]),
                        reads=[b_uts, B_identb], writes=[pb])
                evac_copy(V(Utok, j * 16, [[128, 64], [1, 16]]),
                          bass.AP(tensor=psb.tensor, offset=0, ap=[[1024, 128], [16, 64], [1, 16]]), [pb], [b_utok], "act")
            Ug, b_ug = ug_r.get()
            stage2(Utok, b_utok, Ug, b_ug, 128, 128, "act")
            VSS, _unused = vss_r.get()
            b_vs = [Buf() for _ in range(RB + 1)]
            for bk in range(16):
                ps, pb = psA.get()
                for ee in range(4):
                    en = bk * 4 + ee
                    ri, i = divmod(en, 32)
                    for h in range(2):
                        S.pe(lambda e, ps=ps, ee=ee, ri=ri, i=i, h=h, Ug=Ug: e.matmul(
                            out=ps[h * 64:(h + 1) * 64, ee * 128:(ee + 1) * 128],
                            lhsT=V(WVT, (i * 2 + ri) * 128 + h * 64, [[1, 64]]),
                            rhs=V(Ug, (2 * i + h) * 128, [[1, 128]]), start=True, stop=True),
                            reads=[b_fin, b_ug], writes=[pb])
                evac_copy(V(VSS, bk * 4 * 129 + 1, [[129, 4], [1, 128]]), V(ps, 0, [[128, 4], [1, 128]]), [pb], list(b_vs), "act")
            vss_tiles.append((Ug, b_ug, VSS, b_vs))
        state = (ZST, b_zst, 0)
        for st in range(2):
            state = scan_blocked(state, vss_tiles[st][2], vss_tiles[st][3], False, st)

        pt, pbuf, poff = state
        dbg("P1", V(pt, poff, [[1, 64]]), [128, 64], [pbuf])
        stop("s5a")
        b_ccin, b_ccout = Buf(), Buf()
        S.dma("pool", cc_in[l].ap(), V(pt, poff, [[1, 64]]), reads=[pbuf], writes=[b_ccin])
        S.cc(lambda e: e.collective_compute("AllGather", ALU.bypass, replica_groups=[[0, 1], [2, 3], [4, 5], [6, 7]],
                                            ins=[cc_in[l].ap().opt()], outs=[cc_out[l].ap().opt()]),
             reads=[b_ccin], writes=[b_ccout])
        S.dma("pool", SI[:, :], cc_out[l].ap()[0:128, :], reads=[b_ccout], writes=[b_si])
        S.dve(lambda e: e.tensor_scalar(out=SI[:, :], in0=SI[:, :], scalar1=FLAG[:, 0:1], scalar2=None, op0=ALU.mult),
              reads=[b_si, B_c2], writes=[b_si])

        dbg("SI", SI[:, :], [128, 64], [b_si])
        stop("xchg")
        tile = TILES[8]
        xt, xb = load_xt_input(tile, TT1, b_tt1) if (l == 0 and FUSE_IO) else load_xt(tile)
        uts, utsb = rmsnorm(xt, xb, TS, 8 * l)
        S.dve(lambda e: e.memset(Utok[0:TS, :, :], 0.0), writes=[b_utok])
        ps, pb = psA.get()
        psb = ps[:, :].bitcast(BF16)
        for q in range(8):
            S.pe(lambda e, psb=psb, q=q, uts=uts: e.transpose(
                out=psb[0:TS, q * 128:(q + 1) * 128], in_=uts[:, q, 0:TS], identity=ident_b[:, :]),
                reads=[utsb, B_identb], writes=[pb])
        S.act(lambda e, psb=psb: e.copy(out=V(Utok, 0, [[128, 64], [1, 16]], npart=TS),
                                        in_=bass.AP(tensor=psb.tensor, offset=0, ap=[[1024, TS], [16, 64], [1, 16]])),
              reads=[pb], writes=[b_utok])
        S.dve(lambda e, psb=psb: e.tensor_copy(out=V(Utok, 7 * 16, [[128, 64], [1, 16]], npart=TS),
                                               in_=bass.AP(tensor=psb.tensor, offset=0, ap=[[1024, TS], [16, 64], [1, 16]])),
              reads=[pb], writes=[b_utok])
        stage2(Utok, b_utok, Ugs, b_ugs, TS, TS)

        state = (SI, b_si, 0)
        tl = vss_tiles
        for st in range(2):
            Ug, b_ug, VSS, b_vs = tl[st]
            state = scan_blocked(state, VSS, b_vs, True, st)
        for st in range(2):
            Ug, b_ug, VSS, b_vs = tl[st]
            for h in range(2):
                for i0 in range(0, 32, 4):
                    banks = [psA.get() for _ in range(4)]
                    for ri in range(2):
                        for ii in range(4):
                            i = i0 + ii
                            ps, pb = banks[ii]
                            S.pe(lambda e, ps=ps, i=i, ri=ri, h=h, VSS=VSS: e.matmul(
                                out=ps[:, 0:128],
                                lhsT=V(VSS, (ri * 32 + i) * 129, [[1, 128]], p0=h * 64, npart=64),
                                rhs=V(WO, (i * 2 + ri) * 128, [[1, 128]], p0=h * 64, npart=64),
                                start=(ri == 0), stop=False), reads=list(b_vs) + [b_fin], writes=[pb])
                    for ii in range(4):
                        g = 2 * (i0 + ii) + h
                        ps, pb = banks[ii]
                        S.pe(lambda e, ps=ps, g=g, Ug=Ug: e.matmul(
                            out=ps[:, 0:128], lhsT=V(Ug, g * 128, [[1, 128]]),
                            rhs=V(WK, g * 128, [[1, 128]]), start=False, stop=True),
                            reads=[b_ug, b_fin], writes=[pb])
                    for ii in range(4):
                        g = 2 * (i0 + ii) + h
                        ps, pb = banks[ii]
                        S.act(lambda e, ps=ps, g=g: e.activation(
                            out=V(Utok, 16 * g, [[1024, 8], [1, 16]]),
                            in_=V(ps, 0, [[16, 8], [1, 16]]), func=AF.Gelu_apprx_tanh),
                            reads=[pb], writes=[b_utok])
            for j in range(8):
                ps, pb = psA.get()
                psb = ps[:, :].bitcast(BF16)
                for q in range(8):
                    S.pe(lambda e, psb=psb, q=q, j=j: e.transpose(
                        out=psb[:, q * 128:(q + 1) * 128], in_=Utok[:, j, q * 128:(q + 1) * 128], identity=ident_b[:, :]),
                        reads=[b_utok, B_identb], writes=[pb])
                evac_copy(V(UTS, j, [[1024, 8], [8, 128]]),
                          bass.AP(tensor=psb.tensor, offset=0, ap=[[1024, 128], [128, 8], [1, 128]]), [pb], [b_uts], "act")
            S.dma("pool", UT_d[:, :, st * 1024:(st + 1) * 1024], UTS[:, :, :], reads=[b_uts],
                  writes=[UT_b[i] for i in range(st * 8, st * 8 + 8)])

        stop("s5b")
        pt, pbuf, poff = state
        ps, pb = psA.get()
        S.pe(lambda e, ps=ps, pt=pt, poff=poff: e.transpose(out=ps[0:64, 0:128], in_=V(pt, poff, [[1, 64]]),
                                                            identity=ident_f[:, :]),
             reads=[pbuf, B_identf], writes=[pb])
        fin_o, b_fino = TB("fin_o", [64, 128], F32)
        S.dve(lambda e, ps=ps: e.tensor_copy(out=fin_o[:, :], in_=ps[0:64, 0:128]), reads=[pb], writes=[b_fino])
        S.dma("pool", sp_state[l], fin_o[:, :], reads=[b_fino])

        stop("fin")
        AR.release(m_work)
        S.barrier()
        S0IN, b_s0in = TB("S0IN", [TS, 4096], F32)
        S0, b_s0 = TB("S0", [128, 64, TS], F32)
        S0b, b_s0b = TB("S0b", [128, 64, TS], BF16)
        for half in range(2):
            S.dma("sp", S0IN[:, :], (st_re_d if half == 0 else st_im_d)[l], writes=[b_s0in])
            ps, pb = psA.get()
            for ee in range(32):
                en = half * 32 + ee
                S.pe(lambda e, ps=ps, ee=ee, en=en: e.transpose(
                    out=ps[:, ee * TS:(ee + 1) * TS], in_=V(S0IN, ee * 128, [[1, 128]], npart=TS),
                    identity=ident_f[0:TS, 0:TS]), reads=[b_s0in, B_identf], writes=[pb])
            S.dve(lambda e, ps=ps, half=half: e.tensor_copy(out=V(S0, half * 512, [[1, 512]]), in_=ps[:, :]),
                  reads=[pb], writes=[b_s0])
            S.act(lambda e, ps=ps, half=half: e.copy(out=V(S0b, half * 512, [[1, 512]]), in_=ps[:, :]),
                  reads=[pb], writes=[b_s0b])
        stop("s0")
        Gs, b_gs = TB("Gs", [TS, D], BF16)
        for h in range(2):
            ps, pb = psA.get()
            for i in range(32):
                g = 2 * i + h
                for ri in range(2):
                    S.pe(lambda e, ps=ps, i=i, ri=ri, h=h: e.matmul(
                        out=ps[0:TS, i * 16:(i + 1) * 16],
                        lhsT=V(S0b, (ri * 32 + i) * TS, [[1, TS]], p0=h * 64, npart=64),
                        rhs=V(WO, (i * 2 + ri) * 128, [[1, 16]], p0=h * 64, npart=64),
                        start=(ri == 0), stop=False), reads=[b_s0b, b_fin], writes=[pb])
                S.pe(lambda e, ps=ps, i=i, g=g: e.matmul(
                    out=ps[0:TS, i * 16:(i + 1) * 16], lhsT=V(Ugs, g * TS, [[1, TS]]),
                    rhs=V(WK, g * 128, [[1, 16]]), start=False, stop=True), reads=[b_ugs, b_fin], writes=[pb])
            S.act(lambda e, ps=ps, h=h: e.activation(
                out=V(Gs, 16 * h, [[32, 32], [1, 16]], npart=TS), in_=V(ps, 0, [[16, 32], [1, 16]], npart=TS),
                func=AF.Gelu_apprx_tanh), reads=[pb], writes=[b_gs])
        ps, pb = psA.get()
        psb = ps[:, :].bitcast(BF16)
        for q in range(8):
            S.pe(lambda e, psb=psb, q=q: e.transpose(out=psb[:, q * TS:(q + 1) * TS], in_=Gs[:, q * 128:(q + 1) * 128],
                                                     identity=ident_b[0:TS, 0:TS]), reads=[b_gs, B_identb], writes=[pb])
        gts, b_gts = TB("gts", [128, 8, TS], BF16)
        S.dve(lambda e, psb=psb: e.tensor_copy(out=V(gts, 0, [[1, 8 * TS]]), in_=psb[:, 0:8 * TS]), reads=[pb], writes=[b_gts])
        S.dma("pool", UT_d[:, :, TP:TP + TS], gts[:, :, :], reads=[b_gts], writes=[UT_b[16]])
        stop("ys")
        SN, b_sn = TB("SN", [128, 64, TS], F32)
        TA, b_ta = TB("TA", [128, 64, TS], F32)
        TBt, b_tb = TB("TBt", [128, 64, TS], F32)
        S.dve(lambda e: e.tensor_tensor(out=TA[:, :, :], in0=S0[:, :, :], in1=V(A1x, 0, [[1, 64], [0, TS]]), op=ALU.mult),
              reads=[b_s0, b_fin], writes=[b_ta])
        S.dve(lambda e: e.tensor_tensor(out=V(TBt, 0, [[32 * TS, 2], [TS, 32], [1, TS]]),
                                        in0=V(S0, 32 * TS, [[-32 * TS, 2], [TS, 32], [1, TS]]),
                                        in1=V(Bm1, 0, [[32, 2], [1, 32], [0, TS]]), op=ALU.mult),
              reads=[b_s0, b_fin], writes=[b_tb])
        S.dve(lambda e: e.tensor_tensor(out=TA[:, :, :], in0=TA[:, :, :], in1=TBt[:, :, :], op=ALU.add),
              reads=[b_ta, b_tb], writes=[b_ta])
        for half in range(2):
            ps, pb = psA.get()
            for ee in range(32):
                en = half * 32 + ee
                ri, i = divmod(en, 32)
                for h in range(2):
                    S.pe(lambda e, ps=ps, ee=ee, ri=ri, i=i, h=h: e.matmul(
                        out=ps[h * 64:(h + 1) * 64, ee * TS:(ee + 1) * TS],
                        lhsT=V(WVT, (i * 2 + ri) * 128 + h * 64, [[1, 64]], p0=64, npart=64),
                        rhs=V(Ugs, (2 * i + h) * TS, [[1, TS]], p0=64, npart=64), start=True, stop=True),
                        reads=[b_fin, b_ugs], writes=[pb])
            S.dve(lambda e, ps=ps, half=half: e.tensor_tensor(
                out=V(SN, half * 512, [[1, 512]]), in0=ps[:, :], in1=V(TA, half * 512, [[1, 512]]), op=ALU.add),
                reads=[pb, b_ta], writes=[b_sn])
        OUTS, b_outs = TB("OUTS", [TS, 32, 128], F32)
        for ri in range(2):
            for bk in range(8):
                ps, pb = psA.get()
                for ee in range(4):
                    i = bk * 4 + ee
                    S.pe(lambda e, ps=ps, ee=ee, i=i, ri=ri: e.transpose(
                        out=ps[0:TS, ee * 128:(ee + 1) * 128], in_=V(SN, (ri * 32 + i) * TS, [[1, TS]]),
                        identity=ident_f[:, :]), reads=[b_sn, B_identf], writes=[pb])
                evac_copy(V(OUTS, bk * 512, [[1, 512]], npart=TS), ps[0:TS, :], [pb], [b_outs])
            S.dma("pool", ss_state[l, ri], V(OUTS, 0, [[1, 4096]], npart=TS), reads=[b_outs])

        AR.release(m_layer)
        S.barrier()

    def glu(l, after_w=None):
        m = AR.mark()
        WG = AR.alloc("WG", [128, 8, 2048], BF16)
        b_wg = [Buf() for _ in range(4)]
        for j in range(4):
            S.dma("pool", WG[:, :, j * 512:(j + 1) * 512],
                  w_glu_d[l, :, j * 512:(j + 1) * 512].rearrange("(q p) f -> p q f", p=128), writes=[b_wg[j]])
        lazy = after_w() if after_w is not None else []
        sg_r = Rot(AR, "sg", [128, TW], F32, 2)
        gt_r = Rot(AR, "gt", [128, TW], F32, 2)
        for tile in TILES:
            c0, n, bl = tile
            xt, xb = load_xt(tile)
            ut, utb = ut_r.get()
            S.dma("sp", ut[:, :, 0:n], UT_d[:, :, c0:c0 + n], reads=[UT_b[i] for i in bl], writes=[utb])
            for m8 in range(8):
                ps, pb = psA.get()
                for k in range(2):
                    col = k * 1024 + m8 * 128
                    for q in range(8):
                        S.pe(lambda e, ps=ps, k=k, col=col, q=q, ut=ut, n=n: e.matmul(
                            out=ps[:, k * 256:k * 256 + n], lhsT=WG[:, q, col:col + 128], rhs=ut[:, q, 0:n],
                            start=(q == 0), stop=(q == 7)), reads=[utb, b_wg[col // 512]], writes=[pb])
                sg, sgb = sg_r.get()
                r1 = 72 + 16 * l + m8
                S.act(lambda e, ps=ps, sg=sg, r1=r1, n=n: e.activation(
                    out=sg[:, 0:n], in_=ps[:, 256:256 + n], func=AF.Sigmoid, bias=VEC[:, r1 + 8:r1 + 9]),
                    reads=[pb, B_vec], writes=[sgb])
                gt, gtb = gt_r.get()
                S.dve(lambda e, ps=ps, sg=sg, gt=gt, r1=r1, n=n: e.scalar_tensor_tensor(
                    out=gt[:, 0:n], in0=ps[:, 0:n], scalar=VEC[:, r1:r1 + 1], in1=sg[:, 0:n],
                    op0=ALU.add, op1=ALU.mult), reads=[pb, sgb, B_vec], writes=[gtb])
                S.pool(lambda e, xt=xt, gt=gt, m8=m8, n=n: e.tensor_tensor(
                    out=xt[:, m8, 0:n], in0=xt[:, m8, 0:n], in1=gt[:, 0:n], op=ALU.add),
                    reads=[gtb, xb], writes=[xb])
            store_xt(tile, xt, xb)
            if len(lazy) > 8:
                lazy.pop(0)()
        while len(lazy) > 8:
            lazy.pop(0)()
        phase_end(m)
        return lazy

    w_k_d = din("w_k", [D, 256])
    w_v_d = din("w_v", [D, 256])
    w_q_d = din("w_q", [2, D, D])
    w_o_d = din("w_o", [2, D, D])
    gqk_d = din("gqk", [128, 6])
    sinks_d = din("sinks", [128, 16])
    rot_d = din("rot", [128, 128])
    bones_d = din("bones", [128, 128])
    mask_d = din("masks", [3, 128, 128])
    cos_d = din("cos_t", [128, NT])
    sin_d = din("sin_t", [128, NT])
    cache_k_d = din("cache_k", [TS, 128, 256])
    cache_v_d = din("cache_v", [TS, 128, 256])
    k_last = dout("k_last", [128, 256])
    v_last = dout("v_last", [128, 256])
    ks_out = dout("ks_out", [TS, 128, 256])
    vs_out = dout("vs_out", [TS, 128, 256])
    kvx_in = nc.dram_tensor("kvx_in", [128, 384], F32)
    kvx_out = nc.dram_tensor("kvx_out", [256, 384], F32)

    KT_d = nc.dram_tensor("KT_d", [128, 4, 128 + NT], BF16).ap()
    V_d = nc.dram_tensor("V_d", [128, 18, 256], BF16).ap()
    KTs_d = nc.dram_tensor("KTs_d", [128, TS, 4, 128], BF16).ap()
    Vs_d = nc.dram_tensor("Vs_d", [128, TS, 256], BF16).ap()
    b_kvd = Buf("kv_dram")

    if True:
        MASK_F = AR.alloc("MASK_F", [128, 256], BF16)
        MASK_R = AR.alloc("MASK_R", [128, 256], BF16)
        ROT_b = AR.alloc("ROT_b", [128, 128], BF16)
        BONES_b = AR.alloc("BONES_b", [128, 128], BF16)
        GQK = AR.alloc("GQK", [128, 6], F32)
        ESINK = AR.alloc("ESINK", [128, 16], F32)
        EPSV = AR.alloc("EPSV", [128, 1], F32)
        b_c3 = Buf()
        S.dve(lambda e: e.memset(EPSV[:, :], EPS), writes=[b_c3])
        m_tmp = AR.mark()
        tmpc = AR.alloc("tmpc", [128, 256], F32)
        tmpm = AR.alloc("tmpm", [128, 3, 128], F32)
        for i3 in range(3):
            S.dma("sp", tmpm[:, i3, :], mask_d[i3], writes=[b_c3])
        S.dve(lambda e: e.tensor_copy(out=MASK_R[:, :], in_=V(tmpm, 0, [[1, 256]])), reads=[b_c3], writes=[b_c3])
        S.dve(lambda e: e.tensor_copy(out=MASK_F[:, 0:128], in_=tmpm[:, 2, :]), reads=[b_c3], writes=[b_c3])
        S.dve(lambda e: e.tensor_copy(out=MASK_F[:, 128:256], in_=tmpm[:, 1, :]), reads=[b_c3], writes=[b_c3])
        S.dma("sp", GQK[:, :], gqk_d, writes=[b_c3])
        S.dma("sp", ESINK[:, :], sinks_d, writes=[b_c3])
        S.act(lambda e: e.activation(out=ESINK[:, :], in_=ESINK[:, :], func=AF.Exp), reads=[b_c3], writes=[b_c3])
        S.dma("sp", tmpc[:, 0:128], rot_d, writes=[b_c3])
        S.dma("sp", tmpc[:, 128:256], bones_d, writes=[b_c3])
        S.dve(lambda e: e.tensor_copy(out=ROT_b[:, :], in_=tmpc[:, 0:128]), reads=[b_c3], writes=[b_c3])
        S.dve(lambda e: e.tensor_copy(out=BONES_b[:, :], in_=tmpc[:, 128:256]), reads=[b_c3], writes=[b_c3])
        AR.release(m_tmp)
        S.barrier()

    class QK:
        def __init__(self):
            self.cs_r = Rot(AR, "cs", [128, 2, TW], F32, 2)
            self.sqb_r = Rot(AR, "sqb", [128, TW], BF16, 2)
            self.kb_r = Rot(AR, "kbb", [128, TW], BF16, 2)
            self.rq_r = Rot(AR, "rq", [128, TW], F32, 2)
            self.ta_r = Rot(AR, "ta", [128, TW], F32, 2)
            self.tb_r = Rot(AR, "tbb", [128, TW], F32, 2)

    if True:
        def load_cs(H, tile):
            c0, n, bl = tile
            cs, csb = H.cs_r.get()
            S.dma("sp", cs[:, 0, 0:n], cos_d[:, c0:c0 + n], writes=[csb])
            S.dma("sp", cs[:, 1, 0:n], sin_d[:, c0:c0 + n], writes=[csb])
            return cs, csb

        def qk_post(H, ps, pb, n, gcol, cs, csb, out_ap, out_bs):
            sq, sqb = H.sqb_r.get()
            kb, kbb = H.kb_r.get()
            S.act(lambda e: e.activation(out=sq[:, 0:n], in_=ps[:, 0:n], func=AF.Square), reads=[pb], writes=[sqb])
            S.act(lambda e: e.copy(out=kb[:, 0:n], in_=ps[:, 0:n]), reads=[pb], writes=[kbb])
            ps2, pb2 = psA.get()
            S.pe(lambda e: e.matmul(out=ps2[:, 0:n], lhsT=BONES_b[:, :], rhs=sq[:, 0:n], start=True, stop=True),
                 reads=[sqb, b_c3], writes=[pb2])
            S.pe(lambda e: e.matmul(out=ps2[:, 256:256 + n], lhsT=ROT_b[:, :], rhs=kb[:, 0:n], start=True, stop=True),
                 reads=[kbb, b_c3], writes=[pb2])
            rq, rqb = H.rq_r.get()
            S.act(lambda e: e.activation(out=rq[:, 0:n], in_=ps2[:, 0:n], func=AF.Ln, bias=EPSV[:, 0:1], scale=1.0 / 64),
                  reads=[pb2, b_c3], writes=[rqb])
            S.act(lambda e: e.activation(out=rq[:, 0:n], in_=rq[:, 0:n], func=AF.Exp, scale=-0.5),
                  reads=[rqb], writes=[rqb])
            ta, tab = H.ta_r.get()
            tb, tbb = H.tb_r.get()
            S.dve(lambda e: e.scalar_tensor_tensor(out=ta[:, 0:n], in0=ps[:, 0:n], scalar=GQK[:, gcol:gcol + 1],
                                                   in1=cs[:, 0, 0:n], op0=ALU.mult, op1=ALU.mult),
                  reads=[pb, csb, b_c3], writes=[tab])
            S.dve(lambda e: e.scalar_tensor_tensor(out=tb[:, 0:n], in0=ps2[:, 256:256 + n],
                                                   scalar=GQK[:, gcol + 1:gcol + 2], in1=cs[:, 1, 0:n],
                                                   op0=ALU.mult, op1=ALU.mult),
                  reads=[pb2, csb, b_c3], writes=[tbb])
            S.pool(lambda e: e.tensor_tensor(out=ta[:, 0:n], in0=ta[:, 0:n], in1=tb[:, 0:n], op=ALU.add),
                   reads=[tab, tbb], writes=[tab])
            S.dve(lambda e: e.tensor_tensor(out=out_ap, in0=ta[:, 0:n], in1=rq[:, 0:n], op=ALU.mult),
                  reads=[tab, rqb], writes=list(out_bs))

    def kv_phase(hook=None):
        m_kv = AR.mark()
        H = QK()
        KT_all = AR.alloc("KT_all", [128, 4, 128 + NT], BF16)
        V_all = AR.alloc("V_all", [128, 18, 256], BF16)
        KTs = AR.alloc("KTs", [128, TS, 4, 128], BF16)
        Vs = AR.alloc("Vs", [128, TS, 256], BF16)
        b_kt = [Buf() for _ in range(18)]
        b_vv = [Buf() for _ in range(18)]
        b_kts, b_vs2 = Buf(), Buf()
        b_kso, b_vso = Buf(), Buf()
        S.dma("sp", ks_out[:, 0:127, :], cache_k_d[:, 1:128, :], writes=[b_kso])
        S.dma("sp", vs_out[:, 0:127, :], cache_v_d[:, 1:128, :], writes=[b_vso])
        S.pool(lambda e: e.memset(V_all[:, 17, :], 0.0), writes=[b_vv[17]])
        WKd = AR.alloc("WKd", [128, 8, 4, 2, 64], BF16)
        WVt = AR.alloc("WVt", [128, 8, 256], BF16)
        b_wk, b_wv = Buf(), Buf()
        WKs = AR.alloc("WKs", [128, 8, 256], BF16)
        b_wks = Buf()
        S.dma("pool", WKs[:, :, :], w_k_d.rearrange("(q p) f -> p q f", p=128), writes=[b_wks])
        for q in range(8):
            S.pool(lambda e, q=q: e.tensor_copy(out=V(WKd, q * 512, [[128, 4], [64, 2], [1, 64]]),
                                                in_=V(WKs, q * 256, [[64, 4], [0, 2], [1, 64]])),
                   reads=[b_wks], writes=[b_wk])
        S.dma("pool", WVt[:, :, :], w_v_d.rearrange("(q p) f -> p q f", p=128), writes=[b_wv])
        if hook is not None:
            hook()
        for ti, tile in enumerate(TILES):
            c0, n, bl = tile
            xt, xb = load_xt(tile)
            ut, utb = rmsnorm(xt, xb, n, 64, lnexp=True)
            cs, csb = load_cs(H, tile)
            kbufs = [b_kt[1 + i] for i in bl]
            kps = {}

            def projk(kvh, ut=ut, utb=utb, n=n, kps=kps):
                ps, pb = psA.get()
                for q in range(8):
                    S.pe(lambda e, ps=ps, q=q, kvh=kvh, ut=ut, n=n: e.matmul(
                        out=ps[:, 0:n], lhsT=V(WKd, (q * 4 + kvh) * 128, [[1, 128]]), rhs=ut[:, q, 0:n],
                        start=(q == 0), stop=(q == 7)), reads=[utb, b_wk], writes=[pb])
                kps[kvh] = (ps, pb)

            projk(0)
            projk(1)
            for kvh in range(4):
                ps, pb = kps.pop(kvh)
                qk_post(H, ps, pb, n, 4, cs, csb, KT_all[:, kvh, 128 + c0:128 + c0 + n], kbufs)
                if kvh + 2 < 4:
                    projk(kvh + 2)
            for sb in range((n + 127) // 128):
                nn = min(128, n - sb * 128)
                blk = bl[sb]
                ps, pb = psA.get()
                for q in range(8):
                    S.pe(lambda e, ps=ps, q=q, ut=ut, sb=sb, nn=nn: e.matmul(
                        out=ps[0:nn, 0:256], lhsT=ut[:, q, sb * 128:sb * 128 + nn], rhs=WVt[:, q, :],
                        start=(q == 0), stop=(q == 7)), reads=[utb, b_wv], writes=[pb])
                evac_copy(V_all[0:nn, 1 + blk, :], ps[0:nn, 0:256], [pb], [b_vv[1 + blk]])

        kl, b_kl = TB("kl", [128, 4, 64], F32)
        vl, b_vl = TB("vl", [128, 256], F32)
        ps, pb = psA.get()
        psb = ps[:, :].bitcast(BF16)
        for kvh in range(4):
            S.pe(lambda e, psb=psb, kvh=kvh: e.transpose(
                out=psb[:, kvh * 128:(kvh + 1) * 128], in_=KT_all[:, kvh, 128 + TP - 128:128 + TP], identity=ident_b[:, :]),
                reads=[b_kt[16], B_identb], writes=[pb])
        S.dve(lambda e, psb=psb: e.tensor_copy(out=kl[:, :, :], in_=bass.AP(tensor=psb.tensor, offset=0,
                                                                           ap=[[1024, 128], [128, 4], [1, 64]])),
              reads=[pb], writes=[b_kl])
        S.dma("pool", k_last, V(kl, 0, [[1, 256]]), reads=[b_kl])
        S.act(lambda e: e.copy(out=vl[:, :], in_=V_all[:, 16, :]), reads=[b_vv[16]], writes=[b_vl])
        S.dma("pool", v_last, vl[:, :], reads=[b_vl])
        kn, b_kn = TB("kn", [TS, 4, 64], F32)
        vn, b_vn = TB("vn", [TS, 256], F32)
        ps, pb = psA.get()
        psb = ps[:, :].bitcast(BF16)
        for kvh in range(4):
            S.pe(lambda e, psb=psb, kvh=kvh: e.transpose(
                out=psb[0:TS, kvh * 128:(kvh + 1) * 128], in_=KT_all[:, kvh, 128 + TP:128 + TP + TS], identity=ident_b[:, :]),
                reads=[b_kt[17], B_identb], writes=[pb])
        S.dve(lambda e, psb=psb: e.tensor_copy(out=kn[:, :, :], in_=bass.AP(tensor=psb.tensor, offset=0,
                                                                           ap=[[1024, TS], [128, 4], [1, 64]])),
              reads=[pb], writes=[b_kn])
        S.act(lambda e: e.copy(out=vn[:, :], in_=V_all[0:TS, 17, :]), reads=[b_vv[17]], writes=[b_vn])
        S.dma("pool", ks_out[:, 127, :], V(kn, 0, [[1, 256]], npart=TS), reads=[b_kn], writes=[b_kso])
        S.dma("pool", vs_out[:, 127, :], vn[:, :], reads=[b_vn], writes=[b_vso])
        b_xin, b_xout = Buf(), Buf()
        xin_b = kvx_in.ap().bitcast(BF16)
        for kvh in range(4):
            S.dma("pool", xin_b[:, kvh * 128:(kvh + 1) * 128], KT_all[:, kvh, 128 + TP - 128:128 + TP],
                  reads=[b_kt[16]], writes=[b_xin])
        S.dma("pool", xin_b[:, 512:768], V_all[:, 16, :], reads=[b_vv[16]], writes=[b_xin])
        S.cc(lambda e: e.collective_compute("AllGather", ALU.bypass, replica_groups=[[0, 1], [2, 3], [4, 5], [6, 7]],
                                            ins=[kvx_in.ap().opt()], outs=[kvx_out.ap().opt()]),
             reads=[b_xin], writes=[b_xout])
        kc_r = Rot(AR, "kc", [128, 256], F32, 2)
        kc2_r = Rot(AR, "kc2", [128, 4, 2, 64], BF16, 2)
        for b in range(TS):
            kc, kcb = kc_r.get()
            S.dma("sp", kc[:, :], ks_out[b], reads=[b_kso], writes=[kcb])
            kc2, kc2b = kc2_r.get()
            S.dve(lambda e, kc=kc, kc2=kc2: e.tensor_copy(out=kc2[:, :, :, :], in_=V(kc, 0, [[64, 4], [0, 2], [1, 64]])),
                  reads=[kcb], writes=[kc2b])
            ps, pb = psA.get()
            psb = ps[:, :].bitcast(BF16)
            for kvh in range(4):
                S.pe(lambda e, psb=psb, kvh=kvh, kc2=kc2: e.transpose(
                    out=psb[:, kvh * 128:(kvh + 1) * 128], in_=V(kc2, kvh * 128, [[1, 128]]), identity=ident_b[:, :]),
                    reads=[kc2b, B_identb], writes=[pb])
            evac_copy(V(KTs, b * 512, [[1, 512]]), psb[:, 0:512], [pb], [b_kts])
            S.dma("pool", Vs[:, b, :], vs_out[b], reads=[b_vso], writes=[b_vs2])
        xout_b = kvx_out.ap().bitcast(BF16)
        for kvh in range(4):
            S.dma("sp", KT_all[:, kvh, 0:128], xout_b[0:128, kvh * 128:(kvh + 1) * 128], reads=[b_xout], writes=[b_kt[0]])
        S.dma("sp", V_all[:, 0, :], xout_b[0:128, 512:768], reads=[b_xout], writes=[b_vv[0]])
        S.dma("pool", KT_d, KT_all[:, :, :], reads=b_kt, writes=[b_kvd])
        S.dma("pool", V_d, V_all[:, :, :], reads=b_vv, writes=[b_kvd])
        S.dma("pool", KTs_d, KTs[:, :, :, :], reads=[b_kts], writes=[b_kvd])
        S.dma("pool", Vs_d, Vs[:, :, :], reads=[b_vs2], writes=[b_kvd])
        phase_end(m_kv)

    def attn_w_alloc(which):
        w = {}
        for nm in which:
            w[nm] = (AR.alloc("W%st" % nm, [128, 8, D], BF16), [Buf(), Buf()])
        return w

    def attn_w_load(j, w):
        for nm, srcd in (("Q", w_q_d), ("O", w_o_d)):
            if nm in w and not w.get(nm + "_loaded"):
                t, bs = w[nm]
                for hf in range(2):
                    S.dma("pool", t[:, :, hf * 512:(hf + 1) * 512],
                          srcd[j, :, hf * 512:(hf + 1) * 512].rearrange("(q p) f -> p q f", p=128), writes=[bs[hf]])
                w[nm + "_loaded"] = True

    def attn_layer(j, w=None, hook=None):
        if True:
            layer = 2 + j
            m_at = AR.mark()
            w = dict(w) if w is not None else {}
            H = QK()
            KT_all = AR.alloc("KT_all", [128, 4, 128 + NT], BF16)
            V_all = AR.alloc("V_all", [128, 18, 256], BF16)
            KTs = AR.alloc("KTs", [128, TS, 4, 128], BF16)
            Vs = AR.alloc("Vs", [128, TS, 256], BF16)
            b_kvs = Buf()
            b_kt = [b_kvs] * 18
            b_vv = [b_kvs] * 18
            b_kts = b_vs2 = b_kvs
            S.dma("sp", KT_all[:, :, :], KT_d, reads=[b_kvd], writes=[b_kvs])
            S.dma("sp", V_all[:, :, :], V_d, reads=[b_kvd], writes=[b_kvs])
            S.dma("sp", KTs[:, :, :, :], KTs_d, reads=[b_kvd], writes=[b_kvs])
            S.dma("sp", Vs[:, :, :], Vs_d, reads=[b_kvd], writes=[b_kvs])
            missing = [nm for nm in ("Q", "O") if nm not in w]
            w.update(attn_w_alloc(missing))
            attn_w_load(j, w)
            WQt, b_wq = w["Q"]
            WOt, b_wo = w["O"]
            if hook is not None:
                hook()
            qt_r = Rot(AR, "qt", [128, 8, TW], BF16, 2)
            ot_r = Rot(AR, "ot", [128, 8, TW], BF16, 2)
            pt_r = Rot(AR, "pt", [128, 256], BF16, 12)
            ln_r = Rot(AR, "lnr", [128, 128], F32, 4)
            for ti, tile in enumerate(TILES):
                c0, n, bl = tile
                xt, xb = load_xt(tile)
                ut, utb = rmsnorm(xt, xb, n, 8 * layer, lnexp=True)
                cs, csb = load_cs(H, tile)
                qt, qtb = qt_r.get()
                pss = {}

                def projq(c):
                    ps, pb = psA.get()
                    for q in range(8):
                        S.pe(lambda e, ps=ps, q=q, c=c, ut=ut, n=n: e.matmul(
                            out=ps[:, 0:n], lhsT=WQt[:, q, c * 128:(c + 1) * 128], rhs=ut[:, q, 0:n],
                            start=(q == 0), stop=(q == 7)), reads=[utb, b_wq[c // 4]], writes=[pb])
                    pss[c] = (ps, pb)

                projq(0)
                projq(1)
                for c in range(8):
                    ps, pb = pss.pop(c)
                    qk_post(H, ps, pb, n, 2 * j, cs, csb, qt[:, c, 0:n], [qtb])
                    if c + 2 < 8:
                        projq(c + 2)
                ot, otb = ot_r.get()
                if ti < 8:
                    def stage_a(sb, c):
                        qb = 2 * ti + sb
                        MASK = MASK_F if qb == 0 else MASK_R
                        kvh = c // 2
                        pts = []
                        for hh in range(2):
                            psS, pbS = psA.get()
                            for kt in range(2):
                                S.pe(lambda e, psS=psS, hh=hh, kt=kt, kvh=kvh, qb=qb, c=c, qt=qt, sb=sb: e.matmul(
                                    out=psS[:, kt * 128:(kt + 1) * 128],
                                    lhsT=KT_all[hh * 64:(hh + 1) * 64, kvh, (qb + kt) * 128:(qb + kt + 1) * 128],
                                    rhs=qt[hh * 64:(hh + 1) * 64, c, sb * 128:(sb + 1) * 128], start=True, stop=True),
                                    reads=[b_kt[qb + kt], qtb], writes=[pbS])
                            pt, ptb = pt_r.get()
                            S.act(lambda e, psS=psS, pt=pt: e.activation(out=pt[:, :], in_=psS[:, 0:256], func=AF.Exp,
                                                                         scale=0.125), reads=[pbS], writes=[ptb])
                            (S.pool if hh == 0 else S.dve)(
                                lambda e, pt=pt, MASK=MASK: e.tensor_tensor(out=pt[:, :], in0=pt[:, :], in1=MASK[:, :],
                                                                            op=ALU.mult), reads=[ptb, b_c3], writes=[ptb])
                            pts.append((pt, ptb))
                        return pts

                    def stage_b(sb, c, pts):
                        qb = 2 * ti + sb
                        kvh = c // 2
                        psO, pbO = psA.get()
                        for part in range(2):
                            for hh in range(2):
                                pt, ptb = pts[hh]
                                for kt in range(2):
                                    if part == 0:
                                        lhs = V_all[:, qb + kt, kvh * 64:(kvh + 1) * 64]
                                        rd = [b_vv[qb + kt], ptb]
                                    else:
                                        lhs = ones_b[:, 0:64]
                                        rd = [B_ones, ptb]
                                    S.pe(lambda e, psO=psO, hh=hh, kt=kt, lhs=lhs, pt=pt, part=part: e.matmul(
                                        out=psO[hh * 64:(hh + 1) * 64, part * 128:(part + 1) * 128], lhsT=lhs,
                                        rhs=pt[:, kt * 128:(kt + 1) * 128], start=(kt == 0), stop=(kt == 1)),
                                        reads=rd, writes=[pbO])
                        ln, lnb = ln_r.get()
                        S.act(lambda e, psO=psO, ln=ln, c=c, j=j: e.activation(
                            out=ln[:, :], in_=psO[:, 128:256], func=AF.Ln, bias=ESINK[:, j * 8 + c:j * 8 + c + 1]),
                            reads=[pbO, b_c3], writes=[lnb])
                        S.act(lambda e, ln=ln: e.activation(out=ln[:, :], in_=ln[:, :], func=AF.Exp, scale=-1.0),
                              reads=[lnb], writes=[lnb])
                        S.dve(lambda e, psO=psO, ln=ln, ot=ot, c=c, sb=sb: e.tensor_tensor(
                            out=ot[:, c, sb * 128:(sb + 1) * 128], in0=psO[:, 0:128], in1=ln[:, :], op=ALU.mult),
                            reads=[pbO, lnb], writes=[otb])

                    its = [(sb, c) for sb in range(2) for c in range(8)]
                    pend = []
                    SKEW = 2
                    for it in range(len(its) + SKEW):
                        if it < len(its):
                            pend.append(stage_a(*its[it]))
                        if it >= SKEW:
                            stage_b(*its[it - SKEW], pend[it - SKEW])
                else:
                    pts = []
                    for hh in range(2):
                        psS, pbS = psA.get()
                        for b in range(TS):
                            for kvh in range(4):
                                S.pe(lambda e, psS=psS, hh=hh, b=b, kvh=kvh, qt=qt: e.matmul(
                                    out=psS[:, (b * 4 + kvh) * 2:(b * 4 + kvh) * 2 + 2],
                                    lhsT=V(KTs, (b * 4 + kvh) * 128, [[1, 128]], p0=hh * 64, npart=64),
                                    rhs=V(qt, (2 * kvh) * TW + b, [[TW, 2]], p0=hh * 64, npart=64), start=True, stop=True),
                                    reads=[b_kts, qtb], writes=[pbS])
                        pt, ptb = pt_r.get()
                        S.act(lambda e, psS=psS, pt=pt: e.activation(out=pt[:, 0:128], in_=psS[:, 0:128], func=AF.Exp,
                                                                     scale=0.125), reads=[pbS], writes=[ptb])
                        pts.append((pt, ptb))
                    psO, pbO = psA.get()
                    for part in range(2):
                        for hh in range(2):
                            pt, ptb = pts[hh]
                            for b in range(TS):
                                for kvh in range(4):
                                    col = (b * 4 + kvh) * 2
                                    if part == 0:
                                        lhs = Vs[:, b, kvh * 64:(kvh + 1) * 64]
                                        rd = [b_vs2, ptb]
                                    else:
                                        lhs = ones_b[:, 0:64]
                                        rd = [B_ones, ptb]
                                    S.pe(lambda e, psO=psO, hh=hh, lhs=lhs, pt=pt, part=part, col=col: e.matmul(
                                        out=psO[hh * 64:(hh + 1) * 64, part * 128 + col:part * 128 + col + 2], lhsT=lhs,
                                        rhs=pt[:, col:col + 2], start=True, stop=True), reads=rd, writes=[pbO])
                    ln, lnb = ln_r.get()
                    S.dve(lambda e, psO=psO, ln=ln, j=j: e.tensor_tensor(
                        out=V(ln, 0, [[8, TS], [1, 8]]), in0=V(psO, 128, [[8, TS], [1, 8]]),
                        in1=V(ESINK, j * 8, [[0, TS], [1, 8]]), op=ALU.add),
                        reads=[pbO, b_c3], writes=[lnb])
                    S.act(lambda e, ln=ln: e.activation(out=ln[:, :], in_=ln[:, :], func=AF.Ln), reads=[lnb], writes=[lnb])
                    S.act(lambda e, ln=ln: e.activation(out=ln[:, :], in_=ln[:, :], func=AF.Exp, scale=-1.0),
                          reads=[lnb], writes=[lnb])
                    S.dve(lambda e, psO=psO, ln=ln, ot=ot: e.tensor_tensor(
                        out=V(ot, 0, [[1, TS], [TW, 8]]), in0=V(psO, 0, [[8, TS], [1, 8]]),
                        in1=V(ln, 0, [[8, TS], [1, 8]]), op=ALU.mult), reads=[pbO, lnb], writes=[otb])
                for d2 in range(4):
                    ps, pb = psA.get()
                    for k in range(2):
                        dm = d2 * 2 + k
                        for c in range(8):
                            S.pe(lambda e, ps=ps, k=k, dm=dm, c=c, ot=ot, n=n: e.matmul(
                                out=ps[:, k * 256:k * 256 + n], lhsT=WOt[:, c, dm * 128:(dm + 1) * 128],
                                rhs=ot[:, c, 0:n], start=(c == 0), stop=(c == 7)),
                                reads=[otb, b_wo[dm // 4]], writes=[pb])
                    psv = V(ps, 0, [[256, 2], [1, n]])
                    S.dve(lambda e, xt=xt, d2=d2, psv=psv, n=n: e.tensor_tensor(
                        out=xt[:, 2 * d2:2 * d2 + 2, 0:n], in0=psv, in1=xt[:, 2 * d2:2 * d2 + 2, 0:n], op=ALU.add),
                        reads=[pb, xb], writes=[xb])
                store_xt(tile, xt, xb)
            phase_end(m_at)

    def mlp_alloc(win0=None):
        if win0 is None:
            win0 = (AR.alloc("win0", [128, 8, 2048], BF16), [Buf() for _ in range(4)])
        win_t = [win0[0], AR.alloc("win1", [128, 8, 2048], BF16)]
        wout_t = [AR.alloc("wout%d" % i, [128, 16, 1024], BF16) for i in range(2)]
        win_b = [win0[1], [Buf() for _ in range(4)]]
        wout_b = [[Buf() for _ in range(4)] for _ in range(2)]
        return win_t, wout_t, win_b, wout_b

    def win0_load(l, win0):
        t, bs = win0
        for j in range(4):
            S.dma("pool", t[:, :, j * 512:(j + 1) * 512],
                  w_mlp_in[l, :, j * 512:(j + 1) * 512].rearrange("(q p) f -> p q f", p=128), writes=[bs[j]])

    def mlp_weights(l, pre=None, lazy=None, skip_win0=False):
        win_t, wout_t, win_b, wout_b = pre if pre is not None else mlp_alloc()

        class _Q:
            def dma(self, *a, **k):
                if lazy is None:
                    S.dma(*a, **k)
                else:
                    lazy.append(lambda a=a, k=k: S.dma(*a, **k))
        Sx = _Q()
        for hf in range(2):
            for j in range(4):
                if hf == 0 and skip_win0:
                    continue
                Sx.dma("pool", win_t[hf][:, :, j * 512:(j + 1) * 512],
                      w_mlp_in[l, :, hf * 2048 + j * 512: hf * 2048 + (j + 1) * 512].rearrange("(q p) f -> p q f", p=128),
                      writes=[win_b[hf][j]])
            for j in range(4):
                Sx.dma("pool", wout_t[hf][:, j * 4:(j + 1) * 4, :],
                      w_mlp_out[l, hf * 2048 + j * 512: hf * 2048 + (j + 1) * 512, :].rearrange("(f p) d -> p f d", p=128),
                      writes=[wout_b[hf][j]])
        return win_t, wout_t, win_b, wout_b

    def mlp(l, pre=None, rest=(), win0=None, hook=None):
        m_mlp = AR.mark()
        if pre is None and win0 is not None:
            pre = mlp_weights(l, mlp_alloc(win0), None, True)
        win_t, wout_t, win_b, wout_b = pre if pre is not None else mlp_weights(l)
        if hook is not None:
            hook()
        for fn in rest:
            fn()
        hT_r = Rot(AR, "hT", [128, 16, TW], BF16, 2)
        rl_r = Rot(AR, "rl", [128, 512], F32, 2)
        if l == 3 and FUSE_IO:
            yo_r = Rot(AR, "yo", [128, D], F32, 2)
        for hf in range(2):
            for tile in TILES:
                c0, n, bl = tile
                xt, xb = load_xt(tile)
                if hf == 0:
                    ut, utb = rmsnorm(xt, xb, n, 32 + 8 * l)
                    S.dma("pool", UT_d[:, :, c0:c0 + n], ut[:, :, 0:n], reads=[utb], writes=[UT_b[i] for i in bl])
                else:
                    ut, utb = ut_r.get()
                    S.dma("sp", ut[:, :, 0:n], UT_d[:, :, c0:c0 + n], reads=[UT_b[i] for i in bl], writes=[utb])
                hT, hb = hT_r.get()
                for f2 in range(8):
                    ps, pb = psA.get()
                    for k in range(2):
                        fc = f2 * 2 + k
                        for q in range(8):
                            S.pe(lambda e, ps=ps, k=k, fc=fc, q=q, hf=hf, ut=ut, n=n: e.matmul(
                                out=ps[:, k * 256:k * 256 + n], lhsT=win_t[hf][:, q, fc * 128:(fc + 1) * 128],
                                rhs=ut[:, q, 0:n], start=(q == 0), stop=(q == 7)),
                                reads=[utb, win_b[hf][fc // 4]], writes=[pb])
                    rl, rlb = rl_r.get()
                    psv = V(ps, 0, [[256, 2], [1, n]])
                    rlv = V(rl, 0, [[256, 2], [1, n]])
                    S.act(lambda e, rlv=rlv, psv=psv: e.activation(out=rlv, in_=psv, func=AF.Relu), reads=[pb], writes=[rlb])
                    S.dve(lambda e, hT=hT, f2=f2, rlv=rlv, n=n: e.tensor_tensor(
                        out=hT[:, 2 * f2:2 * f2 + 2, 0:n], in0=rlv, in1=rlv, op=ALU.mult), reads=[rlb], writes=[hb])
                for d2 in range(4):
                    ps, pb = psA.get()
                    for k in range(2):
                        dm = d2 * 2 + k
                        for fc in range(16):
                            S.pe(lambda e, ps=ps, k=k, dm=dm, fc=fc, hf=hf, hT=hT, n=n: e.matmul(
                                out=ps[:, k * 256:k * 256 + n], lhsT=wout_t[hf][:, fc, dm * 128:(dm + 1) * 128],
                                rhs=hT[:, fc, 0:n], start=(fc == 0), stop=(fc == 15)),
                                reads=[hb, wout_b[hf][fc // 4]], writes=[pb])
                    psv = V(ps, 0, [[256, 2], [1, n]])
                    S.dve(lambda e, xt=xt, d2=d2, psv=psv, n=n: e.tensor_tensor(
                        out=xt[:, 2 * d2:2 * d2 + 2, 0:n], in0=psv, in1=xt[:, 2 * d2:2 * d2 + 2, 0:n], op=ALU.add),
                        reads=[pb, xb], writes=[xb])
                if l == 3 and hf == 1 and FUSE_IO:
                    for sb, blk in enumerate(bl):
                        nn = blk_rows(blk)
                        yt, yb = yo_r.get()
                        for half in range(2):
                            ps, pb = psA.get()
                            for qq in range(4):
                                q = half * 4 + qq
                                S.pe(lambda e, ps=ps, xt=xt, q=q, qq=qq, nn=nn, sb=sb: e.transpose(
                                    out=ps[0:nn, qq * 128:(qq + 1) * 128], in_=xt[:, q, sb * 128:sb * 128 + nn],
                                    identity=ident_f[:, :]), reads=[xb, B_identf], writes=[pb])
                            evac_copy(yt[0:nn, half * 512:(half + 1) * 512], ps[0:nn, :], [pb], [yb], "act")
                        dst = y_p[blk * 128:(blk + 1) * 128, :] if blk < 16 else y_s[:, :]
                        S.dma("pool", dst, yt[0:nn, :], reads=[yb])
                else:
                    store_xt(tile, xt, xb)
        phase_end(m_mlp)

    try:
        for l in range(4):
            if l < 2 and on("s5_%d" % l):
                s5_layer(l)
            pre = None
            rest = ()
            m_pre = AR.mark()
            if l < 2 and on("glu%d" % l):
                hook = None
                if on("mlp%d" % l):
                    pre = mlp_alloc()
                    def hook(l=l, pre=pre):
                        lz = []
                        mlp_weights(l, pre, lz)
                        return lz
                rest = glu(l, hook)
            if l >= 2 and ALL:
                if l == 2:
                    win0 = (AR.alloc("win0p", [128, 8, 2048], BF16), [Buf() for _ in range(4)])
                    m_l2 = AR.mark()
                    w0 = attn_w_alloc(["Q", "O"])
                    kv_phase(hook=lambda: attn_w_load(0, w0))
                    attn_layer(0, w0, hook=lambda: win0_load(2, win0))
                    AR.release(m_l2)
                    S.barrier()
                    w1 = attn_w_alloc(["Q"])
                    mlp(2, win0=win0, hook=lambda: attn_w_load(1, w1))
                else:
                    attn_layer(1, w1, hook=lambda: win0_load(3, win0))
                    AR.release(m_l2)
                    S.barrier()
                    mlp(3, win0=win0)
                continue
            if l == 2 and on("kv"):
                kv_phase()
            if l >= 2 and on("attn%d" % (l - 2)):
                attn_layer(l - 2)
            if on("mlp%d" % l):
                mlp(l, pre, rest)
            if pre is not None:
                phase_end(m_pre)
    except StopBuild:
        S.emit()
        return nc

    xfin = Rot(AR, "xfin", [128, 8, 128], F32, 2)
    yout = Rot(AR, "yout", [128, D], F32, 2)
    for b in range(0 if FUSE_IO else 17):
        n = blk_rows(b)
        t, tb = xfin.get()
        S.dma("sp", t[:, :, 0:n], XT_d[:, :, b * 128:b * 128 + n], reads=[XT_b[b]], writes=[tb])
        yt, yb = yout.get()
        for half in range(2):
            ps, pb = psA.get()
            for qq in range(4):
                q = half * 4 + qq
                S.pe(lambda e, ps=ps, t=t, q=q, qq=qq, n=n: e.transpose(
                    out=ps[0:n, qq * 128:(qq + 1) * 128], in_=t[:, q, 0:n],
                    identity=ident_f[:, :]), reads=[tb, B_identf], writes=[pb])
            evac_copy(yt[0:n, half * 512:(half + 1) * 512], ps[0:n, :], [pb], [yb])
        dst = y_p[b * 128:(b + 1) * 128, :] if b < 16 else y_s[:, :]
        S.dma("pool", dst, yt[0:n, :], reads=[yb])

    S.emit()
    return nc


def make_in_maps(inputs):
    f = lambda k: np.ascontiguousarray(inputs[k], dtype=np.float32)
    x_prompt = f("x_prompt")
    x_sample = f("x_sample")
    ident = np.eye(128, dtype=np.float32)
    shared = {k: f(k) for k in ("norm_mix", "norm_mlp", "norm_kv", "b_glu", "w_mlp_in", "w_mlp_out",
                                "ssm_a_re", "ssm_a_im", "ssm_log_dt", "ssm_b_re", "ssm_b_im",
                                "ssm_c_re", "ssm_c_im", "ssm_d", "w_glu", "w_k", "w_v", "w_q", "w_o")}
    ev = np.array([7, 6, 5, 4, 3, 2, 1, 0] + list(range(1, 9)) + [-k for k in range(1, 9)] + [64], np.float32)
    shared["ev"] = np.ascontiguousarray(np.broadcast_to(ev[None, :], (128, 25)))
    jj = np.arange(128) // 16
    shared["cmask"] = (jj[:, None] <= jj[None, :]).astype(np.float32)
    hd = np.arange(128) % 64
    hd_sw = (hd + 32) % 64
    qn = f("q_norm")
    kn = f("k_norm")
    shared["gqk"] = np.ascontiguousarray(np.stack([qn[0][hd], qn[0][hd_sw], qn[1][hd], qn[1][hd_sw],
                                                   kn[hd], kn[hd_sw]], axis=1))
    sk = f("attn_sinks")
    half = (np.arange(128) >= 64).astype(np.int64)
    sinks = np.zeros((128, 16), np.float32)
    for j in range(2):
        for c in range(8):
            sinks[:, j * 8 + c] = sk[j][2 * c + half]
    shared["sinks"] = sinks
    rot = np.zeros((128, 128), np.float32)
    for m in range(128):
        if m % 64 < 32:
            rot[m + 32, m] = -1.0
        else:
            rot[m - 32, m] = 1.0
    shared["rot"] = rot
    blk = np.arange(128) // 64
    shared["bones"] = (blk[:, None] == blk[None, :]).astype(np.float32)
    kj = np.arange(128)[:, None]
    qi = np.arange(128)[None, :]
    m_prev = (kj > qi).astype(np.float32)
    m_own = (kj <= qi).astype(np.float32)
    m_none = np.zeros((128, 128), np.float32)
    inv = (np.float32(10000.0) ** (-np.arange(32, dtype=np.float32) / np.float32(32))).astype(np.float32)
    fidx = (np.arange(128) % 64) % 32
    st_re = f("state_ssm_re").reshape(2, 128, 4096)
    st_im = f("state_ssm_im").reshape(2, 128, 4096)
    ck = f("cache_k").reshape(128, 128, 256)
    cv = f("cache_v").reshape(128, 128, 256)
    in_maps = []
    for c in range(NCORES):
        b, h = c // 2, c % 2
        pos = np.concatenate([np.arange(h * TP, (h + 1) * TP), np.full(TS, 8192)]).astype(np.float32)
        ang = (pos[None, :] * inv[fidx][:, None]).astype(np.float32)
        m = {
            "xp": np.ascontiguousarray(x_prompt[b, h * TP:(h + 1) * TP, :]),
            "xs": np.ascontiguousarray(x_sample[c * TS:(c + 1) * TS, 0, :]),
            "ident": ident,
            "flag": np.full((128, 1), float(h), np.float32),
            "st_re": np.ascontiguousarray(st_re[:, c * TS:(c + 1) * TS]),
            "st_im": np.ascontiguousarray(st_im[:, c * TS:(c + 1) * TS]),
            "masks": np.ascontiguousarray(np.stack([m_prev, m_own, m_prev if h == 1 else m_none])),
            "cos_t": np.cos(ang.astype(np.float64)).astype(np.float32),
            "sin_t": np.sin(ang.astype(np.float64)).astype(np.float32),
            "cache_k": np.ascontiguousarray(ck[c * TS:(c + 1) * TS]),
            "cache_v": np.ascontiguousarray(cv[c * TS:(c + 1) * TS]),
        }
        m.update(shared)
        in_maps.append(m)
    return in_maps


def kernel(_stages=("all",), **inputs):
    nc = build(_stages)
    in_maps = make_in_maps(inputs)
    res = run_bass_kernel_spmd(nc, in_maps, core_ids=list(range(NCORES)))
    r = res.results
    if any(s.startswith("dbg_") for s in _stages):
        return r
    y_prompt = np.zeros((4, 4096, D), np.float32)
    y_sample = np.zeros((128, 1, D), np.float32)
    sp_re = np.zeros((2, 4, 64, 64), np.float32)
    sp_im = np.zeros((2, 4, 64, 64), np.float32)
    ss_re = np.zeros((2, 128, 64, 64), np.float32)
    ss_im = np.zeros((2, 128, 64, 64), np.float32)
    kp = np.zeros((4, 128, 4, 64), np.float32)
    vp = np.zeros((4, 128, 4, 64), np.float32)
    ks = np.zeros((128, 128, 4, 64), np.float32)
    vs = np.zeros((128, 128, 4, 64), np.float32)
    for c in range(NCORES):
        b, h = c // 2, c % 2
        rc = r[c]
        y_prompt[b, h * TP:(h + 1) * TP] = rc["y_p"]
        y_sample[c * TS:(c + 1) * TS, 0] = rc["y_s"]
        if "sp_state" in rc:
            if h == 1:
                sp = rc["sp_state"]
                sp_re[:, b] = sp[:, 0:32].reshape(2, 64, 64)
                sp_im[:, b] = sp[:, 32:64].reshape(2, 64, 64)
            ss = rc["ss_state"]
            ss_re[:, c * TS:(c + 1) * TS] = ss[:, 0].reshape(2, TS, 64, 64)
            ss_im[:, c * TS:(c + 1) * TS] = ss[:, 1].reshape(2, TS, 64, 64)
        if "k_last" in rc:
            if h == 1:
                kp[b] = rc["k_last"].reshape(128, 4, 64)
                vp[b] = rc["v_last"].reshape(128, 4, 64)
            ks[c * TS:(c + 1) * TS] = rc["ks_out"].reshape(TS, 128, 4, 64)
            vs[c * TS:(c + 1) * TS] = rc["vs_out"].reshape(TS, 128, 4, 64)
    return y_prompt, y_sample, sp_re, sp_im, kp, vp, ss_re, ss_im, ks, vs
```
